# Optimizing a Trainium2 kernel written in Bass

```python
import jax
import jax.numpy as jnp
from jax import lax
import numpy as np

D_MODEL = 1024
BATCH = 8
SEQ = 8192
DEPTH = 4

GRID_W = 64
CTX_LEN = 256
N_MIXERS = 3
N_ATTN_LAYERS = (DEPTH + 2) // N_MIXERS
N_POOL_LAYERS = (DEPTH + 1) // N_MIXERS
N_RET_LAYERS = DEPTH // N_MIXERS

ATTN_HEADS = 16
ATTN_KV_HEADS = 4
ATTN_GROUP = ATTN_HEADS // ATTN_KV_HEADS
HEAD_DIM = D_MODEL // ATTN_HEADS
WINDOW = 128
ATTN_BLOCK = 128
ROPE_BASE = 10000.0
NEG_INF = -1e30

POOL_WINDOWS = (2, 4, 8, 16)
POOL_GROUPS = len(POOL_WINDOWS)
POOL_GROUP_DIM = D_MODEL // POOL_GROUPS

RET_HEADS = 4
RET_DK = D_MODEL // RET_HEADS
RET_DV = 2 * RET_DK
RET_CHUNK = 128
RET_BWD_OFFSET = 0.5

FFN_HIDDEN = ((8 * D_MODEL + 3 * 256 - 1) // (3 * 256)) * 256
NORM_EPS = 1e-6

kernel_name = 'hybrid_interleaved_dit_block'


def rmsnorm(x, gain=None):
    xf = x.astype(jnp.float32)
    y = xf * lax.rsqrt(jnp.mean(xf * xf, axis=-1, keepdims=True) + NORM_EPS)
    if gain is not None:
        y = y * gain.astype(jnp.float32)
    return y.astype(x.dtype)


def pre_norm(xs, gain, shift, scale):
    return rmsnorm(xs, gain) * (1 + scale) + shift


def apply_rotary(x, cos, sin):
    half = x.shape[-1] // 2
    shape = (1, x.shape[1]) + (1,) * (x.ndim - 3) + (half,)
    cos = cos.reshape(shape).astype(x.dtype)
    sin = sin.reshape(shape).astype(x.dtype)
    x1, x2 = x[..., :half], x[..., half:]
    return jnp.concatenate([x1 * cos - x2 * sin, x2 * cos + x1 * sin], axis=-1)


def axial_rotary_tables(n_tokens):
    rows = n_tokens // GRID_W
    row = jnp.repeat(jnp.arange(rows, dtype=jnp.float32), GRID_W)
    col = jnp.tile(jnp.arange(GRID_W, dtype=jnp.float32), rows)
    n_freq = HEAD_DIM // 4
    inv = ROPE_BASE ** (-jnp.arange(n_freq, dtype=jnp.float32) / n_freq)
    ang = jnp.concatenate([row[:, None] * inv, col[:, None] * inv], axis=-1)
    return jnp.cos(ang), jnp.sin(ang)


def retention_rotary_tables(n_tokens):
    inv = ROPE_BASE ** (-jnp.linspace(0.0, 1.0, RET_DK // 2, dtype=jnp.float32))
    ang = jnp.arange(n_tokens, dtype=jnp.float32)[:, None] * inv
    return jnp.cos(ang), jnp.sin(ang)


def softmax_with_sink(logits, sink):
    full = jnp.concatenate([logits, jnp.broadcast_to(sink, logits.shape[:-1] + (1,))], axis=-1)
    return jax.nn.softmax(full, axis=-1)[..., :-1]


def banded_window_attention(q, k, v, k_ctx, v_ctx, sink):
    B, T = q.shape[:2]
    n_blk = T // ATTN_BLOCK
    span = 3 * ATTN_BLOCK
    pad = ((0, 0), (ATTN_BLOCK, ATTN_BLOCK), (0, 0), (0, 0))
    kp = jnp.pad(k, pad)
    vp = jnp.pad(v, pad)
    offs = jnp.arange(span) - ATTN_BLOCK
    rel = offs[None, :] - jnp.arange(ATTN_BLOCK)[:, None]
    near = jnp.abs(rel) <= WINDOW
    q_blocks = jnp.moveaxis(q.reshape((B, n_blk, ATTN_BLOCK) + q.shape[2:]), 1, 0)

    def one_block(args):
        qb, i = args
        start = i * ATTN_BLOCK
        kb = lax.dynamic_slice_in_dim(kp, start, span, axis=1)
        vb = lax.dynamic_slice_in_dim(vp, start, span, axis=1)
        key_pos = start + offs
        valid = near & ((key_pos >= 0) & (key_pos < T))[None, :]
        s_loc = jnp.einsum('bqkgd,bskd->bkgqs', qb, kb).astype(jnp.float32)
        s_loc = jnp.where(valid, s_loc, NEG_INF)
        s_ctx = jnp.einsum('bqkgd,bskd->bkgqs', qb, k_ctx).astype(jnp.float32)
        p = softmax_with_sink(jnp.concatenate([s_loc, s_ctx], axis=-1), sink).astype(v.dtype)
        return (jnp.einsum('bkgqs,bskd->bqkgd', p[..., :span], vb)
                + jnp.einsum('bkgqs,bskd->bqkgd', p[..., span:], v_ctx))

    out = lax.map(one_block, (q_blocks, jnp.arange(n_blk)))
    return jnp.moveaxis(out, 0, 1).reshape(B, T, ATTN_HEADS * HEAD_DIM)


def attention_mixer(h_ctx, h_lat, w_qkv, w_o, q_gain, k_gain, sink, need_ctx_out):
    q_cols = ATTN_HEADS * HEAD_DIM
    scale = HEAD_DIM ** -0.5
    sink_logit = sink.astype(jnp.float32).reshape(ATTN_KV_HEADS, ATTN_GROUP, 1, 1)

    def heads_q(q):
        B, T, _ = q.shape
        return rmsnorm(q.reshape(B, T, ATTN_KV_HEADS, ATTN_GROUP, HEAD_DIM), q_gain) * scale

    def heads_kv(kv):
        B, T, _ = kv.shape
        k, v = jnp.split(kv, 2, axis=-1)
        return (rmsnorm(k.reshape(B, T, ATTN_KV_HEADS, HEAD_DIM), k_gain),
                v.reshape(B, T, ATTN_KV_HEADS, HEAD_DIM))

    B, T, _ = h_lat.shape
    cos, sin = axial_rotary_tables(T)
    qkv = h_lat @ w_qkv
    q_l = apply_rotary(heads_q(qkv[..., :q_cols]), cos, sin)
    k_l, v_l = heads_kv(qkv[..., q_cols:])
    k_l = apply_rotary(k_l, cos, sin)
    if need_ctx_out:
        qkv_c = h_ctx @ w_qkv
        q_c = heads_q(qkv_c[..., :q_cols])
        k_c, v_c = heads_kv(qkv_c[..., q_cols:])
    else:
        k_c, v_c = heads_kv(h_ctx @ w_qkv[:, q_cols:])
    y_l = banded_window_attention(q_l, k_l, v_l, k_c, v_c, sink_logit) @ w_o
    if not need_ctx_out:
        return None, y_l
    L = h_ctx.shape[1]
    s = jnp.einsum('bqkgd,bskd->bkgqs', q_c, k_c).astype(jnp.float32)
    p = softmax_with_sink(s, sink_logit).astype(v_c.dtype)
    y_c = jnp.einsum('bkgqs,bskd->bqkgd', p, v_c).reshape(B, L, q_cols) @ w_o
    return y_c, y_l


def window_means(h, window):
    T = h.shape[1]
    cs = jnp.pad(jnp.cumsum(h.astype(jnp.float32), axis=1), ((0, 0), (1, 0), (0, 0)))
    t = jnp.arange(T)
    lo = jnp.maximum(t - window // 2, 0)
    hi = jnp.minimum(t + window // 2, T)
    s = jnp.take(cs, hi, axis=1) - jnp.take(cs, lo, axis=1)
    return (s / (hi - lo).astype(jnp.float32)[None, :, None]).astype(h.dtype)


def pool_mixer(h, w_group, layer_scale):
    B, T, _ = h.shape
    groups = jnp.split(h, POOL_GROUPS, axis=-1)
    pooled = jnp.stack([window_means(g, w) - g for g, w in zip(groups, POOL_WINDOWS)], axis=2)
    y = jnp.einsum('btgc,gcd->btgd', pooled, w_group).reshape(B, T, D_MODEL)
    return y * layer_scale


def retention_chunked(q, k, v, log_g, state):
    B, T, H, _ = q.shape
    dv = v.shape[-1]
    n = T // RET_CHUNK

    def to_chunks(a):
        return jnp.moveaxis(a.reshape(B, n, RET_CHUNK, H, a.shape[-1]), 1, 0)

    pos = jnp.arange(RET_CHUNK, dtype=jnp.float32)
    diff = pos[:, None] - pos[None, :]
    intra = jnp.where(diff >= 0, jnp.exp(jnp.maximum(diff, 0.0)[None] * log_g[:, None, None]), 0.0)
    q_dec = jnp.exp((pos[:, None] + 1.0) * log_g[None, :])[None, :, :, None]
    k_dec = jnp.exp((RET_CHUNK - 1.0 - pos)[:, None] * log_g[None, :])[None, :, :, None]
    chunk_dec = jnp.exp(RET_CHUNK * log_g)[None, :, None, None]

    def step(s, blk):
        qc, kc, vc = blk
        scores = jnp.einsum('bihd,bjhd->bhij', qc, kc) * intra
        y = (jnp.einsum('bhij,bjhv->bihv', scores, vc)
             + jnp.einsum('bihd,bhdv->bihv', qc, s) * q_dec)
        s = s * chunk_dec + jnp.einsum('bjhd,bjhv->bhdv', kc * k_dec, vc)
        return s, y

    state, ys = lax.scan(step, state, (to_chunks(q), to_chunks(k), to_chunks(v)))
    return jnp.moveaxis(ys, 0, 1).reshape(B, T, H, dv), state


def retention_mixer(h_ctx, h_lat, w_in, w_o, need_ctx_out):
    hk = RET_HEADS * RET_DK
    hv = RET_HEADS * RET_DV
    qkv_cols = 2 * hk + hv
    heads = jnp.arange(RET_HEADS, dtype=jnp.float32)
    log_g_f = jnp.log1p(-jnp.exp2(-5.0 - heads))
    log_g_b = jnp.log1p(-jnp.exp2(-5.0 - RET_BWD_OFFSET - heads))

    def project_qkv(h, rotate):
        B, T, _ = h.shape
        qkv = h @ w_in[:, :qkv_cols]
        q = qkv[..., :hk].reshape(B, T, RET_HEADS, RET_DK)
        k = qkv[..., hk:2 * hk].reshape(B, T, RET_HEADS, RET_DK) * (RET_DK ** -0.5)
        v = qkv[..., 2 * hk:].reshape(B, T, RET_HEADS, RET_DV)
        if rotate:
            cos, sin = retention_rotary_tables(T)
            q = apply_rotary(q, cos, sin)
            k = apply_rotary(k, cos, sin)
        return q, k, v

    def combine(h, y_f, y_b):
        B, T, _ = h.shape
        g_f, g_b = jnp.split(h @ w_in[:, qkv_cols:], 2, axis=-1)
        y = (jax.nn.silu(g_f) * rmsnorm(y_f).reshape(B, T, hv).astype(h.dtype)
             + jax.nn.silu(g_b) * rmsnorm(y_b).reshape(B, T, hv).astype(h.dtype))
        return y @ w_o

    def flip(a):
        return jnp.flip(a, axis=1)

    B = h_lat.shape[0]
    zeros = jnp.zeros((B, RET_HEADS, RET_DK, RET_DV), jnp.float32)
    q_c, k_c, v_c = project_qkv(h_ctx, rotate=False)
    yc_f, s_f = retention_chunked(q_c, k_c, v_c, log_g_f, zeros)
    yc_b, s_b = retention_chunked(flip(q_c), flip(k_c), flip(v_c), log_g_b, zeros)
    q_l, k_l, v_l = project_qkv(h_lat, rotate=True)
    yl_f, _ = retention_chunked(q_l, k_l, v_l, log_g_f, s_f)
    yl_b, _ = retention_chunked(flip(q_l), flip(k_l), flip(v_l), log_g_b, s_b)
    y_l = combine(h_lat, yl_f, flip(yl_b))
    y_c = combine(h_ctx, yc_f, flip(yc_b)) if need_ctx_out else None
    return y_c, y_l


def swiglu(h, w_gate, w_up, w_down):
    return (jax.nn.silu(h @ w_gate) * (h @ w_up)) @ w_down


def setup_inputs(seed: int = 0) -> dict:
    key = jax.random.key(seed)
    ks = jax.random.split(key, 20)

    def nrm(k, shape, scale):
        return jax.random.normal(k, shape, jnp.float32) * scale

    qkv_cols = (ATTN_HEADS + 2 * ATTN_KV_HEADS) * HEAD_DIM
    ret_cols = 2 * RET_HEADS * RET_DK + 3 * RET_HEADS * RET_DV
    return {
        'x': nrm(ks[0], (BATCH, SEQ, D_MODEL), 1.0),
        'c': nrm(ks[1], (BATCH, D_MODEL), 1.0),
        'ctx': nrm(ks[2], (BATCH, CTX_LEN, D_MODEL), 1.0),
        'c_ctx': nrm(ks[3], (D_MODEL,), 1.0),
        'ada_w': nrm(ks[4], (DEPTH, D_MODEL, 6 * D_MODEL), 0.5 * D_MODEL ** -0.5),
        'ada_b': nrm(ks[5], (DEPTH, 6 * D_MODEL), 0.02),
        'norm_mix': 1.0 + nrm(ks[6], (DEPTH, D_MODEL), 0.1),
        'norm_ffn': 1.0 + nrm(ks[7], (DEPTH, D_MODEL), 0.1),
        'attn_w_qkv': nrm(ks[8], (N_ATTN_LAYERS, D_MODEL, qkv_cols), D_MODEL ** -0.5),
        'attn_w_o': nrm(ks[9], (N_ATTN_LAYERS, ATTN_HEADS * HEAD_DIM, D_MODEL), (ATTN_HEADS * HEAD_DIM) ** -0.5),
        'attn_q_norm': 1.0 + nrm(ks[10], (N_ATTN_LAYERS, HEAD_DIM), 0.1),
        'attn_k_norm': 1.0 + nrm(ks[11], (N_ATTN_LAYERS, HEAD_DIM), 0.1),
        'attn_sink': nrm(ks[12], (N_ATTN_LAYERS, ATTN_HEADS), 1.0),
        'pool_w': nrm(ks[13], (N_POOL_LAYERS, POOL_GROUPS, POOL_GROUP_DIM, POOL_GROUP_DIM), POOL_GROUP_DIM ** -0.5),
        'pool_scale': 1.0 + nrm(ks[14], (N_POOL_LAYERS, D_MODEL), 0.1),
        'ret_w_in': nrm(ks[15], (N_RET_LAYERS, D_MODEL, ret_cols), D_MODEL ** -0.5),
        'ret_w_o': nrm(ks[16], (N_RET_LAYERS, RET_HEADS * RET_DV, D_MODEL), (RET_HEADS * RET_DV) ** -0.5),
        'ffn_w_gate': nrm(ks[17], (DEPTH, D_MODEL, FFN_HIDDEN), D_MODEL ** -0.5),
        'ffn_w_up': nrm(ks[18], (DEPTH, D_MODEL, FFN_HIDDEN), D_MODEL ** -0.5),
        'ffn_w_down': nrm(ks[19], (DEPTH, FFN_HIDDEN, D_MODEL), FFN_HIDDEN ** -0.5),
    }


def reference(x, c, ctx, c_ctx, ada_w, ada_b, norm_mix, norm_ffn, attn_w_qkv, attn_w_o, attn_q_norm,
              attn_k_norm, attn_sink, pool_w, pool_scale, ret_w_in, ret_w_o, ffn_w_gate, ffn_w_up, ffn_w_down):
    x_lat, x_ctx = x, ctx
    cond_lat = jax.nn.silu(c)[:, None, :]
    cond_ctx = jax.nn.silu(c_ctx)[None, None, :]
    for i in range(DEPTH):
        kind, slot = i % N_MIXERS, i // N_MIXERS
        need_ctx_out = i < DEPTH - 1
        sh_m, sc_m, g_m, sh_f, sc_f, g_f = jnp.split(cond_lat @ ada_w[i] + ada_b[i], 6, axis=-1)
        h_l = pre_norm(x_lat, norm_mix[i], sh_m, sc_m)
        if need_ctx_out or kind != 1:
            csh_m, csc_m, cg_m, csh_f, csc_f, cg_f = jnp.split(cond_ctx @ ada_w[i] + ada_b[i], 6, axis=-1)
            h_c = pre_norm(x_ctx, norm_mix[i], csh_m, csc_m)
        if kind == 0:
            y_c, y_l = attention_mixer(h_c, h_l, attn_w_qkv[slot], attn_w_o[slot], attn_q_norm[slot],
                                       attn_k_norm[slot], attn_sink[slot], need_ctx_out)
        elif kind == 1:
            y_l = pool_mixer(h_l, pool_w[slot], pool_scale[slot])
            y_c = pool_mixer(h_c, pool_w[slot], pool_scale[slot]) if need_ctx_out else None
        else:
            y_c, y_l = retention_mixer(h_c, h_l, ret_w_in[slot], ret_w_o[slot], need_ctx_out)
        x_lat = x_lat + g_m * y_l
        x_lat = x_lat + g_f * swiglu(pre_norm(x_lat, norm_ffn[i], sh_f, sc_f),
                                     ffn_w_gate[i], ffn_w_up[i], ffn_w_down[i])
        if need_ctx_out:
            x_ctx = x_ctx + cg_m * y_c
            x_ctx = x_ctx + cg_f * swiglu(pre_norm(x_ctx, norm_ffn[i], csh_f, csc_f),
                                          ffn_w_gate[i], ffn_w_up[i], ffn_w_down[i])
    return x_lat
```

```python
import contextlib
import numpy as np
import concourse.bass as bass
import concourse.mybir as mybir
from concourse.bass_utils import run_bass_kernel_spmd

F32 = mybir.dt.float32
BF16 = mybir.dt.bfloat16
AF = mybir.ActivationFunctionType
ALU = mybir.AluOpType
AX = mybir.AxisListType

D = 1024
NC8 = 8
FH = 2816
NJ = 22
CTX = 256
EPS = 1e-6
USE_LN = False
USE_POW = False
DBG = {}


class Buf:
    __slots__ = ("name", "writers", "readers", "dsem", "dcount")

    def __init__(self, name):
        self.name = name
        self.writers = {}
        self.readers = {}
        self.dsem = None
        self.dcount = 0


class Op:
    __slots__ = ("eng", "fn", "deps", "needed", "value", "is_dma", "dbuf", "multi")

    def __init__(self, eng, fn, is_dma=False, dbuf=None, multi=False):
        self.eng = eng
        self.fn = fn
        self.deps = []
        self.needed = False
        self.value = None
        self.is_dma = is_dma
        self.dbuf = dbuf
        self.multi = multi


ENGS = ("pe", "act", "dve", "pool", "sp")


class Prog:
    def __init__(self, nc):
        self.nc = nc
        self.ops = {e: [] for e in ENGS}
        self.all_ops = []
        self.dma_bufs = []
        self.barrier_deps = None
        self.barrier_pending = set()

    def _key(self, op):
        return ("d", id(op.dbuf)) if op.is_dma else op.eng

    def _record(self, op, reads, writes):
        deps = op.deps
        if self.barrier_deps is not None and op.eng in self.barrier_pending:
            deps.extend(self.barrier_deps)
            self.barrier_pending.discard(op.eng)
        for b in reads:
            for w in b.writers.values():
                deps.append(w)
        for b in writes:
            same_dma = op.is_dma and b.writers and all(
                w.is_dma and w.dbuf is op.dbuf for w in b.writers.values()) and not b.readers
            if not same_dma:
                for w in b.writers.values():
                    deps.append(w)
                for r in b.readers.values():
                    deps.append(r)
        k = self._key(op)
        for b in reads:
            b.readers[k] = op
        for b in writes:
            same_dma = op.is_dma and b.writers and all(
                w.is_dma and w.dbuf is op.dbuf for w in b.writers.values()) and not b.readers
            if not same_dma:
                b.writers = {}
                b.readers = {}
            b.writers[k] = op
        self.ops[op.eng].append(op)
        self.all_ops.append(op)
        return op

    def op(self, eng, fn, reads=(), writes=(), multi=False):
        return self._record(Op(eng, fn, multi=multi), reads, writes)

    def call(self, eng, method, reads, writes, *args, **kw):
        multi = kw.pop("_multi", False)

        def fn(e, method=method, args=args, kw=kw):
            return getattr(e, method)(*args, **kw)

        return self._record(Op(eng, fn, multi=multi), reads, writes)

    def mm(self, out, lhsT, rhs, start, stop, reads, writes):
        return self.call("pe", "matmul", reads, writes, out, lhsT, rhs, start=start, stop=stop)

    def act(self, out, in_, func, reads, writes, **kw):
        return self.call("act", "activation", reads, writes, out=out, in_=in_, func=func, **kw)

    def tt(self, eng, out, in0, in1, op, reads, writes):
        return self.call(eng, "tensor_tensor", reads, writes, out=out, in0=in0, in1=in1, op=op)

    def stt(self, out, in0, scalar, in1, op0, op1, reads, writes):
        return self.call("dve", "scalar_tensor_tensor", reads, writes, out=out, in0=in0, scalar=scalar,
                         in1=in1, op0=op0, op1=op1)

    def ts(self, eng, out, in0, s1, s2, op0, op1, reads, writes):
        return self.call(eng, "tensor_scalar", reads, writes, out=out, in0=in0, scalar1=s1, scalar2=s2,
                         op0=op0, op1=op1)

    def copy(self, eng, out, in_, reads, writes):
        if eng == "act":
            return self.call("act", "activation", reads, writes, out=out, in_=in_, func=AF.Copy)
        return self.call(eng, "tensor_copy", reads, writes, out=out, in_=in_)

    def memset(self, eng, ap, val, writes):
        return self.call(eng, "memset", [], writes, ap, val)

    def dma(self, eng, out_ap, in_ap, reads, writes, dbuf=None):
        dbuf = dbuf or writes[0]
        if dbuf.dsem is None:
            dbuf.dsem = True
            self.dma_bufs.append(dbuf)

        def fn(e, out_ap=out_ap, in_ap=in_ap):
            return e.dma_start(out=out_ap, in_=in_ap)

        return self._record(Op(eng, fn, is_dma=True, dbuf=dbuf), reads, writes)

    def barrier(self):
        deps = []
        for e in ENGS:
            if self.ops[e]:
                last = [o for o in self.ops[e] if not o.is_dma]
                if last:
                    deps.append(last[-1])
        for b in self.dma_bufs:
            pass
        for o in self.all_ops[::-1]:
            if o.is_dma and not any((d.is_dma and d.dbuf is o.dbuf) for d in deps):
                deps.append(o)
        self.barrier_deps = deps
        self.barrier_pending = set(ENGS)

    def emit(self, final_bufs):
        nc = self.nc
        for o in self.all_ops:
            for d in o.deps:
                d.needed = True
        cnt = {e: 0 for e in ENGS}
        for o in self.all_ops:
            if o.is_dma:
                o.dbuf.dcount += 16
                o.value = o.dbuf.dcount
            elif o.needed:
                cnt[o.eng] += 1
                o.value = cnt[o.eng]
        for b in self.dma_bufs:
            b.dcount = 0
        with contextlib.ExitStack() as st:
            esem = {e: st.enter_context(nc.semaphore("S_" + e)) for e in ENGS}
            for i, b in enumerate(self.dma_bufs):
                b.dsem = st.enter_context(nc.semaphore("D%d" % i))
            assert len(self.dma_bufs) < 90, len(self.dma_bufs)
            block = st.enter_context(nc.Block())

            def resolve(op, waited):
                best = {}
                for d in op.deps:
                    if d.is_dma:
                        key, sem = ("d", id(d.dbuf)), d.dbuf.dsem
                    else:
                        if d.eng == op.eng and (op.eng == "pe"):
                            continue
                        key, sem = d.eng, esem[d.eng]
                    v = d.value
                    if waited.get(key, 0) >= v:
                        continue
                    if key not in best or best[key][1] < v:
                        best[key] = (sem, v)
                for key, (sem, v) in best.items():
                    waited[key] = v
                return list(best.values())

            def run(eng_name, e):
                waited = {}
                for op in self.ops[eng_name]:
                    ws = resolve(op, waited)
                    if op.multi:
                        for (s, v) in ws:
                            e.wait_ge(s, v)
                        ins = op.fn(e)
                    else:
                        for (s, v) in ws[:-1]:
                            e.wait_ge(s, v)
                        ins = op.fn(e)
                        if ws:
                            ins._wait_ge(ws[-1][0], ws[-1][1])
                    if op.is_dma:
                        ins.then_inc(op.dbuf.dsem, 16)
                    elif op.needed:
                        ins.then_inc(esem[eng_name], 1)
                if eng_name == "sp":
                    tot = {}
                    for o in self.all_ops:
                        if o.is_dma:
                            tot[id(o.dbuf)] = (o.dbuf.dsem, o.value)
                    for (s, v) in tot.values():
                        e.wait_ge(s, v)

            @block.tensor
            def _(e):
                run("pe", e)

            @block.scalar
            def _(e):
                run("act", e)

            @block.vector
            def _(e):
                run("dve", e)

            @block.gpsimd
            def _(e):
                run("pool", e)

            @block.sync
            def _(e):
                run("sp", e)


class Ctx:
    pass


def _alloc(nc, st, name, shape, dt):
    return st.enter_context(nc.sbuf_tensor(name, list(shape), dt))


def build(T, phases, depth=4, dbg=None):
    nc = bass.Bass("TRN2", target_bir_lowering=False)
    p = Prog(nc)
    g = Ctx()
    g.nc, g.p, g.T = nc, p, T
    NT = T // 512
    g.NT = NT

    decl = {}

    def dram(name, shape, dt=F32, kind="ExternalInput"):
        if kind == "ExternalInput":
            decl[name] = tuple(shape)
        return nc.dram_tensor(name, list(shape), dt, kind=kind).ap()

    nc._decl_inputs = decl

    g.x_in = dram("xT", [D, T])
    g.ctx_in = dram("ctxT", [D, CTX])
    g.cvec = dram("cvec", [128, 16])
    g.ada_w = dram("ada_w", [depth, D, 6 * D])
    g.ada_b = dram("ada_bT", [depth, 128, 48])
    g.normT = dram("normT", [depth, 128, 16])
    g.w_gate = dram("ffn_w_gate", [depth, D, FH])
    g.w_up = dram("ffn_w_up", [depth, D, FH])
    g.w_down = dram("ffn_w_down", [depth, FH, D])
    g.attn_wqkv = dram("attn_wqkv", [2, D, 1536])
    g.attn_wo = dram("attn_wo", [2, D, D])
    g.attn_qn = dram("attn_qn", [2, 128, 64])
    g.attn_kn = dram("attn_kn", [2, 128, 64])
    g.attn_qns = dram("attn_qns", [2, 128, 64])
    g.attn_kns = dram("attn_kns", [2, 128, 64])
    g.attn_sinkB = dram("attn_sinkB", [2, 128, 16])
    g.attn_cs = dram("attn_cs", [T, 96])
    g.attn_masks = dram("attn_masks", [128, 1024])
    g.ident = dram("ident", [128, 128])
    g.ret_win = dram("ret_w_in", [1, D, 8192])
    g.ret_wo = dram("ret_w_o", [1, 2048, D])
    g.ret_cs = dram("ret_cs", [2, 128, T])
    g.ret_dec = dram("ret_dec", [2, 128, 1024])
    g.ret_masks = dram("ret_masks", [2, 128, 128])
    g.ZT = dram("ZT", [2048, T], BF16, kind="Internal")
    g.ZTC = dram("ZTC", [2048, CTX], BF16, kind="Internal")
    g.VS = dram("VS", [T, 2048], BF16, kind="Internal")
    g.VSC = dram("VSC", [CTX, 2048], BF16, kind="Internal")
    g.pool_w = dram("pool_w", [1, 4, 256, 256])
    g.pool_scT = dram("pool_scT", [1, 128, 8])
    g.pool_rc = dram("pool_rc", [4, 128, 4 * 512])
    g.y = dram("yT", [D, T], kind="ExternalOutput")
    g.X2 = dram("X2", [D, T], kind="Internal")
    g.XC2 = dram("XC2", [D, CTX], kind="Internal")
    g.X1 = dram("X1", [D, T], kind="Internal")
    g.XC = dram("XC", [D, CTX], kind="Internal")
    g.dbg = {}
    if dbg:
        for k, shp in dbg.items():
            g.dbg[k] = dram(k, shp, kind="ExternalOutput")

    with contextlib.ExitStack() as st:
        g.st = st
        g.ones_bf = _alloc(nc, st, "ones_bf", [128, 128], BF16)
        g.ADA = _alloc(nc, st, "ADA", [128, depth, 48, 2], F32)
        g.AB = _alloc(nc, st, "AB", [128, depth, 2, 2, 8, 2], F32)
        g.eps_t = _alloc(nc, st, "eps_t", [128, 1], F32)
        g.eps64 = _alloc(nc, st, "eps64", [128, 1], F32)
        g.b_eps = Buf("eps")
        g.b_ADA = Buf("ADA")
        g.b_ones = Buf("ones")
        g.psum = [st.enter_context(nc.psum_tensor("ps%d" % i, [128, 512], F32)) for i in range(8)]
        g.b_ps = [Buf("ps%d" % i) for i in range(8)]
        g.ARENA_W = 51000
        g.ST = [Buf("ST0"), Buf("ST1"), Buf("ST2")]
        g.OUT = [Buf("OUT0"), Buf("OUT1"), Buf("OUT2")]
        g.arena = _alloc(nc, st, "arena", [128, g.ARENA_W], F32)

        p.memset("dve", g.ones_bf[:], (1.0 / D) if USE_POW else 1.0, [g.b_ones])
        p.memset("dve", g.eps_t[:], EPS, [g.b_eps])
        p.memset("dve", g.eps64[:], 64 * EPS, [g.b_eps])
        for ph in phases:
            ph(g)
            p.barrier()
        finals = [b for b in p.dma_bufs if b.name.startswith("OUT")]
        p.emit(finals)
    return nc


class Arena:
    def __init__(self, g):
        self.g = g
        self.off = 0

    def f32(self, words, shape=None):
        a = self.g.arena[:, self.off:self.off + words]
        self.off += words
        assert self.off <= self.g.ARENA_W, self.off
        return a

    def bf16(self, elems):
        words = (elems + 1) // 2
        a = self.g.arena[:, self.off:self.off + words].bitcast(BF16)
        self.off += words
        assert self.off <= self.g.ARENA_W, self.off
        return a


def v3(ap, a, b):
    return ap.rearrange("p (a b) -> p a b", a=a, b=b)


def phase_ada(g, depth=4):
    nc, p = g.nc, g.p
    ar = Arena(g)
    cv = ar.f32(16)
    cond = ar.f32(16)
    sig = ar.f32(16)
    badab = ar.f32(depth * 48)
    nrm = ar.f32(depth * 2 * 8)
    wst = [ar.f32(8 * 512) for _ in range(3)]
    b_cv, b_cond, b_bias, b_nrm = Buf("cv"), Buf("cond"), Buf("adab"), Buf("nrm")
    b_sig = Buf("sig")
    b_w = [Buf("adaw%d" % i) for i in range(3)]
    p.dma("sp", cv, g.cvec, [], [b_cv])
    for l in range(depth):
        p.dma("sp", badab[:, l * 48:(l + 1) * 48], g.ada_b[l], [], [b_bias])
        p.dma("sp", nrm[:, (l * 2) * 8:(l * 2 + 2) * 8], g.normT[l], [], [b_nrm])
    p.act(sig, cv, AF.Sigmoid, [b_cv], [b_sig])
    p.tt("dve", cond, cv, sig, ALU.mult, [b_cv, b_sig], [b_cond])
    cond3 = v3(cond, 8, 2)
    ps = g.psum[0]
    bps = g.b_ps[0]
    piece = 0
    for l in range(depth):
        for nb in range(12):
            slot = piece % 3
            piece += 1
            w3 = v3(wst[slot], 8, 512)
            p.dma("sp", w3, g.ada_w[l].rearrange("(c q) n -> q c n", q=128)[:, :, nb * 512:(nb + 1) * 512],
                  [], [b_w[slot]])
            for j in range(4):
                n = nb * 4 + j
                for c in range(8):
                    p.mm(ps[:, 2 * n:2 * n + 2], w3[:, c, j * 128:(j + 1) * 128], cond3[:, c, :],
                         (c == 0), (c == 7), [b_w[slot], b_cond], [bps])
        bb = badab[:, l * 48:(l + 1) * 48]
        for t in range(2):
            p.tt("dve", g.ADA[:, l, :, t], v3(ps[:, 0:96], 48, 2)[:, :, t], bb, ALU.add,
                 [bps, b_bias], [g.b_ADA])
        for s in range(2):
            nr = nrm[:, (l * 2 + s) * 8:(l * 2 + s + 1) * 8]
            for t in range(2):
                sc = g.ADA[:, l, s * 24 + 8:s * 24 + 16, t]
                sh = g.ADA[:, l, s * 24 + 0:s * 24 + 8, t]
                p.stt(g.AB[:, l, s, 0, :, t], sc, 1.0, nr, ALU.add, ALU.mult, [g.b_ADA, b_nrm], [g.b_ADA])
                p.copy("dve", g.AB[:, l, s, 1, :, t], sh, [g.b_ADA], [g.b_ADA])


def vecs(g, l, s, t):
    A = g.AB[:, l, s, 0, :, t]
    B = g.AB[:, l, s, 1, :, t]
    G = g.ADA[:, l, s * 24 + 16:s * 24 + 24, t]
    return A, B, G


def load_w_cast(g, dst3, src3, buf, ncols, rows):
    p = g.p
    step = 1024
    for c0 in range(0, ncols, step):
        c1 = min(ncols, c0 + step)
        for r0 in range(0, rows, 4):
            r1 = min(rows, r0 + 4)
            p.dma("pool", dst3[:, r0:r1, c0:c1], src3[:, r0:r1, c0:c1], [], [buf])


def prenorm(g, x3, N, A, B, h3, b_x, b_h, ps_i=0, ps_t=7, sq3=None, b_sq=None, lnexp=False):
    p = g.p
    ps = g.psum[ps_i][:, 0:N]
    bps = g.b_ps[ps_i]
    pt = g.psum[ps_t][:, 0:N]
    bpt = g.b_ps[ps_t]
    if sq3 is None:
        sq3, b_sq = h3, b_h
    p.act(sq3, x3, AF.Square, [b_x], [b_sq])
    for c in range(8):
        p.mm(ps, g.ones_bf[:], sq3[:, c, :], (c == 0), (c == 7), [b_sq, g.b_ones], [bps])
    if USE_POW:
        p.ts("dve", ps, ps, EPS, -0.5, ALU.add, ALU.pow, [bps], [bps])
    else:
        p.act(ps, ps, AF.Sqrt, [bps, g.b_eps], [bps], scale=1.0 / D, bias=g.eps_t[:, 0:1])
        p.call("dve", "reciprocal", [bps], [bps], out=ps, in_=ps)
    for c in range(8):
        p.stt(pt, x3[:, c, :], A[:, c:c + 1], ps, ALU.mult, ALU.mult, [b_x, bps, g.b_ADA], [bpt])
        p.act(h3[:, c, :], pt, AF.Identity, [bpt, g.b_ADA], [b_h], scale=1.0, bias=B[:, c:c + 1])


def prenorm_stats(g, x3, N, h3, b_x, b_h, ps_i=0, rstd=None, b_rstd=None):
    p = g.p
    ps = g.psum[ps_i][:, 0:N]
    bps = g.b_ps[ps_i]
    p.act(h3, x3, AF.Square, [b_x], [b_h])
    for c in range(8):
        p.mm(ps, g.ones_bf[:], h3[:, c, :], (c == 0), (c == 7), [b_h, g.b_ones], [bps])
    p.act(ps, ps, AF.Sqrt, [bps, g.b_eps], [bps], scale=1.0 / D, bias=g.eps_t[:, 0:1])
    if rstd is None:
        p.call("dve", "reciprocal", [bps], [bps], out=ps, in_=ps)
    else:
        p.call("dve", "reciprocal", [bps], [b_rstd], out=rstd, in_=ps)


def prenorm_apply(g, x3, N, A, B, h3, b_x, b_h, c, tmp, b_tmp, ps_i=0, rstd=None, b_rstd=None):
    p = g.p
    if rstd is None:
        rstd, b_rstd = g.psum[ps_i][:, 0:N], g.b_ps[ps_i]
    p.stt(tmp, x3[:, c, :], A[:, c:c + 1], rstd, ALU.mult, ALU.mult, [b_x, b_rstd, g.b_ADA], [b_tmp])
    p.act(h3[:, c, :], tmp, AF.Identity, [b_tmp, g.b_ADA], [b_h], scale=1.0, bias=B[:, c:c + 1])


def phase_pool(g, l, slot, src, dst, src_c, dst_c):
    nc, p = g.nc, g.p
    ar = Arena(g)
    W = 528
    pw = ar.bf16(4 * 2 * 256)
    pw4 = pw.rearrange("p (g k d) -> p g k d", g=4, k=2, d=256)
    b_pw = Buf("pw")
    load_w_cast(g, pw.rearrange("p (r d) -> p r d", r=8, d=256),
                g.pool_w[slot].rearrange("g (k q) d -> q (g k) d", q=128), b_pw, 256, 8)
    rc = [ar.f32(4 * 512) for _ in range(4)]
    b_rc = Buf("rc")
    for v in range(4):
        p.dma("sp", rc[v], g.pool_rc[v], [], [b_rc])
    psc = ar.f32(8)
    GS = ar.f32(16)
    b_psc, b_GS = Buf("psc"), Buf("GS")
    p.dma("sp", psc, g.pool_scT[slot], [], [b_psc])
    for t in range(2):
        _, _, G = vecs(g, l, 0, t)
        p.tt("dve", v3(GS, 8, 2)[:, :, t], G, psc, ALU.mult, [g.b_ADA, b_psc], [b_GS])
    xs = [ar.f32(8 * W) for _ in range(2)]
    b_xs = [Buf("px0"), Buf("px1")]
    hp = ar.f32(8 * W)
    sq = ar.bf16(8 * W)
    S2, S4, S8, S16 = ar.f32(8 * W), ar.f32(6 * W), ar.f32(4 * W), ar.f32(2 * W)
    tmp = ar.f32(2 * 512)
    pooled = ar.bf16(8 * 512)
    prs = ar.f32(W)
    b_prs = Buf("prs")
    ptm = [ar.f32(W) for _ in range(2)]
    b_ptm = [Buf("ptm0"), Buf("ptm1")]
    b_hp, b_sq, b_S2, b_S4, b_S8, b_S16, b_tmp = (Buf(n) for n in ("hp", "sq", "S2", "S4", "S8", "S16", "ptmp"))
    b_pl = [Buf("pl%d" % i) for i in range(4)]

    tiles = [(src, dst, t * 512, 512, 0, g.T) for t in range(g.NT)] + [(src_c, dst_c, 0, CTX, 1, CTX)]

    def issue_load(i):
        s_, d_, t0, N, strm, Ttot = tiles[i]
        x3 = v3(xs[i % 2], 8, W)
        lo, hi = max(t0 - 8, 0), min(t0 + N + 8, Ttot)
        c0 = lo - (t0 - 8)
        if lo != t0 - 8:
            p.memset("pool", x3[:, :, 0:8], 0.0, [b_xs[i % 2]])
        if hi != t0 + N + 8:
            p.memset("pool", x3[:, :, N + 8:N + 16], 0.0, [b_xs[i % 2]])
        p.dma("sp", x3[:, :, c0:c0 + (hi - lo)], s_.rearrange("(c q) t -> q c t", q=128)[:, :, lo:hi],
              [], [b_xs[i % 2]])

    issue_load(0)
    for i, (s_, d_, t0, N, strm, Ttot) in enumerate(tiles):
        if i + 1 < len(tiles):
            issue_load(i + 1)
        WN = N + 16
        x3 = v3(xs[i % 2], 8, W)
        b_x = b_xs[i % 2]
        h3 = v3(hp, 8, W)
        sq3 = v3(sq, 8, W)
        A, B, _ = vecs(g, l, 0, strm)
        p.act(sq3[:, :, 0:WN], x3[:, :, 0:WN], AF.Square, [b_x], [b_sq])
        for gi_, (a0, a1) in enumerate(((0, min(512, WN)), (512, WN))):
            if a1 > a0:
                psr = g.psum[(0, 7)[gi_]][:, 0:a1 - a0]
                bpsr = g.b_ps[(0, 7)[gi_]]
                for c in range(8):
                    p.mm(psr, g.ones_bf[:], sq3[:, c, a0:a1], (c == 0), (c == 7), [b_sq, g.b_ones], [bpsr])
                p.act(psr, psr, AF.Sqrt, [bpsr, g.b_eps], [bpsr], scale=1.0 / D, bias=g.eps_t[:, 0:1])
                p.call("dve", "reciprocal", [bpsr], [b_prs], out=prs[:, a0:a1], in_=psr)
        for c in range(8):
            p.stt(ptm[c % 2][:, 0:WN], x3[:, c, 0:WN], A[:, c:c + 1], prs[:, 0:WN], ALU.mult, ALU.mult,
                  [b_x, b_prs, g.b_ADA], [b_ptm[c % 2]])
            p.act(h3[:, c, 0:WN], ptm[c % 2][:, 0:WN], AF.Identity, [b_ptm[c % 2], g.b_ADA], [b_hp],
                  scale=1.0, bias=B[:, c:c + 1])
        if t0 == 0:
            p.memset("pool", h3[:, :, 0:8], 0.0, [b_hp])
        if t0 + N == Ttot:
            p.memset("pool", h3[:, :, N + 8:N + 16], 0.0, [b_hp])
        s2, s4, s8, s16 = v3(S2, 8, W), v3(S4, 6, W), v3(S8, 4, W), v3(S16, 2, W)
        p.tt("dve", s2[:, :, 1:WN], h3[:, :, 0:WN - 1], h3[:, :, 1:WN], ALU.add, [b_hp], [b_S2])
        p.tt("dve", s4[:, :, 2:WN - 1], s2[:, 2:8, 1:WN - 2], s2[:, 2:8, 3:WN], ALU.add, [b_S2], [b_S4])
        p.tt("pool", s8[:, :, 4:WN - 3], s4[:, 2:6, 2:WN - 5], s4[:, 2:6, 6:WN - 1], ALU.add, [b_S4], [b_S8])
        p.tt("pool", s16[:, :, 8:WN - 8], s8[:, 2:4, 4:WN - 12], s8[:, 2:4, 12:WN - 4], ALU.add, [b_S8], [b_S16])
        if strm == 1:
            var = 3
        elif t0 == 0:
            var = 1
        elif t0 + N == Ttot:
            var = 2
        else:
            var = 0
        rcv = v3(rc[var], 4, 512)
        srcs = [(s2, 0, b_S2), (s4, 0, b_S4), (s8, 0, b_S8), (s16, 0, b_S16)]
        pl3 = v3(pooled, 8, 512)
        tmp3 = v3(tmp, 2, 512)
        for gi in range(4):
            sw, _, b_sw = srcs[gi]
            if var == 0:
                for k in range(2):
                    p.stt(pl3[:, gi * 2 + k, 0:N], sw[:, k, 8:8 + N], 1.0 / (2, 4, 8, 16)[gi], h3[:, gi * 2 + k, 8:8 + N],
                          ALU.mult, ALU.subtract, [b_sw, b_hp], [b_pl[gi]])
                continue
            for k in range(2):
                eng = "dve" if k == 0 else "pool"
                p.tt(eng, tmp3[:, k, 0:N], sw[:, k, 8:8 + N], rcv[:, gi, 0:N], ALU.mult, [b_sw, b_rc], [b_tmp])
                p.tt(eng, pl3[:, gi * 2 + k, 0:N], tmp3[:, k, 0:N], h3[:, gi * 2 + k, 8:8 + N], ALU.subtract,
                     [b_tmp, b_hp], [b_pl[gi]])
        gsv = v3(GS, 8, 2)[:, :, strm]
        for gi in range(4):
            for oc in range(2):
                c = gi * 2 + oc
                po = 5 + (c % 2)
                for k in range(2):
                    p.mm(g.psum[po][:, 0:N], pw4[:, gi, k, oc * 128:(oc + 1) * 128], pl3[:, gi * 2 + k, 0:N],
                         (k == 0), (k == 1), [b_pw, b_pl[gi]], [g.b_ps[po]])
                p.stt(x3[:, c, 8:8 + N], g.psum[po][:, 0:N], gsv[:, c:c + 1], x3[:, c, 8:8 + N],
                      ALU.mult, ALU.add, [g.b_ps[po], b_x, b_GS], [b_x])
        ob = (g.OUT if d_ is g.y else g.ST)[i % 2]
        p.dma("sp", d_.rearrange("(c q) t -> q c t", q=128)[:, :, t0:t0 + N], x3[:, :, 8:8 + N], [b_x], [ob])


def phase_ffn(g, l, src, dst, src_c, dst_c, with_ctx=True):
    nc, p = g.nc, g.p
    ar = Arena(g)
    wg = ar.bf16(8 * FH)
    wu = ar.bf16(8 * FH)
    wd = ar.bf16(NJ * D)
    wg3, wu3, wd3 = v3(wg, 8, FH), v3(wu, 8, FH), v3(wd, NJ, D)
    b_wg, b_wu, b_wd = Buf("wg"), Buf("wu"), Buf("wd")
    xs = [ar.f32(8 * 512) for _ in range(2)]
    b_xs = [Buf("x%d" % i) for i in range(2)]
    h = ar.bf16(8 * 512)
    a = ar.bf16(NJ * 512)
    sg = ar.f32(512)
    b_h = Buf("h")
    b_a = [Buf("a%d" % j) for j in range(NJ)]
    b_sg = Buf("sg0")

    load_w_cast(g, wg3, g.w_gate[l].rearrange("(c q) n -> q c n", q=128), b_wg, FH, 8)
    load_w_cast(g, wu3, g.w_up[l].rearrange("(c q) n -> q c n", q=128), b_wu, FH, 8)
    load_w_cast(g, wd3, g.w_down[l].rearrange("(j q) n -> q j n", q=128), b_wd, D, NJ)

    tiles = [(src, dst, t * 512, 512, 0) for t in range(g.NT)]
    if with_ctx:
        tiles.append((src_c, dst_c, 0, CTX, 1))

    def issue_load(i):
        s_, d_, t0, N, strm = tiles[i]
        x3 = v3(xs[i % 2], 8, 512)[:, :, 0:N]
        p.dma("sp", x3, s_.rearrange("(c q) t -> q c t", q=128)[:, :, t0:t0 + N], [], [b_xs[i % 2]])

    def do_prenorm(i):
        s_, d_, t0, N, strm = tiles[i]
        A, B, _ = vecs(g, l, 1, strm)
        prenorm(g, v3(xs[i % 2], 8, 512)[:, :, 0:N], N, A, B, v3(h, 8, 512)[:, :, 0:N], b_xs[i % 2], b_h)

    issue_load(0)
    if len(tiles) > 1:
        issue_load(1)
    do_prenorm(0)
    for i, (s_, d_, t0, N, strm) in enumerate(tiles):
        x3 = v3(xs[i % 2], 8, 512)[:, :, 0:N]
        b_x = b_xs[i % 2]
        h3 = v3(h, 8, 512)[:, :, 0:N]
        a3 = v3(a, NJ, 512)[:, :, 0:N]
        _, _, G = vecs(g, l, 1, strm)
        for j in range(NJ):
            pg, pu = 1 + 2 * (j % 2), 2 + 2 * (j % 2)
            for c in range(8):
                p.mm(g.psum[pg][:, 0:N], wg3[:, c, j * 128:(j + 1) * 128], h3[:, c, :],
                     (c == 0), (c == 7), [b_wg, b_h], [g.b_ps[pg]])
            for c in range(8):
                p.mm(g.psum[pu][:, 0:N], wu3[:, c, j * 128:(j + 1) * 128], h3[:, c, :],
                     (c == 0), (c == 7), [b_wu, b_h], [g.b_ps[pu]])
            sgt = sg[:, 0:N]
            p.act(sgt, g.psum[pg][:, 0:N], AF.Silu, [g.b_ps[pg]], [b_sg])
            p.tt("dve", a3[:, j, :], g.psum[pu][:, 0:N], sgt, ALU.mult, [g.b_ps[pu], b_sg], [b_a[j]])
        for c in range(8):
            po = 5 + (c % 2)
            for j in range(NJ):
                p.mm(g.psum[po][:, 0:N], wd3[:, j, c * 128:(c + 1) * 128], a3[:, j, :],
                     (j == 0), (j == NJ - 1), [b_wd, b_a[j]], [g.b_ps[po]])
            p.stt(x3[:, c, :], g.psum[po][:, 0:N], G[:, c:c + 1], x3[:, c, :], ALU.mult, ALU.add,
                  [g.b_ps[po], b_x, g.b_ADA], [b_x])
            if c == 1 and i + 1 < len(tiles):
                do_prenorm(i + 1)
        ob = (g.OUT if d_ is g.y else g.ST)[i % 2]
        p.dma("sp", d_.rearrange("(c q) t -> q c t", q=128)[:, :, t0:t0 + N], x3, [b_x], [ob])
        if i + 2 < len(tiles):
            issue_load(i + 2)


def host_small(c_b, c_ctx, ada_b, norm_mix, norm_ffn):
    depth = ada_b.shape[0]
    cvec = np.stack([c_b.reshape(8, 128).T, c_ctx.reshape(8, 128).T], axis=2).reshape(128, 16)
    ada_bT = np.ascontiguousarray(ada_b.reshape(depth, 48, 128).transpose(0, 2, 1))
    normT = np.stack([norm_mix.reshape(depth, 8, 128).transpose(0, 2, 1),
                      norm_ffn.reshape(depth, 8, 128).transpose(0, 2, 1)], axis=2).reshape(depth, 128, 16)
    return dict(cvec=np.ascontiguousarray(cvec, dtype=np.float32), ada_bT=ada_bT.astype(np.float32),
                normT=np.ascontiguousarray(normT, dtype=np.float32))


def host_pool_rc(T):
    out = np.zeros((4, 4, 512), np.float32)
    for gi, w in enumerate((2, 4, 8, 16)):
        def rcp(Ttot, t):
            lo = np.maximum(t - w // 2, 0)
            hi = np.minimum(t + w // 2, Ttot)
            return (1.0 / (hi - lo)).astype(np.float32)
        out[0, gi, :] = 1.0 / w
        out[1, gi, :] = rcp(T, np.arange(512))
        out[2, gi, :] = rcp(T, np.arange(T - 512, T))
        out[3, gi, :256] = rcp(CTX, np.arange(256))
        out[3, gi, 256:] = 1.0 / w
    return np.ascontiguousarray(np.broadcast_to(out.reshape(4, 1, 2048), (4, 128, 2048)))


def qk_norm_rot(g, src, nh, gain, cs, is_q, out_bf, b_src, b_gain, b_cs, b_out, W):
    qk_part1(g, src, nh, W["st"][:, 0:nh], b_src, W)
    p = g.p
    st, b_st = W["st"][:, 0:nh], W["b_st"]
    if is_q:
        p.act(st, st, AF.Sqrt, [b_st, W["b_e"]], [b_st], scale=1.0, bias=W["eps64"][:, 0:1])
    else:
        p.act(st, st, AF.Sqrt, [b_st, g.b_eps], [b_st], scale=1.0 / 64, bias=g.eps_t[:, 0:1])
    p.call("dve", "reciprocal", [b_st], [b_st], out=st, in_=st)
    qk_part2(g, src, nh, gain, cs, st, b_st, out_bf, b_src, b_gain, b_cs, b_out, W)


def qk_part1(g, src, nh, st, b_src, W):
    p = g.p
    n = nh * 64
    sq = W["sq"][:, 0:n]
    p.act(sq, src, AF.Square, [b_src], [W["b_sq"]])
    p.call("dve", "tensor_reduce", [W["b_sq"]], [W["b_st"]], out=st, in_=v3(sq, nh, 64), op=ALU.add, axis=AX.X)


def qk_part2(g, src, nh, gain, cs, st, b_st, out_bf, b_src, b_gain, b_cs, b_out, W):
    p = g.p
    n = nh * 64
    qn, t2 = W["qn"][:, 0:n], W["t2"][:, 0:n]
    b_qn, b_t2 = W["b_qn"], W["b_t2"]
    p.tt("dve", v3(qn, nh, 64), v3(src, nh, 64), st.unsqueeze(2).broadcast_to([128, nh, 64]), ALU.mult,
         [b_src, b_st], [b_qn])
    if cs is None:
        p.tt("pool", v3(out_bf, nh, 64), v3(qn, nh, 64), gain.unsqueeze(1).broadcast_to([128, nh, 64]), ALU.mult,
             [b_qn, b_gain], [b_out])
        return
    p.tt("pool", v3(qn, nh, 64), v3(qn, nh, 64), gain.unsqueeze(1).broadcast_to([128, nh, 64]), ALU.mult,
         [b_qn, b_gain], [b_qn])
    q4 = qn.rearrange("p (h two d) -> p h two d", h=nh, two=2, d=32)
    t4 = t2.rearrange("p (h two d) -> p h two d", h=nh, two=2, d=32)
    cosb = cs[:, 0:32].unsqueeze(1).broadcast_to([128, nh, 32])
    sinb = cs[:, 32:64].unsqueeze(1).broadcast_to([128, nh, 32])
    nsinb = cs[:, 64:96].unsqueeze(1).broadcast_to([128, nh, 32])
    p.tt("pool", t4[:, :, 0, :], q4[:, :, 1, :], nsinb, ALU.mult, [b_qn, b_cs], [b_t2])
    p.tt("pool", t4[:, :, 1, :], q4[:, :, 0, :], sinb, ALU.mult, [b_qn, b_cs], [b_t2])
    ce = "pool" if W.get("pool_only") else "dve"
    for two in range(2):
        p.tt(ce, q4[:, :, two, :], q4[:, :, two, :], cosb, ALU.mult, [b_qn, b_cs, b_t2], [b_qn])
    p.tt("pool", out_bf, qn, t2, ALU.add, [b_qn, b_t2], [b_out])


def phase_attn_v1(g, l, slot, src, dst, src_c, dst_c, need_ctx_out):
    nc, p = g.nc, g.p
    T, NT = g.T, g.NT
    NB = T // 128
    ar = Arena(g)
    KT = ar.bf16(2 * (T + CTX))
    KT3 = v3(KT, 2, T + CTX)
    VA = ar.bf16((NB + 2) * 4 * 65)
    VA4 = VA.rearrange("p (b g d) -> p b g d", b=NB + 2, g=4, d=65)
    wqkv = ar.bf16(8 * 1536)
    wo = ar.bf16(8 * 1024)
    wqkv3, wo3 = v3(wqkv, 8, 1536), v3(wo, 8, 1024)
    b_wqkv, b_wo = Buf("wqkv"), Buf("wo")
    load_w_cast(g, wqkv3, g.attn_wqkv[slot].rearrange("(c q) n -> q c n", q=128), b_wqkv, 1536, 8)
    load_w_cast(g, wo3, g.attn_wo[slot].rearrange("(c q) n -> q c n", q=128), b_wo, 1024, 8)
    ident = ar.bf16(128)
    masks = ar.bf16(2 * 512)
    b_ident, b_masks = Buf("ident"), Buf("masks")
    p.dma("pool", ident, g.ident, [], [b_ident])
    p.dma("pool", masks, g.attn_masks, [], [b_masks])
    gq, gk = ar.f32(64), ar.f32(64)
    b_gq, b_gk = Buf("gq"), Buf("gk")
    p.dma("sp", gq, g.attn_qn[slot], [], [b_gq])
    p.dma("sp", gk, g.attn_kn[slot], [], [b_gk])
    esink = ar.f32(16)
    b_esink = Buf("esink")
    p.dma("sp", esink, g.attn_sinkB[slot], [], [b_esink])
    p.act(esink, esink, AF.Exp, [b_esink], [b_esink])
    W = dict(sq=ar.f32(512), qn=ar.f32(512), t2=ar.f32(512), st=ar.f32(8), eps64=ar.f32(1),
             b_sq=Buf("sq"), b_qn=Buf("qn"), b_t2=Buf("t2"), b_st=Buf("st"), b_e=Buf("e64"))
    p.memset("dve", W["eps64"], 64 * EPS, [W["b_e"]])
    xs = [ar.f32(8 * 512) for _ in range(2)]
    b_xs = [Buf("ax0"), Buf("ax1")]
    cst = [ar.f32(4 * 96) for _ in range(2)]
    b_cst = [Buf("cs0"), Buf("cs1")]
    h = ar.bf16(8 * 512)
    b_h = Buf("ah")
    krot = ar.bf16(256)
    qrot = ar.bf16(1024)
    b_krot, b_qrot = Buf("krot"), [Buf("qrot0"), Buf("qrot1")]
    QT = ar.bf16(8 * 128)
    QT3 = v3(QT, 8, 128)
    b_QT = Buf("QT")
    PT = [ar.bf16(5 * 512) for _ in range(2)]
    b_PT = [[Buf("PT%d_%d" % (s, c)) for c in range(5)] for s in range(2)]
    O = ar.bf16(1024)
    b_O = Buf("O")
    OT = ar.bf16(8 * 512)
    OT3 = v3(OT, 8, 512)
    b_OT = Buf("OT")
    den = ar.f32(4)
    b_den = Buf("den")
    b_KT = [Buf("KT%d" % i) for i in range(NB + 2)]
    b_V = [Buf("V%d" % i) for i in range(NB + 2)]
    p.memset("pool", VA4[:, :, :, 64:65], 1.0, b_V)
    psT = g.psum[3][:].bitcast(BF16)
    psT3 = v3(psT, 8, 128)
    b_psT = g.b_ps[3]

    lat_tiles = [(src, dst, t * 512, 512, 0) for t in range(NT)]
    ctx_tile = (src_c, dst_c, 0, CTX, 1)

    def load(i, tile):
        s_, d_, t0, N, strm = tile
        x3 = v3(xs[i % 2], 8, 512)[:, :, 0:N]
        p.dma("sp", x3, s_.rearrange("(c q) t -> q c t", q=128)[:, :, t0:t0 + N], [], [b_xs[i % 2]])
        if strm == 0:
            p.dma("sp", v3(cst[i % 2], 4, 96), g.attn_cs[t0:t0 + 512].rearrange("(b q) d -> q b d", q=128),
                  [], [b_cst[i % 2]])

    tilesA = [ctx_tile] + lat_tiles
    load(0, tilesA[0])
    for i, tile in enumerate(tilesA):
        if i + 1 < len(tilesA):
            load(i + 1, tilesA[i + 1])
        s_, d_, t0, N, strm = tile
        x3 = v3(xs[i % 2], 8, 512)[:, :, 0:N]
        h3 = v3(h, 8, 512)[:, :, 0:N]
        A, B, _ = vecs(g, l, 0, strm)
        prenorm(g, x3, N, A, B, h3, b_xs[i % 2], b_h)
        for bl in range(N // 128):
            kb = (NB + bl) if strm == 1 else (t0 // 128 + bl)
            kcol = (T + bl * 128) if strm == 1 else (t0 + bl * 128)
            pk = 1 + (bl % 2)
            for c in range(8):
                p.mm(g.psum[pk][:, 0:512], h3[:, c, bl * 128:(bl + 1) * 128], wqkv3[:, c, 1024:1536],
                     (c == 0), (c == 7), [b_h, b_wqkv], [g.b_ps[pk]])
            cs = v3(cst[i % 2], 4, 96)[:, bl, :] if strm == 0 else None
            qk_norm_rot(g, g.psum[pk][:, 0:256], 4, gk, cs, False, krot, g.b_ps[pk], b_gk, b_cst[i % 2], b_krot, W)
            p.copy("act", VA4[:, kb, :, 0:64], v3(g.psum[pk][:, 256:512], 4, 64), [g.b_ps[pk]], [b_V[kb]])
            for pr in range(2):
                p.call("pe", "transpose", [b_krot, b_ident], [b_psT], psT3[:, pr, :],
                       krot[:, pr * 128:(pr + 1) * 128], ident)
            p.copy("dve", KT3[:, :, kcol:kcol + 128], psT3[:, 0:2, :], [b_psT], [b_KT[kb]])

    tilesB = lat_tiles + ([ctx_tile] if need_ctx_out else [])
    base = len(tilesA)
    load(base, tilesB[0])
    for ii, tile in enumerate(tilesB):
        i = base + ii
        if ii + 1 < len(tilesB):
            load(i + 1, tilesB[ii + 1])
        s_, d_, t0, N, strm = tile
        x3 = v3(xs[i % 2], 8, 512)[:, :, 0:N]
        b_x = b_xs[i % 2]
        h3 = v3(h, 8, 512)[:, :, 0:N]
        A, B, G = vecs(g, l, 0, strm)
        prenorm(g, x3, N, A, B, h3, b_x, b_h)
        for bl in range(N // 128):
            blk = t0 // 128 + bl
            for hf in range(2):
                for c in range(8):
                    p.mm(g.psum[1 + hf][:, 0:512], h3[:, c, bl * 128:(bl + 1) * 128],
                         wqkv3[:, c, hf * 512:(hf + 1) * 512], (c == 0), (c == 7), [b_h, b_wqkv], [g.b_ps[1 + hf]])
            cs = v3(cst[i % 2], 4, 96)[:, bl, :] if strm == 0 else None
            for hf in range(2):
                qk_norm_rot(g, g.psum[1 + hf][:, 0:512], 8, gq, cs, True, qrot[:, hf * 512:(hf + 1) * 512],
                            g.b_ps[1 + hf], b_gq, b_cst[i % 2], b_qrot[hf], W)
            for s in range(8):
                p.call("pe", "transpose", [b_qrot[s // 4], b_ident], [b_psT], psT3[:, s, :],
                       qrot[:, s * 128:(s + 1) * 128], ident)
            p.copy("dve", QT, psT, [b_psT], [b_QT])
            chunks = [(T, NB, None), (T + 128, NB + 1, None)]
            if strm == 0:
                if blk > 0:
                    chunks.append(((blk - 1) * 128, blk - 1, 0))
                chunks.append((blk * 128, blk, None))
                if blk < NB - 1:
                    chunks.append(((blk + 1) * 128, blk + 1, 1))
            for gi in range(4):
                pr, half = gi // 2, gi % 2
                P0 = half * 64
                pts = gi % 2
                PT3 = v3(PT[pts], 5, 512)
                for ci, (kcol, vb, mk) in enumerate(chunks):
                    pss = 4 + (ci % 2)
                    p.mm(g.psum[pss][:, 0:512], KT3[P0:P0 + 64, pr, kcol:kcol + 128],
                         QT3[P0:P0 + 64, pr * 4:(pr + 1) * 4, :], True, True, [b_KT[vb], b_QT], [g.b_ps[pss]])
                    p.act(PT3[:, ci, :], g.psum[pss][:, 0:512], AF.Exp, [g.b_ps[pss]], [b_PT[pts][ci]])
                    if mk is not None:
                        p.tt("pool", PT3[:, ci, :], PT3[:, ci, :], masks[:, mk * 512:(mk + 1) * 512], ALU.mult,
                             [b_PT[pts][ci], b_masks], [b_PT[pts][ci]])
                po = g.psum[6][:, 0:260].rearrange("p (h d) -> p h d", h=4, d=65)
                for hl in range(4):
                    for ci, (kcol, vb, mk) in enumerate(chunks):
                        p.mm(po[:, hl, :], PT3[:, ci, hl * 128:(hl + 1) * 128], VA4[:, vb, gi, :],
                             (ci == 0), (ci == len(chunks) - 1), [b_PT[pts][ci], b_V[vb]], [g.b_ps[6]])
                p.tt("dve", den, po[:, :, 64], esink[:, gi * 4:(gi + 1) * 4], ALU.add, [g.b_ps[6], b_esink], [b_den])
                p.call("dve", "reciprocal", [b_den], [b_den], out=den, in_=den)
                p.tt("dve", v3(O, 16, 64)[:, gi * 4:(gi + 1) * 4, :], po[:, :, 0:64],
                     den.unsqueeze(2).broadcast_to([128, 4, 64]), ALU.mult, [g.b_ps[6], b_den], [b_O])
            for c in range(8):
                p.call("pe", "transpose", [b_O, b_ident], [b_psT], psT3[:, c, :], O[:, c * 128:(c + 1) * 128], ident)
            p.copy("dve", OT3[:, :, bl * 128:(bl + 1) * 128], psT3, [b_psT], [b_OT])
        for c in range(8):
            py = 1 + (c % 2)
            for k in range(8):
                p.mm(g.psum[py][:, 0:N], wo3[:, k, c * 128:(c + 1) * 128], OT3[:, k, 0:N],
                     (k == 0), (k == 7), [b_wo, b_OT], [g.b_ps[py]])
            p.stt(x3[:, c, :], g.psum[py][:, 0:N], G[:, c:c + 1], x3[:, c, :], ALU.mult, ALU.add,
                  [g.b_ps[py], b_x, g.b_ADA], [b_x])
        ob = (g.OUT if d_ is g.y else g.ST)[i % 2]
        p.dma("sp", d_.rearrange("(c q) t -> q c t", q=128)[:, :, t0:t0 + N], x3, [b_x], [ob])


def qk_chain(g, src, nh, is_q, rot, CG, SG, gain, out_bf, b_src, b_tab, b_out, Wk):
    p = g.p
    n = nh * 64
    sq, t1, t2, st = Wk["sq"][:, 0:n], Wk["t1"][:, 0:n], Wk["t2"][:, 0:n], Wk["st"][:, 0:nh]
    b_sq, b_t1, b_t2, b_st = Wk["b_sq"], Wk["b_t1"], Wk["b_t2"], Wk["b_st"]
    p.act(sq, src, AF.Square, [b_src], [b_sq])
    p.call("dve", "tensor_reduce", [b_sq], [b_st], out=st, in_=v3(sq, nh, 64), op=ALU.add, axis=AX.X)
    if USE_LN:
        if is_q:
            p.act(st, st, AF.Ln, [b_st, g.b_eps], [b_st], scale=1.0, bias=g.eps64[:, 0:1])
        else:
            p.act(st, st, AF.Ln, [b_st, g.b_eps], [b_st], scale=1.0 / 64, bias=g.eps_t[:, 0:1])
        p.act(st, st, AF.Exp, [b_st], [b_st], scale=-0.5)
    else:
        if is_q:
            p.act(st, st, AF.Sqrt, [b_st, g.b_eps], [b_st], scale=1.0, bias=g.eps64[:, 0:1])
        else:
            p.act(st, st, AF.Sqrt, [b_st, g.b_eps], [b_st], scale=1.0 / 64, bias=g.eps_t[:, 0:1])
        p.call("dve", "reciprocal", [b_st], [b_st], out=st, in_=st)
    s3 = v3(src, nh, 64)
    stb = st.unsqueeze(2).broadcast_to([128, nh, 64])
    if not rot:
        p.tt("dve", v3(t1, nh, 64), s3, gain.unsqueeze(1).broadcast_to([128, nh, 64]), ALU.mult,
             [b_src, b_tab], [b_t1])
        p.tt("pool", v3(out_bf, nh, 64), v3(t1, nh, 64), stb, ALU.mult, [b_t1, b_st], [b_out])
        return
    p.tt("dve", v3(t1, nh, 64), s3, CG.unsqueeze(1).broadcast_to([128, nh, 64]), ALU.mult, [b_src, b_tab], [b_t1])
    s4 = src.rearrange("p (h two d) -> p h two d", h=nh, two=2, d=32)
    t4 = t2.rearrange("p (h two d) -> p h two d", h=nh, two=2, d=32)
    p.tt("dve", t4[:, :, 0, :], s4[:, :, 1, :], SG[:, 0:32].unsqueeze(1).broadcast_to([128, nh, 32]), ALU.mult,
         [b_src, b_tab], [b_t2])
    p.tt("dve", t4[:, :, 1, :], s4[:, :, 0, :], SG[:, 32:64].unsqueeze(1).broadcast_to([128, nh, 32]), ALU.mult,
         [b_src, b_tab], [b_t2])
    p.tt("pool", t1, t1, t2, ALU.add, [b_t1, b_t2], [b_t1])
    p.tt("pool", v3(out_bf, nh, 64), v3(t1, nh, 64), stb, ALU.mult, [b_t1, b_st], [b_out])


def phase_attn(g, l, slot, src, dst, src_c, dst_c, need_ctx_out):
    nc, p = g.nc, g.p
    T, NT = g.T, g.NT
    NB = T // 128
    ar = Arena(g)
    NS = 6
    KT = ar.bf16(2 * NS * 128)
    KT3 = v3(KT, 2, NS * 128)
    VA = ar.bf16(NS * 4 * 65)
    VA4 = VA.rearrange("p (b g d) -> p b g d", b=NS, g=4, d=65)
    wqkv = ar.bf16(8 * 1536)
    wo = ar.bf16(8 * 1024)
    wqkv3, wo3 = v3(wqkv, 8, 1536), v3(wo, 8, 1024)
    b_wqkv, b_wo = Buf("wqkv"), Buf("wo")
    load_w_cast(g, wqkv3, g.attn_wqkv[slot].rearrange("(c q) n -> q c n", q=128), b_wqkv, 1536, 8)
    load_w_cast(g, wo3, g.attn_wo[slot].rearrange("(c q) n -> q c n", q=128), b_wo, 1024, 8)
    ident = ar.bf16(128)
    masks = ar.bf16(2 * 512)
    b_ident, b_masks = Buf("ident"), Buf("masks")
    p.dma("pool", ident, g.ident, [], [b_ident])
    p.dma("pool", masks, g.attn_masks, [], [b_masks])
    gq, gk, gqs, gks = ar.f32(64), ar.f32(64), ar.f32(64), ar.f32(64)
    b_gn = Buf("gains")
    p.dma("sp", gq, g.attn_qn[slot], [], [b_gn])
    p.dma("sp", gk, g.attn_kn[slot], [], [b_gn])
    p.dma("sp", gqs, g.attn_qns[slot], [], [b_gn])
    p.dma("sp", gks, g.attn_kns[slot], [], [b_gn])
    esink = ar.f32(16)
    b_esink = Buf("esink")
    p.dma("sp", esink, g.attn_sinkB[slot], [], [b_esink])
    p.act(esink, esink, AF.Exp, [b_esink], [b_esink])

    def wk(name):
        return dict(sq=ar.f32(512), t1=ar.f32(512), t2=ar.f32(512), st=ar.f32(8),
                    b_sq=Buf(name + "sq"), b_t1=Buf(name + "t1"), b_t2=Buf(name + "t2"), b_st=Buf(name + "st"))
    WQ = [wk("q0"), wk("q1")]
    WK = dict(sq=ar.f32(256), t1=ar.f32(256), t2=ar.f32(256), st=ar.f32(4),
              b_sq=Buf("ksq"), b_t1=Buf("kt1"), b_t2=Buf("kt2"), b_st=Buf("kst"))
    def wk1(name, n):
        return dict(sq=ar.f32(n), qn=ar.f32(n), t2=ar.f32(n), st=ar.f32(8), eps64=g.eps64,
                    b_sq=Buf(name + "sq"), b_qn=Buf(name + "qn"), b_t2=Buf(name + "t2"), b_st=Buf(name + "st"),
                    b_e=g.b_eps)
    WQ1 = [wk1("q0", 512), wk1("q1", 512)]
    WK1 = wk1("k", 256)
    for w_ in (WQ1[0], WQ1[1], WK1):
        w_["pool_only"] = True
    ptmp = [ar.f32(512) for _ in range(2)]
    b_ptmp = [Buf("ptmp0"), Buf("ptmp1")]
    rstd_sb = ar.f32(512)
    b_rstd_sb = Buf("rstd_sb")
    tabs = [ar.f32(4 * 64) for _ in range(2)]
    b_tabs = [Buf("tab0"), Buf("tab1")]
    xs = [ar.f32(8 * 512) for _ in range(2)]
    b_xs = [Buf("ax0"), Buf("ax1")]
    cst = [ar.f32(4 * 96) for _ in range(2)]
    b_cst = [Buf("cs0"), Buf("cs1")]
    h = ar.bf16(8 * 512)
    b_h = Buf("ah")
    h2 = ar.bf16(8 * 512)
    b_h2 = Buf("ah2")
    krot = [ar.bf16(256) for _ in range(2)]
    qrot = [ar.bf16(1024) for _ in range(2)]
    b_krot = [Buf("krot0"), Buf("krot1")]
    b_qrot = [[Buf("qrot%d_%d" % (a, b)) for b in range(2)] for a in range(2)]
    QTz = [[ar.bf16(8 * 128) for _ in range(2)] for _ in range(2)]
    b_QT = [Buf("QT0"), Buf("QT1")]
    for par_ in range(2):
        for hf_ in range(2):
            p.memset("pool", QTz[par_][hf_], 0.0, [b_QT[par_]])
    PT = [ar.bf16(5 * 512) for _ in range(2)]
    b_PT = [[Buf("PT%d_%d" % (s, c)) for c in range(5)] for s in range(2)]
    O = ar.bf16(1024)
    b_O = [Buf("O%d" % i) for i in range(4)]
    OT = ar.bf16(8 * 512)
    OT3 = v3(OT, 8, 512)
    b_OT = Buf("OT")
    den = [ar.f32(4) for _ in range(2)]
    b_den = [Buf("den0"), Buf("den1")]
    b_KT = [Buf("KT%d" % i) for i in range(NS)]
    b_V = [Buf("V%d" % i) for i in range(NS)]
    p.memset("pool", VA4[:, :, :, 64:65], 1.0, b_V)
    rk = ar.f32(NS * 4)
    rk3 = v3(rk, NS, 4)
    b_rk = [Buf("rk%d" % i) for i in range(NS)]
    psQT = g.psum[3][:].bitcast(BF16)
    psQT3 = v3(psQT, 8, 128)
    psOT = g.psum[7][:].bitcast(BF16)
    psOT3 = v3(psOT, 8, 128)
    psKT3 = v3(g.psum[6][:].bitcast(BF16)[:, 768:1024], 2, 128)
    b_psKT = g.b_ps[6]
    b_psPV = g.b_ps[6]

    tiles = [(src_c, dst_c, 0, CTX, 1)] + [(src, dst, t * 512, 512, 0) for t in range(NT)]
    units = []
    for ti, (s_, d_, t0, N, strm) in enumerate(tiles):
        nb = N // 128
        for bl in range(nb):
            units.append(dict(ti=ti, bl=bl, strm=strm, first=(bl == 0), last=(bl == nb - 1), N=N, t0=t0,
                              blk=(t0 // 128 + bl), need_q=(strm == 0 or need_ctx_out),
                              slot=(4 + bl) if strm == 1 else ((t0 // 128 + bl) % 4)))
    for k, u in enumerate(units):
        u["par"] = k % 2

    def load(ti):
        if ti >= len(tiles):
            return
        s_, d_, t0, N, strm = tiles[ti]
        x3 = v3(xs[ti % 2], 8, 512)[:, :, 0:N]
        p.dma("sp", x3, s_.rearrange("(c q) t -> q c t", q=128)[:, :, t0:t0 + N], [], [b_xs[ti % 2]])
        if strm == 0:
            p.dma("sp", v3(cst[ti % 2], 4, 96), g.attn_cs[t0:t0 + 512].rearrange("(b q) d -> q b d", q=128),
                  [], [b_cst[ti % 2]])

    hbuf = [h, h2]
    b_hb = [b_h, b_h2]

    def S0_stats(u):
        ti, strm, N = u["ti"], u["strm"], u["N"]
        x3 = v3(xs[ti % 2], 8, 512)[:, :, 0:N]
        h3 = v3(hbuf[ti % 2], 8, 512)[:, :, 0:N]
        prenorm_stats(g, x3, N, h3, b_xs[ti % 2], b_hb[ti % 2], ps_i=0, rstd=rstd_sb[:, 0:N], b_rstd=b_rstd_sb)

    def S0_apply(u, c):
        ti, strm, N = u["ti"], u["strm"], u["N"]
        x3 = v3(xs[ti % 2], 8, 512)[:, :, 0:N]
        h3 = v3(hbuf[ti % 2], 8, 512)[:, :, 0:N]
        A, B, _ = vecs(g, l, 0, strm)
        prenorm_apply(g, x3, N, A, B, h3, b_xs[ti % 2], b_hb[ti % 2], c, ptmp[c % 2][:, 0:N], b_ptmp[c % 2],
                      rstd=rstd_sb[:, 0:N], b_rstd=b_rstd_sb)

    st20 = ar.f32(20)
    b_st20 = Buf("st20")

    def proj_q(u, hf):
        ti, bl, strm, N, par = u["ti"], u["bl"], u["strm"], u["N"], u["par"]
        hb = v3(hbuf[ti % 2], 8, 512)[:, :, bl * 128:(bl + 1) * 128]
        for c in range(8):
            p.mm(g.psum[1 + hf][:, 0:512], hb[:, c, :], wqkv3[:, c, hf * 512:(hf + 1) * 512],
                 (c == 0), (c == 7), [b_hb[ti % 2], b_wqkv], [g.b_ps[1 + hf]])
        WQ1[hf]["b_st"] = b_st20
        qk_part1(g, g.psum[1 + hf][:, 0:512], 8, st20[:, hf * 8:(hf + 1) * 8], g.b_ps[1 + hf], WQ1[hf])

    def proj_kv(u):
        ti, bl, strm, N, par = u["ti"], u["bl"], u["strm"], u["N"], u["par"]
        hb = v3(hbuf[ti % 2], 8, 512)[:, :, bl * 128:(bl + 1) * 128]
        for c in range(8):
            p.mm(g.psum[0][:, 0:512], hb[:, c, :], wqkv3[:, c, 1024:1536], (c == 0), (c == 7),
                 [b_hb[ti % 2], b_wqkv], [g.b_ps[0]])
        cs = v3(cst[ti % 2], 4, 96)[:, bl, :] if strm == 0 else None
        WK1["b_st"] = b_st20
        qk_part1(g, g.psum[0][:, 0:256], 4, st20[:, 16:20], g.b_ps[0], WK1)
        p.copy("act", VA4[:, u["slot"], :, 0:64], v3(g.psum[0][:, 256:512], 4, 64), [g.b_ps[0]], [b_V[u["slot"]]])
        ksrc = v3(g.psum[0][:, 0:256], 4, 64)
        gkb = gk.unsqueeze(1).broadcast_to([128, 4, 64])
        if cs is None:
            p.tt("dve", v3(krot[par], 4, 64), ksrc, gkb, ALU.mult, [g.b_ps[0], b_gn], [b_krot[par]])
        else:
            kq, kt2 = WK1["qn"][:, 0:256], WK1["t2"][:, 0:256]
            p.tt("dve", v3(kq, 4, 64), ksrc, gkb, ALU.mult, [g.b_ps[0], b_gn], [WK1["b_qn"]])
            q4 = kq.rearrange("p (h two d) -> p h two d", h=4, two=2, d=32)
            t4 = kt2.rearrange("p (h two d) -> p h two d", h=4, two=2, d=32)
            cosb = cs[:, 0:32].unsqueeze(1).broadcast_to([128, 4, 32])
            sinb = cs[:, 32:64].unsqueeze(1).broadcast_to([128, 4, 32])
            nsinb = cs[:, 64:96].unsqueeze(1).broadcast_to([128, 4, 32])
            bcs = b_cst[ti % 2]
            p.tt("pool", t4[:, :, 0, :], q4[:, :, 1, :], nsinb, ALU.mult, [WK1["b_qn"], bcs], [WK1["b_t2"]])
            p.tt("pool", t4[:, :, 1, :], q4[:, :, 0, :], sinb, ALU.mult, [WK1["b_qn"], bcs], [WK1["b_t2"]])
            for two in range(2):
                p.tt("pool", q4[:, :, two, :], q4[:, :, two, :], cosb, ALU.mult, [WK1["b_qn"], bcs, WK1["b_t2"]],
                     [WK1["b_qn"]])
            p.tt("pool", krot[par], kq, kt2, ALU.add, [WK1["b_qn"], WK1["b_t2"]], [b_krot[par]])

    def chain_b(u):
        ti, bl, strm, N, par = u["ti"], u["bl"], u["strm"], u["N"], u["par"]
        cs = v3(cst[ti % 2], 4, 96)[:, bl, :] if strm == 0 else None
        lo = 0 if u["need_q"] else 16
        sl_ = st20[:, lo:20]
        p.act(sl_, sl_, AF.Sqrt, [b_st20, g.b_eps], [b_st20], scale=1.0, bias=g.eps64[:, 0:1])
        p.call("dve", "reciprocal", [b_st20], [b_st20], out=sl_, in_=sl_)
        p.ts("dve", rk3[:, u["slot"], :], st20[:, 16:20], 8.0, None, ALU.mult, ALU.bypass, [b_st20], [b_rk[u["slot"]]])
        if u["need_q"]:
            for hf in range(2):
                qk_part2(g, g.psum[1 + hf][:, 0:512], 8, gq, cs, st20[:, hf * 8:(hf + 1) * 8], b_st20,
                         qrot[par][:, hf * 512:(hf + 1) * 512], g.b_ps[1 + hf], b_gn, b_cst[ti % 2],
                         b_qrot[par][hf], WQ1[hf])

    def S1b_K(u):
        par, sl = u["par"], u["slot"]
        for pr in range(2):
            p.call("pe", "transpose", [b_krot[par], b_ident], [b_psKT], psKT3[:, pr, :],
                   krot[par][:, pr * 128:(pr + 1) * 128], ident)
        p.copy("dve", KT3[:, :, sl * 128:(sl + 1) * 128], psKT3, [b_psKT], [b_KT[sl]])

    def S1b_Q(u):
        par = u["par"]
        if u["need_q"]:
            for s in range(8):
                p.call("pe", "transpose", [b_qrot[par][s // 4], b_ident], [g.b_ps[7]], psOT3[:, s, :],
                       qrot[par][:, s * 128:(s + 1) * 128], ident)
            p.copy("dve", QTz[par][0][0:64, :], psOT[0:64, :], [g.b_ps[7]], [b_QT[par]])
            p.copy("dve", QTz[par][1][64:128, :], psOT[64:128, :], [g.b_ps[7]], [b_QT[par]])

    SCB = (4, 5, 3)
    pend_ot = []

    def o_transposes(bl):
        for c in range(8):
            p.call("pe", "transpose", [b_O[c // 2], b_ident], [g.b_ps[7]], psOT3[:, c, :],
                   O[:, c * 128:(c + 1) * 128], ident)
        p.copy("dve", OT3[:, :, bl * 128:(bl + 1) * 128], psOT3, [g.b_ps[7]], [b_OT])


    def S2(u, fillers):
        ti, bl, strm, N, par, blk = u["ti"], u["bl"], u["strm"], u["N"], u["par"], u["blk"]
        chunks = [(4, None), (5, None)]
        if strm == 0:
            if blk > 0:
                chunks.append(((blk - 1) % 4, 0))
            chunks.append((blk % 4, None))
            if blk < NB - 1:
                chunks.append(((blk + 1) % 4, 1))
        for gi in range(4):
            pr, half = gi // 2, gi % 2
            QT3 = v3(QTz[par][half], 8, 128)
            pts = gi % 2
            PT3 = v3(PT[pts], 5, 512)
            for ci, (sl, mk) in enumerate(chunks):
                pss = SCB[(gi * 5 + ci) % 3]
                p.mm(g.psum[pss][:, 0:512], KT3[:, pr, sl * 128:(sl + 1) * 128],
                     QT3[:, pr * 4:(pr + 1) * 4, :], True, True, [b_KT[sl], b_QT[par]], [g.b_ps[pss]])
                p.act(PT3[:, ci, :], g.psum[pss][:, 0:512], AF.Exp, [g.b_ps[pss], b_rk[sl]], [b_PT[pts][ci]],
                      scale=rk3[:, sl, gi:gi + 1])
                if mk is not None:
                    p.tt("dve", PT3[:, ci, :], PT3[:, ci, :], masks[:, mk * 512:(mk + 1) * 512], ALU.mult,
                         [b_PT[pts][ci], b_masks], [b_PT[pts][ci]])
            for f in fillers[gi]:
                f()
            po = g.psum[6][:, 0:260].rearrange("p (h d) -> p h d", h=4, d=65)
            for hl in range(4):
                for ci, (sl, mk) in enumerate(chunks):
                    p.mm(po[:, hl, :], PT3[:, ci, hl * 128:(hl + 1) * 128], VA4[:, sl, gi, :],
                         (ci == 0), (ci == len(chunks) - 1), [b_PT[pts][ci], b_V[sl]], [b_psPV])
            dn = den[gi % 2]
            p.tt("dve", dn, po[:, :, 64], esink[:, gi * 4:(gi + 1) * 4], ALU.add, [b_psPV, b_esink], [b_den[gi % 2]])
            p.call("dve", "reciprocal", [b_den[gi % 2]], [b_den[gi % 2]], out=dn, in_=dn)
            p.tt("dve", v3(O, 16, 64)[:, gi * 4:(gi + 1) * 4, :], po[:, :, 0:64],
                 dn.unsqueeze(2).broadcast_to([128, 4, 64]), ALU.mult, [b_psPV, b_den[gi % 2]], [b_O[gi]])
        if not u["last"]:
            pend_ot.append(bl)
        else:
            o_transposes(bl)
        if u["last"]:
            s_, d_, t0, N, strm = tiles[ti]
            x3 = v3(xs[ti % 2], 8, 512)[:, :, 0:N]
            b_x = b_xs[ti % 2]
            _, _, G = vecs(g, l, 0, strm)
            for c in range(8):
                py = 1 + (c % 2)
                for k in range(8):
                    p.mm(g.psum[py][:, 0:N], wo3[:, k, c * 128:(c + 1) * 128], OT3[:, k, 0:N],
                         (k == 0), (k == 7), [b_wo, b_OT], [g.b_ps[py]])
                p.stt(x3[:, c, :], g.psum[py][:, 0:N], G[:, c:c + 1], x3[:, c, :], ALU.mult, ALU.add,
                      [g.b_ps[py], b_x, g.b_ADA], [b_x])
            ob = (g.OUT if d_ is g.y else g.ST)[ti % 2]
            p.dma("sp", d_.rearrange("(c q) t -> q c t", q=128)[:, :, t0:t0 + N], x3, [b_x], [ob])

    load(0)
    load(1)
    n = len(units)
    S0_stats(units[0])
    for c in range(8):
        S0_apply(units[0], c)
    for idx in range(n + 2):
        A = units[idx] if idx < n else None
        Bu = units[idx - 1] if 0 <= idx - 1 < n else None
        C = units[idx - 2] if 0 <= idx - 2 < n else None
        nxt = units[idx + 1] if idx + 1 < n else None
        nx2 = units[idx + 2] if idx + 2 < n else None
        fl = [[], [], [], []]
        if Bu is not None:
            S1b_K(Bu)
        while pend_ot:
            o_transposes(pend_ot.pop(0))
        if A is not None:
            if A["need_q"]:
                fl[0].append(lambda A=A: proj_q(A, 0))
                fl[1].append(lambda A=A: proj_q(A, 1))
            fl[2].append(lambda A=A: proj_kv(A))
            fl[3].append(lambda A=A: chain_b(A))
        if Bu is not None:
            fl[1].append(lambda Bu=Bu: S1b_Q(Bu))
        if nxt is not None and nxt["first"]:
            for k in range(4):
                fl[k].append(lambda nxt=nxt, k=k: (S0_apply(nxt, 2 * k), S0_apply(nxt, 2 * k + 1)))
        if C is not None and C["need_q"]:
            S2(C, fl)
        else:
            for fs in fl:
                for f in fs:
                    f()
        if nx2 is not None and nx2["first"]:
            S0_stats(nx2)
        if C is not None and C["last"]:
            load(C["ti"] + 2)


def host_attn_consts(T):
    rows = T // 64
    row = np.repeat(np.arange(rows, dtype=np.float32), 64)
    col = np.tile(np.arange(64, dtype=np.float32), rows)
    inv = (10000.0 ** (-np.arange(16, dtype=np.float32) / 16)).astype(np.float32)
    ang = np.concatenate([row[:, None] * inv, col[:, None] * inv], axis=-1).astype(np.float32)
    cs = np.concatenate([np.cos(ang), np.sin(ang), -np.sin(ang)], axis=1).astype(np.float32)
    j = np.arange(128)[:, None]
    r = np.arange(128)[None, :]
    m_prev = (j >= r).astype(np.float32)
    m_next = (j <= r).astype(np.float32)
    masks = np.concatenate([np.tile(m_prev, (1, 4)), np.tile(m_next, (1, 4))], axis=1)
    return dict(attn_cs=cs, attn_masks=np.ascontiguousarray(masks), ident=np.eye(128, dtype=np.float32))


def host_attn_weights(w_qkv, w_o, q_norm, k_norm, sink):
    ns = w_qkv.shape[0]
    order = []
    for pr in range(2):
        for i in range(4):
            for half in range(2):
                hd = (2 * pr + half) * 4 + i
                order.extend(range(hd * 64, hd * 64 + 64))
    order = np.array(order)
    wq = w_qkv[:, :, :1024][:, :, order]
    w2 = np.ascontiguousarray(np.concatenate([wq, w_qkv[:, :, 1024:]], axis=2))
    return dict(attn_wqkv=w2, attn_wo=np.ascontiguousarray(w_o),
                attn_qn=np.ascontiguousarray(np.broadcast_to(q_norm[:, None, :], (ns, 128, 64))),
                attn_kn=np.ascontiguousarray(np.broadcast_to(k_norm[:, None, :], (ns, 128, 64))),
                attn_qns=np.ascontiguousarray(np.broadcast_to(np.roll(q_norm, 32, axis=1)[:, None, :], (ns, 128, 64))),
                attn_kns=np.ascontiguousarray(np.broadcast_to(np.roll(k_norm, 32, axis=1)[:, None, :], (ns, 128, 64))),
                attn_sinkB=np.ascontiguousarray(np.broadcast_to(sink[:, None, :], (ns, 128, 16))))


def complete_inputs(nc, im):
    out = dict(im)
    for k, shp in nc._decl_inputs.items():
        if k not in out:
            out[k] = np.zeros(shp, np.float32)
        assert tuple(out[k].shape) == tuple(shp), (k, out[k].shape, shp)
    return out


RET_LG = [[float(np.log1p(-2.0 ** (-5.0 - h))) for h in range(4)],
          [float(np.log1p(-2.0 ** (-5.5 - h))) for h in range(4)]]


def phase_ret_sweep(g, l, slot, dirn, src, src_c):
    nc, p = g.nc, g.p
    T, NT = g.T, g.NT
    ar = Arena(g)
    wqk = ar.bf16(8 * 2048)
    wv = ar.bf16(8 * 2048)
    wg = ar.bf16(8 * 2048)
    wqk3, wv3, wg3 = v3(wqk, 8, 2048), v3(wv, 8, 2048), v3(wg, 8, 2048)
    b_wqk, b_wv, b_wg = Buf("wqk"), Buf("wv"), Buf("wg")
    win = g.ret_win[slot].rearrange("(c q) n -> q c n", q=128)
    load_w_cast(g, wqk3, win[:, :, 0:2048], b_wqk, 2048, 8)
    if dirn == 1:
        load_w_cast(g, wv3, win[:, :, 2048:4096], b_wv, 2048, 8)
    gc0 = 4096 + 2048 * dirn
    load_w_cast(g, wg3, win[:, :, gc0:gc0 + 2048], b_wg, 2048, 8)
    xs = ar.f32(8 * 512)
    b_x = Buf("rx")
    b_scr = [Buf("scr0"), Buf("scr1")]
    h = ar.bf16(8 * 512)
    b_h = Buf("rh")
    Tst = ar.f32(4 * 2 * 512)
    Tst4 = Tst.rearrange("p (h c v) -> p h c v", h=4, c=2, v=512)
    Sbf = ar.bf16(4 * 2 * 512)
    Sbf4 = Sbf.rearrange("p (h c v) -> p h c v", h=4, c=2, v=512)
    b_T = [Buf("T%d" % i) for i in range(4)]
    b_S = [Buf("S%d" % i) for i in range(4)]
    qT, kT = ar.bf16(8 * 512), ar.bf16(8 * 512)
    qT3, kT3 = v3(qT, 8, 512), v3(kT, 8, 512)
    b_qT = [Buf("qT%d" % i) for i in range(4)]
    b_kT = [Buf("kT%d" % i) for i in range(4)]
    cs = ar.f32(2 * 512)
    b_cs = Buf("rcs")
    dec = ar.f32(8 * 128)
    b_dec = Buf("dec")
    p.dma("sp", dec, g.ret_dec[dirn], [], [b_dec])
    dec3 = v3(dec, 8, 128)
    mask = ar.bf16(128)
    ident = ar.bf16(128)
    b_mask = b_ident = Buf("rconst")
    p.dma("pool", mask, g.ret_masks[dirn], [], [b_mask])
    p.dma("pool", ident, g.ident, [], [b_ident])
    Ktok = ar.bf16(1024)
    Ktok3 = v3(Ktok, 8, 128)
    b_Ktok = Buf("Ktok")
    Vtok = ar.bf16(2048)
    b_V = [Buf("Vt%d" % i) for i in range(4)]
    b_VST = [Buf("VST%d" % i) for i in range(4)]
    sg4 = [ar.f32(512) for _ in range(4)]
    b_sg4 = [Buf("rsg%d" % i) for i in range(4)]
    tmp = ar.f32(512)
    b_tmp = Buf("rtmp")
    PT4 = [ar.bf16(128) for _ in range(4)]
    b_PT4 = [Buf("rPT%d" % i) for i in range(4)]
    b_Kt4 = [Buf("Kt%d" % i) for i in range(4)]
    z = ar.bf16(2048)
    b_z = [Buf("z%d" % i) for i in range(4)]
    zT = ar.bf16(16 * 128)
    zT3 = v3(zT, 16, 128)
    b_zT = Buf("zT")
    zbT = ar.bf16(16 * 128)
    zbT3 = v3(zbT, 16, 128)
    b_zbT = Buf("zbT")
    ss = ar.f32(4)
    b_ss = [Buf("ss%d" % i) for i in range(4)]
    for hh in range(4):
        p.memset("pool", Tst4[:, hh], 0.0, [b_T[hh]])
        p.memset("pool", Sbf4[:, hh], 0.0, [b_S[hh]])
    psT = [g.psum[3][:].bitcast(BF16), g.psum[4][:].bitcast(BF16)]
    psT3 = [v3(psT[0], 8, 128), v3(psT[1], 8, 128)]

    lat = [(src, g.ZT, t * 512, 512, 0) for t in range(NT)]
    if dirn == 1:
        lat = lat[::-1]
    tiles = [(src_c, g.ZTC, 0, CTX, 1)] + lat
    cvals = [float(np.exp(128.0 * RET_LG[dirn][hh])) for hh in range(4)]

    def load(i):
        s_, d_, t0, N, strm = tiles[i]
        x3 = v3(xs, 8, 512)[:, :, 0:N]
        p.dma("sp", x3, s_.rearrange("(c q) t -> q c t", q=128)[:, :, t0:t0 + N], [], [b_x])

    load(0)
    for i, (s_, d_, t0, N, strm) in enumerate(tiles):
        x3 = v3(xs, 8, 512)[:, :, 0:N]
        h3 = v3(h, 8, 512)[:, :, 0:N]
        A, B, _ = vecs(g, l, 0, strm)
        if i == 0:
            prenorm(g, x3, N, A, B, h3, b_x, b_h)
        if strm == 0:
            p.dma("sp", v3(cs, 2, 512), g.ret_cs[:, :, t0:t0 + 512].rearrange("s q t -> q s t"), [], [b_cs])
        nb = N // 128
        scr = [xs[:, k * 512:(k + 1) * 512] for k in range(8)]
        for qk in range(2):
            for hh in range(4):
                pb = (1, 2) if ((qk * 4 + hh) % 2 == 0) else (5, 6)
                for dc in range(2):
                    col = qk * 1024 + hh * 256 + dc * 128
                    for c in range(8):
                        p.mm(g.psum[pb[dc]][:, 0:N], wqk3[:, c, col:col + 128], h3[:, c, :],
                             (c == 0), (c == 7), [b_wqk, b_h], [g.b_ps[pb[dc]]])
                x1, x2 = g.psum[pb[0]][:, 0:N], g.psum[pb[1]][:, 0:N]
                bx1, bx2 = g.b_ps[pb[0]], g.b_ps[pb[1]]
                dst3 = (qT3 if qk == 0 else kT3)
                b_dst = (b_qT if qk == 0 else b_kT)[hh]
                dcb = dec3[:, qk * 4 + hh, :].unsqueeze(1).broadcast_to([128, nb, 128])
                if strm == 1:
                    for dc, xx, bxx in ((0, x1, bx1), (1, x2, bx2)):
                        p.tt("dve", dst3[:, hh * 2 + dc, 0:N].rearrange("p (b t) -> p b t", b=nb, t=128),
                             xx.rearrange("p (b t) -> p b t", b=nb, t=128), dcb, ALU.mult,
                             [bxx, b_dec], [b_dst])
                    continue
                sset = (qk * 4 + hh) % 2
                so = sset * 4
                bs = b_scr[sset]
                ta, tb, tc, td = (scr[so + k][:, 0:N] for k in range(4))
                cosv, sinv = cs[:, 0:N], cs[:, 512:512 + N]
                p.tt("dve", ta, x1, cosv, ALU.mult, [bx1, b_cs, b_x], [bs])
                p.tt("dve", tb, x2, sinv, ALU.mult, [bx2, b_cs, b_x], [bs])
                p.tt("dve", tc, x2, cosv, ALU.mult, [bx2, b_cs, b_x], [bs])
                p.tt("dve", td, x1, sinv, ALU.mult, [bx1, b_cs, b_x], [bs])
                p.tt("pool", ta, ta, tb, ALU.subtract, [bs, b_x], [bs])
                p.tt("dve", tc, tc, td, ALU.add, [bs, b_x], [bs])
                for dc, rr in ((0, ta), (1, tc)):
                    p.tt("pool", dst3[:, hh * 2 + dc, 0:N].rearrange("p (b t) -> p b t", b=nb, t=128),
                         rr.rearrange("p (b t) -> p b t", b=nb, t=128), dcb, ALU.mult, [bs, b_x, b_dec], [b_dst])
        if i + 1 < len(tiles):
            load(i + 1)
        order = list(range(nb))
        if dirn == 1:
            order = order[::-1]

        def P1(bl, hh):
            cols = slice(bl * 128, (bl + 1) * 128)
            vc = slice(hh * 512, (hh + 1) * 512)
            vsd = (g.VSC if strm == 1 else g.VS)[t0 + bl * 128:t0 + (bl + 1) * 128, vc]
            if dirn == 1:
                for c in range(8):
                    p.mm(g.psum[1][:, 0:512], h3[:, c, cols], wv3[:, c, vc], (c == 0), (c == 7),
                         [b_h, b_wv], [g.b_ps[1]])
                p.copy("act", Vtok[:, vc], g.psum[1][:, 0:512], [g.b_ps[1]], [b_V[hh]])
                p.dma("sp", vsd, Vtok[:, vc], [b_V[hh]], [b_VST[hh]])
            else:
                p.dma("sp", Vtok[:, vc], vsd, [], [b_V[hh]])
            for dc in range(2):
                p.call("pe", "transpose", [b_kT[hh], b_ident], [g.b_ps[3]], psT3[0][:, hh * 2 + dc, :],
                       kT3[:, hh * 2 + dc, cols], ident)
            p.copy("dve", Ktok3[:, hh * 2:hh * 2 + 2, :], psT3[0][:, hh * 2:hh * 2 + 2, :], [g.b_ps[3]], [b_Kt4[hh]])
            sc_ps = g.psum[4][:, hh * 128:(hh + 1) * 128]
            for dc in range(2):
                p.mm(sc_ps, kT3[:, hh * 2 + dc, cols], qT3[:, hh * 2 + dc, cols], (dc == 0), (dc == 1),
                     [b_kT[hh], b_qT[hh]], [g.b_ps[4]])
            p.tt("dve", PT4[hh], sc_ps, mask, ALU.mult, [g.b_ps[4], b_mask], [b_PT4[hh]])

        def P1g(bl, hh):
            cols = slice(bl * 128, (bl + 1) * 128)
            vc = slice(hh * 512, (hh + 1) * 512)
            for c in range(8):
                p.mm(g.psum[2][:, 0:512], h3[:, c, cols], wg3[:, c, vc], (c == 0), (c == 7),
                     [b_h, b_wg], [g.b_ps[2]])
            p.act(sg4[hh], g.psum[2][:, 0:512], AF.Silu, [g.b_ps[2]], [b_sg4[hh]])

        def P2(bl, hh):
            cols = slice(bl * 128, (bl + 1) * 128)
            vc = slice(hh * 512, (hh + 1) * 512)
            yb = (5, 0)[hh % 2]
            yps = g.psum[yb][:, 0:512]
            p.mm(yps, PT4[hh], Vtok[:, vc], True, False, [b_PT4[hh], b_V[hh]], [g.b_ps[yb]])
            for dc in range(2):
                p.mm(yps, qT3[:, hh * 2 + dc, cols], Sbf4[:, hh, dc, :], False, (dc == 1),
                     [b_qT[hh], b_S[hh]], [g.b_ps[yb]])
            for dc in range(2):
                dps = g.psum[6 + dc][:, 0:512]
                p.mm(dps, Ktok3[:, hh * 2 + dc, :], Vtok[:, vc], True, True, [b_Kt4[hh], b_V[hh]], [g.b_ps[6 + dc]])
                p.stt(Tst4[:, hh, dc, :], Tst4[:, hh, dc, :], cvals[hh], dps, ALU.mult, ALU.add,
                      [g.b_ps[6 + dc], b_T[hh]], [b_T[hh]])
                p.ts("pool", Sbf4[:, hh, dc, :], Tst4[:, hh, dc, :], cvals[hh], 1.0, ALU.mult, ALU.mult,
                     [b_T[hh]], [b_S[hh]])
            p.act(tmp, yps, AF.Square, [g.b_ps[yb]], [b_tmp])
            p.call("dve", "tensor_reduce", [b_tmp], [b_ss[hh]], out=ss[:, hh:hh + 1], in_=tmp, op=ALU.add, axis=AX.X)

        def P2b(bl, hh):
            vc = slice(hh * 512, (hh + 1) * 512)
            yb = (5, 0)[hh % 2]
            yps = g.psum[yb][:, 0:512]
            p.act(ss[:, hh:hh + 1], ss[:, hh:hh + 1], AF.Sqrt, [b_ss[hh], g.b_eps], [b_ss[hh]],
                  scale=1.0 / 512, bias=g.eps_t[:, 0:1])
            p.call("dve", "reciprocal", [b_ss[hh]], [b_ss[hh]], out=ss[:, hh:hh + 1], in_=ss[:, hh:hh + 1])
            p.stt(z[:, vc], yps, ss[:, hh:hh + 1], sg4[hh], ALU.mult, ALU.mult,
                  [g.b_ps[yb], b_ss[hh], b_sg4[hh]], [b_z[hh]])

        for hh in range(4):
            P1(order[0], hh)
            P1g(order[0], hh)
        for k, bl in enumerate(order):
            nxt = order[k + 1] if k + 1 < len(order) else None
            tok0 = t0 + bl * 128
            if dirn == 0:
                p.dma("sp", zbT3, d_.rearrange("(k q) t -> q k t", q=128)[:, :, tok0:tok0 + 128], [], [b_zbT])
            early = (nxt is None and i + 1 < len(tiles))
            if early:
                s2_, d2_, t02, N2, strm2 = tiles[i + 1]
                nx3 = v3(xs, 8, 512)[:, :, 0:N2]
                nh3 = v3(h, 8, 512)[:, :, 0:N2]
                A2, B2, _ = vecs(g, l, 0, strm2)
                prenorm_stats(g, nx3, N2, nh3, b_x, b_h, ps_i=1)
            for hh in range(4):
                P2(bl, hh)
                if nxt is not None:
                    P1(nxt, hh)
                P2b(bl, hh)
                if nxt is not None:
                    P1g(nxt, hh)
                if early:
                    for c2 in (2 * hh, 2 * hh + 1):
                        prenorm_apply(g, nx3, N2, A2, B2, nh3, b_x, b_h, c2, g.psum[2][:, 0:N2], g.b_ps[2], ps_i=1)
            for k2 in range(16):
                p.call("pe", "transpose", [b_z[k2 // 4], b_ident], [g.b_ps[3 + k2 // 8]], psT3[k2 // 8][:, k2 % 8, :],
                       z[:, k2 * 128:(k2 + 1) * 128], ident)
            for hf in range(2):
                if dirn == 1:
                    p.copy("dve", zT3[:, hf * 8:(hf + 1) * 8, :], psT3[hf], [g.b_ps[3 + hf]], [b_zT])
                else:
                    p.tt("dve", zT3[:, hf * 8:(hf + 1) * 8, :], psT3[hf], zbT3[:, hf * 8:(hf + 1) * 8, :], ALU.add,
                         [g.b_ps[3 + hf], b_zbT], [b_zT])
            p.dma("sp", d_.rearrange("(k q) t -> q k t", q=128)[:, :, tok0:tok0 + 128], zT3, [b_zT],
                  [g.ST[2]])


def phase_ret_out(g, l, slot, src, dst, src_c, dst_c, need_ctx_out=True):
    nc, p = g.nc, g.p
    ar = Arena(g)
    wo = ar.bf16(16 * 1024)
    wo3 = v3(wo, 16, 1024)
    b_wo = Buf("rwo")
    load_w_cast(g, wo3, g.ret_wo[slot].rearrange("(k q) n -> q k n", q=128), b_wo, 1024, 16)
    xs = [ar.f32(8 * 512) for _ in range(2)]
    zs = [ar.bf16(16 * 512) for _ in range(2)]
    b_xs = [Buf("ox0"), Buf("ox1")]
    b_zs = [Buf("oz0"), Buf("oz1")]
    tiles = [(src, dst, g.ZT, t * 512, 512, 0) for t in range(g.NT)]
    if need_ctx_out:
        tiles.append((src_c, dst_c, g.ZTC, 0, CTX, 1))

    def load(i):
        s_, d_, z_, t0, N, strm = tiles[i]
        p.dma("sp", v3(xs[i % 2], 8, 512)[:, :, 0:N], s_.rearrange("(c q) t -> q c t", q=128)[:, :, t0:t0 + N],
              [], [b_xs[i % 2]])
        p.dma("sp", v3(zs[i % 2], 16, 512)[:, :, 0:N], z_.rearrange("(k q) t -> q k t", q=128)[:, :, t0:t0 + N],
              [], [b_zs[i % 2]])

    load(0)
    for i, (s_, d_, z_, t0, N, strm) in enumerate(tiles):
        if i + 1 < len(tiles):
            load(i + 1)
        x3 = v3(xs[i % 2], 8, 512)[:, :, 0:N]
        z3 = v3(zs[i % 2], 16, 512)[:, :, 0:N]
        _, _, G = vecs(g, l, 0, strm)
        for c in range(8):
            py = 1 + (c % 2)
            for k in range(16):
                p.mm(g.psum[py][:, 0:N], wo3[:, k, c * 128:(c + 1) * 128], z3[:, k, :], (k == 0), (k == 15),
                     [b_wo, b_zs[i % 2]], [g.b_ps[py]])
            p.stt(x3[:, c, :], g.psum[py][:, 0:N], G[:, c:c + 1], x3[:, c, :], ALU.mult, ALU.add,
                  [g.b_ps[py], b_xs[i % 2], g.b_ADA], [b_xs[i % 2]])
        ob = (g.OUT if d_ is g.y else g.ST)[i % 2]
        p.dma("sp", d_.rearrange("(c q) t -> q c t", q=128)[:, :, t0:t0 + N], x3, [b_xs[i % 2]], [ob])


def host_ret_consts(T):
    inv = (10000.0 ** (-np.linspace(0.0, 1.0, 128, dtype=np.float32))).astype(np.float32)
    ang = (np.arange(T, dtype=np.float32)[:, None] * inv).astype(np.float32)
    cs = np.stack([np.cos(ang).T, np.sin(ang).T]).astype(np.float32)
    dec = np.zeros((2, 8, 128), np.float64)
    pos = np.arange(128, dtype=np.float64)
    for hh in range(4):
        lf, lb = RET_LG[0][hh], RET_LG[1][hh]
        dec[0, hh] = np.exp((pos + 1.0) * lf)
        dec[0, 4 + hh] = np.exp(-(pos + 1.0) * lf) / 16.0
        dec[1, hh] = np.exp((128.0 - pos) * lb)
        dec[1, 4 + hh] = np.exp(-(128.0 - pos) * lb) / 16.0
    decB = np.ascontiguousarray(np.broadcast_to(dec.reshape(2, 1, 1024), (2, 128, 1024))).astype(np.float32)
    j = np.arange(128)[:, None]
    i = np.arange(128)[None, :]
    masks = np.stack([(i >= j), (i <= j)]).astype(np.float32)
    return dict(ret_cs=np.ascontiguousarray(cs), ret_dec=decB, ret_masks=masks)


SEQ = 8192
BATCH = 8


def full_phases():
    def L0a(g):
        phase_attn(g, 0, 0, g.x_in, g.X1, g.ctx_in, g.XC, True)

    def L0f(g):
        phase_ffn(g, 0, g.X1, g.X1, g.XC, g.XC, True)

    def L1p(g):
        phase_pool(g, 1, 0, g.X1, g.X2, g.XC, g.XC2)

    def L1f(g):
        phase_ffn(g, 1, g.X2, g.X2, g.XC2, g.XC2, True)

    def L2b(g):
        phase_ret_sweep(g, 2, 0, 1, g.X2, g.XC2)

    def L2fw(g):
        phase_ret_sweep(g, 2, 0, 0, g.X2, g.XC2)

    def L2o(g):
        phase_ret_out(g, 2, 0, g.X2, g.X2, g.XC2, g.XC2, True)

    def L2f(g):
        phase_ffn(g, 2, g.X2, g.X2, g.XC2, g.XC2, True)

    def L3a(g):
        phase_attn(g, 3, 1, g.X2, g.X2, g.XC2, None, False)

    def L3f(g):
        phase_ffn(g, 3, g.X2, g.y, None, None, False)

    return [phase_ada, L0a, L0f, L1p, L1f, L2b, L2fw, L2o, L2f, L3a, L3f]


def host_inputs(T, x_b, c_b, ctx_b, c_ctx, ada_w, ada_b, norm_mix, norm_ffn, attn_w_qkv, attn_w_o, attn_q_norm,
                attn_k_norm, attn_sink, pool_w, pool_scale, ret_w_in, ret_w_o, ffn_w_gate, ffn_w_up, ffn_w_down,
                shared=None):
    if shared is None:
        shared = {}
        shared.update(host_attn_consts(T))
        shared.update(host_attn_weights(attn_w_qkv, attn_w_o, attn_q_norm, attn_k_norm, attn_sink))
        shared.update(host_ret_consts(T))
        shared["pool_rc"] = host_pool_rc(T)
        shared["pool_w"] = np.ascontiguousarray(pool_w)
        shared["pool_scT"] = np.ascontiguousarray(pool_scale.reshape(-1, 8, 128).transpose(0, 2, 1))
        shared["ret_w_in"] = np.ascontiguousarray(ret_w_in)
        shared["ret_w_o"] = np.ascontiguousarray(ret_w_o)
        shared["ada_w"] = np.ascontiguousarray(ada_w)
        shared["ffn_w_gate"] = np.ascontiguousarray(ffn_w_gate)
        shared["ffn_w_up"] = np.ascontiguousarray(ffn_w_up)
        shared["ffn_w_down"] = np.ascontiguousarray(ffn_w_down)
    im = dict(shared)
    im["xT"] = np.ascontiguousarray(x_b.T)
    im["ctxT"] = np.ascontiguousarray(ctx_b.T)
    im.update(host_small(c_b, c_ctx, ada_b, norm_mix, norm_ffn))
    return im, shared


_NC_CACHE = {}


def kernel(x, c, ctx, c_ctx, ada_w, ada_b, norm_mix, norm_ffn, attn_w_qkv, attn_w_o, attn_q_norm,
           attn_k_norm, attn_sink, pool_w, pool_scale, ret_w_in, ret_w_o, ffn_w_gate, ffn_w_up, ffn_w_down):
    f = lambda a: np.asarray(a, dtype=np.float32)
    x, c, ctx, c_ctx = f(x), f(c), f(ctx), f(c_ctx)
    args = [f(a) for a in (ada_w, ada_b, norm_mix, norm_ffn, attn_w_qkv, attn_w_o, attn_q_norm, attn_k_norm,
                           attn_sink, pool_w, pool_scale, ret_w_in, ret_w_o, ffn_w_gate, ffn_w_up, ffn_w_down)]
    B, T, _ = x.shape
    if T not in _NC_CACHE:
        _NC_CACHE[T] = build(T, full_phases())
    nc = _NC_CACHE[T]
    in_maps = []
    shared = None
    for b in range(B):
        im, shared = host_inputs(T, x[b], c[b], ctx[b], c_ctx, *args, shared=shared)
        in_maps.append(im)
    res = run_bass_kernel_spmd(nc, in_maps, core_ids=list(range(B)))
    out = np.empty((B, T, D), np.float32)
    for b in range(B):
        out[b] = np.asarray(res.results[b]["yT"]).T
    return out
```

```python
import contextlib
import numpy as np
import concourse.bass as bass
import concourse.mybir as mybir
from concourse.bass_utils import run_bass_kernel_spmd

F32 = mybir.dt.float32
BF16 = mybir.dt.bfloat16
AF = mybir.ActivationFunctionType
ALU = mybir.AluOpType
AX = mybir.AxisListType

D = 1024
NC8 = 8
FH = 2816
NJ = 22
CTX = 256
EPS = 1e-6
USE_LN = False
USE_POW = False
DBG = {}


class Buf:
    __slots__ = ("name", "writers", "readers", "dsem", "dcount")

    def __init__(self, name):
        self.name = name
        self.writers = {}
        self.readers = {}
        self.dsem = None
        self.dcount = 0


class Op:
    __slots__ = ("eng", "fn", "deps", "needed", "value", "is_dma", "dbuf", "multi")

    def __init__(self, eng, fn, is_dma=False, dbuf=None, multi=False):
        self.eng = eng
        self.fn = fn
        self.deps = []
        self.needed = False
        self.value = None
        self.is_dma = is_dma
        self.dbuf = dbuf
        self.multi = multi


ENGS = ("pe", "act", "dve", "pool", "sp")


class Prog:
    def __init__(self, nc):
        self.nc = nc
        self.ops = {e: [] for e in ENGS}
        self.all_ops = []
        self.dma_bufs = []
        self.barrier_deps = None
        self.barrier_pending = set()

    def _key(self, op):
        return ("d", id(op.dbuf)) if op.is_dma else op.eng

    def _record(self, op, reads, writes):
        deps = op.deps
        if self.barrier_deps is not None and op.eng in self.barrier_pending:
            deps.extend(self.barrier_deps)
            self.barrier_pending.discard(op.eng)
        for b in reads:
            for w in b.writers.values():
                deps.append(w)
        for b in writes:
            same_dma = op.is_dma and b.writers and all(
                w.is_dma and w.dbuf is op.dbuf for w in b.writers.values()) and not b.readers
            if not same_dma:
                for w in b.writers.values():
                    deps.append(w)
                for r in b.readers.values():
                    deps.append(r)
        k = self._key(op)
        for b in reads:
            b.readers[k] = op
        for b in writes:
            same_dma = op.is_dma and b.writers and all(
                w.is_dma and w.dbuf is op.dbuf for w in b.writers.values()) and not b.readers
            if not same_dma:
                b.writers = {}
                b.readers = {}
            b.writers[k] = op
        self.ops[op.eng].append(op)
        self.all_ops.append(op)
        return op

    def op(self, eng, fn, reads=(), writes=(), multi=False):
        return self._record(Op(eng, fn, multi=multi), reads, writes)

    def call(self, eng, method, reads, writes, *args, **kw):
        multi = kw.pop("_multi", False)

        def fn(e, method=method, args=args, kw=kw):
            return getattr(e, method)(*args, **kw)

        return self._record(Op(eng, fn, multi=multi), reads, writes)

    def mm(self, out, lhsT, rhs, start, stop, reads, writes):
        return self.call("pe", "matmul", reads, writes, out, lhsT, rhs, start=start, stop=stop)

    def act(self, out, in_, func, reads, writes, **kw):
        return self.call("act", "activation", reads, writes, out=out, in_=in_, func=func, **kw)

    def tt(self, eng, out, in0, in1, op, reads, writes):
        return self.call(eng, "tensor_tensor", reads, writes, out=out, in0=in0, in1=in1, op=op)

    def stt(self, out, in0, scalar, in1, op0, op1, reads, writes):
        return self.call("dve", "scalar_tensor_tensor", reads, writes, out=out, in0=in0, scalar=scalar,
                         in1=in1, op0=op0, op1=op1)

    def ts(self, eng, out, in0, s1, s2, op0, op1, reads, writes):
        return self.call(eng, "tensor_scalar", reads, writes, out=out, in0=in0, scalar1=s1, scalar2=s2,
                         op0=op0, op1=op1)

    def copy(self, eng, out, in_, reads, writes):
        if eng == "act":
            return self.call("act", "activation", reads, writes, out=out, in_=in_, func=AF.Copy)
        return self.call(eng, "tensor_copy", reads, writes, out=out, in_=in_)

    def memset(self, eng, ap, val, writes):
        return self.call(eng, "memset", [], writes, ap, val)

    def dma(self, eng, out_ap, in_ap, reads, writes, dbuf=None):
        dbuf = dbuf or writes[0]
        if dbuf.dsem is None:
            dbuf.dsem = True
            self.dma_bufs.append(dbuf)

        def fn(e, out_ap=out_ap, in_ap=in_ap):
            return e.dma_start(out=out_ap, in_=in_ap)

        return self._record(Op(eng, fn, is_dma=True, dbuf=dbuf), reads, writes)

    def barrier(self):
        deps = []
        for e in ENGS:
            if self.ops[e]:
                last = [o for o in self.ops[e] if not o.is_dma]
                if last:
                    deps.append(last[-1])
        for b in self.dma_bufs:
            pass
        for o in self.all_ops[::-1]:
            if o.is_dma and not any((d.is_dma and d.dbuf is o.dbuf) for d in deps):
                deps.append(o)
        self.barrier_deps = deps
        self.barrier_pending = set(ENGS)

    def emit(self, final_bufs):
        nc = self.nc
        for o in self.all_ops:
            for d in o.deps:
                d.needed = True
        cnt = {e: 0 for e in ENGS}
        for o in self.all_ops:
            if o.is_dma:
                o.dbuf.dcount += 16
                o.value = o.dbuf.dcount
            elif o.needed:
                cnt[o.eng] += 1
                o.value = cnt[o.eng]
        for b in self.dma_bufs:
            b.dcount = 0
        with contextlib.ExitStack() as st:
            esem = {e: st.enter_context(nc.semaphore("S_" + e)) for e in ENGS}
            for i, b in enumerate(self.dma_bufs):
                b.dsem = st.enter_context(nc.semaphore("D%d" % i))
            assert len(self.dma_bufs) < 90, len(self.dma_bufs)
            block = st.enter_context(nc.Block())

            def resolve(op, waited):
                best = {}
                for d in op.deps:
                    if d.is_dma:
                        key, sem = ("d", id(d.dbuf)), d.dbuf.dsem
                    else:
                        if d.eng == op.eng and (op.eng == "pe"):
                            continue
                        key, sem = d.eng, esem[d.eng]
                    v = d.value
                    if waited.get(key, 0) >= v:
                        continue
                    if key not in best or best[key][1] < v:
                        best[key] = (sem, v)
                for key, (sem, v) in best.items():
                    waited[key] = v
                return list(best.values())

            def run(eng_name, e):
                waited = {}
                for op in self.ops[eng_name]:
                    ws = resolve(op, waited)
                    if op.multi:
                        for (s, v) in ws:
                            e.wait_ge(s, v)
                        ins = op.fn(e)
                    else:
                        for (s, v) in ws[:-1]:
                            e.wait_ge(s, v)
                        ins = op.fn(e)
                        if ws:
                            ins._wait_ge(ws[-1][0], ws[-1][1])
                    if op.is_dma:
                        ins.then_inc(op.dbuf.dsem, 16)
                    elif op.needed:
                        ins.then_inc(esem[eng_name], 1)
                if eng_name == "sp":
                    tot = {}
                    for o in self.all_ops:
                        if o.is_dma:
                            tot[id(o.dbuf)] = (o.dbuf.dsem, o.value)
                    for (s, v) in tot.values():
                        e.wait_ge(s, v)

            @block.tensor
            def _(e):
                run("pe", e)

            @block.scalar
            def _(e):
                run("act", e)

            @block.vector
            def _(e):
                run("dve", e)

            @block.gpsimd
            def _(e):
                run("pool", e)

            @block.sync
            def _(e):
                run("sp", e)


class Ctx:
    pass


def _alloc(nc, st, name, shape, dt):
    return st.enter_context(nc.sbuf_tensor(name, list(shape), dt))


def build(T, phases, depth=4, dbg=None):
    nc = bass.Bass("TRN2", target_bir_lowering=False)
    p = Prog(nc)
    g = Ctx()
    g.nc, g.p, g.T = nc, p, T
    NT = T // 512
    g.NT = NT

    decl = {}

    def dram(name, shape, dt=F32, kind="ExternalInput"):
        if kind == "ExternalInput":
            decl[name] = tuple(shape)
        return nc.dram_tensor(name, list(shape), dt, kind=kind).ap()

    nc._decl_inputs = decl

    g.x_in = dram("xT", [D, T])
    g.ctx_in = dram("ctxT", [D, CTX])
    g.cvec = dram("cvec", [128, 16])
    g.ada_w = dram("ada_w", [depth, D, 6 * D])
    g.ada_b = dram("ada_bT", [depth, 128, 48])
    g.normT = dram("normT", [depth, 128, 16])
    g.w_gate = dram("ffn_w_gate", [depth, D, FH])
    g.w_up = dram("ffn_w_up", [depth, D, FH])
    g.w_down = dram("ffn_w_down", [depth, FH, D])
    g.attn_wqkv = dram("attn_wqkv", [2, D, 1536])
    g.attn_wo = dram("attn_wo", [2, D, D])
    g.attn_qn = dram("attn_qn", [2, 128, 64])
    g.attn_kn = dram("attn_kn", [2, 128, 64])
    g.attn_qns = dram("attn_qns", [2, 128, 64])
    g.attn_kns = dram("attn_kns", [2, 128, 64])
    g.attn_sinkB = dram("attn_sinkB", [2, 128, 16])
    g.attn_cs = dram("attn_cs", [T, 96])
    g.attn_masks = dram("attn_masks", [128, 1024])
    g.ident = dram("ident", [128, 128])
    g.ret_win = dram("ret_w_in", [1, D, 8192])
    g.ret_wo = dram("ret_w_o", [1, 2048, D])
    g.ret_cs = dram("ret_cs", [2, 128, T])
    g.ret_dec = dram("ret_dec", [2, 128, 1024])
    g.ret_masks = dram("ret_masks", [2, 128, 128])
    g.ZT = dram("ZT", [2048, T], BF16, kind="Internal")
    g.ZTC = dram("ZTC", [2048, CTX], BF16, kind="Internal")
    g.VS = dram("VS", [T, 2048], BF16, kind="Internal")
    g.VSC = dram("VSC", [CTX, 2048], BF16, kind="Internal")
    g.RS = dram("RS", [16, 128, T], F32, kind="Internal")
    g.RSC = dram("RSC", [16, 128, CTX], F32, kind="Internal")
    g.pool_w = dram("pool_w", [1, 4, 256, 256])
    g.pool_scT = dram("pool_scT", [1, 128, 8])
    g.pool_rc = dram("pool_rc", [4, 128, 4 * 512])
    g.y = dram("yT", [D, T], kind="ExternalOutput")
    g.X2 = dram("X2", [D, T], kind="Internal")
    g.XC2 = dram("XC2", [D, CTX], kind="Internal")
    g.X1 = dram("X1", [D, T], kind="Internal")
    g.XC = dram("XC", [D, CTX], kind="Internal")
    g.dbg = {}
    if dbg:
        for k, shp in dbg.items():
            g.dbg[k] = dram(k, shp, kind="ExternalOutput")

    with contextlib.ExitStack() as st:
        g.st = st
        g.ones_bf = _alloc(nc, st, "ones_bf", [128, 128], BF16)
        g.ADA = _alloc(nc, st, "ADA", [128, depth, 48, 2], F32)
        g.AB = _alloc(nc, st, "AB", [128, depth, 2, 2, 8, 2], F32)
        g.eps_t = _alloc(nc, st, "eps_t", [128, 1], F32)
        g.eps64 = _alloc(nc, st, "eps64", [128, 1], F32)
        g.b_eps = Buf("eps")
        g.b_ADA = Buf("ADA")
        g.b_ones = Buf("ones")
        g.psum = [st.enter_context(nc.psum_tensor("ps%d" % i, [128, 512], F32)) for i in range(8)]
        g.b_ps = [Buf("ps%d" % i) for i in range(8)]
        g.ARENA_W = 51000
        g.ST = [Buf("ST0"), Buf("ST1"), Buf("ST2")]
        g.OUT = [Buf("OUT0"), Buf("OUT1"), Buf("OUT2")]
        g.arena = _alloc(nc, st, "arena", [128, g.ARENA_W], F32)

        p.memset("dve", g.ones_bf[:], (1.0 / D) if USE_POW else 1.0, [g.b_ones])
        p.memset("dve", g.eps_t[:], EPS, [g.b_eps])
        p.memset("dve", g.eps64[:], 64 * EPS, [g.b_eps])
        for ph in phases:
            ph(g)
            p.barrier()
        finals = [b for b in p.dma_bufs if b.name.startswith("OUT")]
        p.emit(finals)
    return nc


class Arena:
    def __init__(self, g):
        self.g = g
        self.off = 0

    def f32(self, words, shape=None):
        a = self.g.arena[:, self.off:self.off + words]
        self.off += words
        assert self.off <= self.g.ARENA_W, self.off
        return a

    def bf16(self, elems):
        words = (elems + 1) // 2
        a = self.g.arena[:, self.off:self.off + words].bitcast(BF16)
        self.off += words
        assert self.off <= self.g.ARENA_W, self.off
        return a


def v3(ap, a, b):
    return ap.rearrange("p (a b) -> p a b", a=a, b=b)


def phase_ada(g, depth=4):
    nc, p = g.nc, g.p
    ar = Arena(g)
    cv = ar.f32(16)
    cond = ar.f32(16)
    sig = ar.f32(16)
    badab = ar.f32(depth * 48)
    nrm = ar.f32(depth * 2 * 8)
    wst = [ar.f32(8 * 512) for _ in range(3)]
    b_cv, b_cond, b_bias, b_nrm = Buf("cv"), Buf("cond"), Buf("adab"), Buf("nrm")
    b_sig = Buf("sig")
    b_w = [Buf("adaw%d" % i) for i in range(3)]
    p.dma("sp", cv, g.cvec, [], [b_cv])
    for l in range(depth):
        p.dma("sp", badab[:, l * 48:(l + 1) * 48], g.ada_b[l], [], [b_bias])
        p.dma("sp", nrm[:, (l * 2) * 8:(l * 2 + 2) * 8], g.normT[l], [], [b_nrm])
    p.act(sig, cv, AF.Sigmoid, [b_cv], [b_sig])
    p.tt("dve", cond, cv, sig, ALU.mult, [b_cv, b_sig], [b_cond])
    cond3 = v3(cond, 8, 2)
    ps = g.psum[0]
    bps = g.b_ps[0]
    piece = 0
    for l in range(depth):
        for nb in range(12):
            slot = piece % 3
            piece += 1
            w3 = v3(wst[slot], 8, 512)
            p.dma("sp", w3, g.ada_w[l].rearrange("(c q) n -> q c n", q=128)[:, :, nb * 512:(nb + 1) * 512],
                  [], [b_w[slot]])
            for j in range(4):
                n = nb * 4 + j
                for c in range(8):
                    p.mm(ps[:, 2 * n:2 * n + 2], w3[:, c, j * 128:(j + 1) * 128], cond3[:, c, :],
                         (c == 0), (c == 7), [b_w[slot], b_cond], [bps])
        bb = badab[:, l * 48:(l + 1) * 48]
        for t in range(2):
            p.tt("dve", g.ADA[:, l, :, t], v3(ps[:, 0:96], 48, 2)[:, :, t], bb, ALU.add,
                 [bps, b_bias], [g.b_ADA])
        for s in range(2):
            nr = nrm[:, (l * 2 + s) * 8:(l * 2 + s + 1) * 8]
            for t in range(2):
                sc = g.ADA[:, l, s * 24 + 8:s * 24 + 16, t]
                sh = g.ADA[:, l, s * 24 + 0:s * 24 + 8, t]
                p.stt(g.AB[:, l, s, 0, :, t], sc, 1.0, nr, ALU.add, ALU.mult, [g.b_ADA, b_nrm], [g.b_ADA])
                p.copy("dve", g.AB[:, l, s, 1, :, t], sh, [g.b_ADA], [g.b_ADA])


def vecs(g, l, s, t):
    A = g.AB[:, l, s, 0, :, t]
    B = g.AB[:, l, s, 1, :, t]
    G = g.ADA[:, l, s * 24 + 16:s * 24 + 24, t]
    return A, B, G


def load_w_cast(g, dst3, src3, buf, ncols, rows):
    p = g.p
    step = 1024
    for c0 in range(0, ncols, step):
        c1 = min(ncols, c0 + step)
        for r0 in range(0, rows, 4):
            r1 = min(rows, r0 + 4)
            p.dma("pool", dst3[:, r0:r1, c0:c1], src3[:, r0:r1, c0:c1], [], [buf])


def prenorm(g, x3, N, A, B, h3, b_x, b_h, ps_i=0, ps_t=7, sq3=None, b_sq=None, lnexp=False):
    p = g.p
    ps = g.psum[ps_i][:, 0:N]
    bps = g.b_ps[ps_i]
    pt = g.psum[ps_t][:, 0:N]
    bpt = g.b_ps[ps_t]
    if sq3 is None:
        sq3, b_sq = h3, b_h
    p.act(sq3, x3, AF.Square, [b_x], [b_sq])
    for c in range(8):
        p.mm(ps, g.ones_bf[:], sq3[:, c, :], (c == 0), (c == 7), [b_sq, g.b_ones], [bps])
    if USE_POW:
        p.ts("dve", ps, ps, EPS, -0.5, ALU.add, ALU.pow, [bps], [bps])
    else:
        p.act(ps, ps, AF.Sqrt, [bps, g.b_eps], [bps], scale=1.0 / D, bias=g.eps_t[:, 0:1])
        p.call("dve", "reciprocal", [bps], [bps], out=ps, in_=ps)
    for c in range(8):
        p.stt(pt, x3[:, c, :], A[:, c:c + 1], ps, ALU.mult, ALU.mult, [b_x, bps, g.b_ADA], [bpt])
        p.act(h3[:, c, :], pt, AF.Identity, [bpt, g.b_ADA], [b_h], scale=1.0, bias=B[:, c:c + 1])


def prenorm_stats(g, x3, N, h3, b_x, b_h, ps_i=0, rstd=None, b_rstd=None):
    p = g.p
    ps = g.psum[ps_i][:, 0:N]
    bps = g.b_ps[ps_i]
    p.act(h3, x3, AF.Square, [b_x], [b_h])
    for c in range(8):
        p.mm(ps, g.ones_bf[:], h3[:, c, :], (c == 0), (c == 7), [b_h, g.b_ones], [bps])
    p.act(ps, ps, AF.Sqrt, [bps, g.b_eps], [bps], scale=1.0 / D, bias=g.eps_t[:, 0:1])
    if rstd is None:
        p.call("dve", "reciprocal", [bps], [bps], out=ps, in_=ps)
    else:
        p.call("dve", "reciprocal", [bps], [b_rstd], out=rstd, in_=ps)


def prenorm_apply(g, x3, N, A, B, h3, b_x, b_h, c, tmp, b_tmp, ps_i=0, rstd=None, b_rstd=None):
    p = g.p
    if rstd is None:
        rstd, b_rstd = g.psum[ps_i][:, 0:N], g.b_ps[ps_i]
    p.stt(tmp, x3[:, c, :], A[:, c:c + 1], rstd, ALU.mult, ALU.mult, [b_x, b_rstd, g.b_ADA], [b_tmp])
    p.act(h3[:, c, :], tmp, AF.Identity, [b_tmp, g.b_ADA], [b_h], scale=1.0, bias=B[:, c:c + 1])


def phase_pool(g, l, slot, src, dst, src_c, dst_c):
    nc, p = g.nc, g.p
    ar = Arena(g)
    W = 528
    pw = ar.bf16(4 * 2 * 256)
    pw4 = pw.rearrange("p (g k d) -> p g k d", g=4, k=2, d=256)
    b_pw = Buf("pw")
    load_w_cast(g, pw.rearrange("p (r d) -> p r d", r=8, d=256),
                g.pool_w[slot].rearrange("g (k q) d -> q (g k) d", q=128), b_pw, 256, 8)
    rc = [ar.f32(4 * 512) for _ in range(4)]
    b_rc = Buf("rc")
    for v in range(4):
        p.dma("sp", rc[v], g.pool_rc[v], [], [b_rc])
    psc = ar.f32(8)
    GS = ar.f32(16)
    b_psc, b_GS = Buf("psc"), Buf("GS")
    p.dma("sp", psc, g.pool_scT[slot], [], [b_psc])
    for t in range(2):
        _, _, G = vecs(g, l, 0, t)
        p.tt("dve", v3(GS, 8, 2)[:, :, t], G, psc, ALU.mult, [g.b_ADA, b_psc], [b_GS])
    xs = [ar.f32(8 * W) for _ in range(2)]
    b_xs = [Buf("px0"), Buf("px1")]
    hp = ar.f32(8 * W)
    sq = ar.bf16(8 * W)
    S2, S4, S8, S16 = ar.f32(8 * W), ar.f32(6 * W), ar.f32(4 * W), ar.f32(2 * W)
    tmp = ar.f32(2 * 512)
    pooled = ar.bf16(8 * 512)
    prs = ar.f32(W)
    b_prs = Buf("prs")
    ptm = [ar.f32(W) for _ in range(2)]
    b_ptm = [Buf("ptm0"), Buf("ptm1")]
    b_hp, b_sq, b_S2, b_S4, b_S8, b_S16, b_tmp = (Buf(n) for n in ("hp", "sq", "S2", "S4", "S8", "S16", "ptmp"))
    b_pl = [Buf("pl%d" % i) for i in range(4)]

    tiles = [(src, dst, t * 512, 512, 0, g.T) for t in range(g.NT)] + [(src_c, dst_c, 0, CTX, 1, CTX)]

    def issue_load(i):
        s_, d_, t0, N, strm, Ttot = tiles[i]
        x3 = v3(xs[i % 2], 8, W)
        lo, hi = max(t0 - 8, 0), min(t0 + N + 8, Ttot)
        c0 = lo - (t0 - 8)
        if lo != t0 - 8:
            p.memset("pool", x3[:, :, 0:8], 0.0, [b_xs[i % 2]])
        if hi != t0 + N + 8:
            p.memset("pool", x3[:, :, N + 8:N + 16], 0.0, [b_xs[i % 2]])
        p.dma("sp", x3[:, :, c0:c0 + (hi - lo)], s_.rearrange("(c q) t -> q c t", q=128)[:, :, lo:hi],
              [], [b_xs[i % 2]])

    issue_load(0)
    for i, (s_, d_, t0, N, strm, Ttot) in enumerate(tiles):
        if i + 1 < len(tiles):
            issue_load(i + 1)
        WN = N + 16
        x3 = v3(xs[i % 2], 8, W)
        b_x = b_xs[i % 2]
        h3 = v3(hp, 8, W)
        sq3 = v3(sq, 8, W)
        A, B, _ = vecs(g, l, 0, strm)
        p.act(sq3[:, :, 0:WN], x3[:, :, 0:WN], AF.Square, [b_x], [b_sq])
        for gi_, (a0, a1) in enumerate(((0, min(512, WN)), (512, WN))):
            if a1 > a0:
                psr = g.psum[(0, 7)[gi_]][:, 0:a1 - a0]
                bpsr = g.b_ps[(0, 7)[gi_]]
                for c in range(8):
                    p.mm(psr, g.ones_bf[:], sq3[:, c, a0:a1], (c == 0), (c == 7), [b_sq, g.b_ones], [bpsr])
                p.act(psr, psr, AF.Sqrt, [bpsr, g.b_eps], [bpsr], scale=1.0 / D, bias=g.eps_t[:, 0:1])
                p.call("dve", "reciprocal", [bpsr], [b_prs], out=prs[:, a0:a1], in_=psr)
        for c in range(8):
            p.stt(ptm[c % 2][:, 0:WN], x3[:, c, 0:WN], A[:, c:c + 1], prs[:, 0:WN], ALU.mult, ALU.mult,
                  [b_x, b_prs, g.b_ADA], [b_ptm[c % 2]])
            p.act(h3[:, c, 0:WN], ptm[c % 2][:, 0:WN], AF.Identity, [b_ptm[c % 2], g.b_ADA], [b_hp],
                  scale=1.0, bias=B[:, c:c + 1])
        if t0 == 0:
            p.memset("pool", h3[:, :, 0:8], 0.0, [b_hp])
        if t0 + N == Ttot:
            p.memset("pool", h3[:, :, N + 8:N + 16], 0.0, [b_hp])
        s2, s4, s8, s16 = v3(S2, 8, W), v3(S4, 6, W), v3(S8, 4, W), v3(S16, 2, W)
        p.tt("dve", s2[:, :, 1:WN], h3[:, :, 0:WN - 1], h3[:, :, 1:WN], ALU.add, [b_hp], [b_S2])
        p.tt("dve", s4[:, :, 2:WN - 1], s2[:, 2:8, 1:WN - 2], s2[:, 2:8, 3:WN], ALU.add, [b_S2], [b_S4])
        p.tt("pool", s8[:, :, 4:WN - 3], s4[:, 2:6, 2:WN - 5], s4[:, 2:6, 6:WN - 1], ALU.add, [b_S4], [b_S8])
        p.tt("pool", s16[:, :, 8:WN - 8], s8[:, 2:4, 4:WN - 12], s8[:, 2:4, 12:WN - 4], ALU.add, [b_S8], [b_S16])
        if strm == 1:
            var = 3
        elif t0 == 0:
            var = 1
        elif t0 + N == Ttot:
            var = 2
        else:
            var = 0
        rcv = v3(rc[var], 4, 512)
        srcs = [(s2, 0, b_S2), (s4, 0, b_S4), (s8, 0, b_S8), (s16, 0, b_S16)]
        pl3 = v3(pooled, 8, 512)
        tmp3 = v3(tmp, 2, 512)
        for gi in range(4):
            sw, _, b_sw = srcs[gi]
            if var == 0:
                for k in range(2):
                    p.stt(pl3[:, gi * 2 + k, 0:N], sw[:, k, 8:8 + N], 1.0 / (2, 4, 8, 16)[gi], h3[:, gi * 2 + k, 8:8 + N],
                          ALU.mult, ALU.subtract, [b_sw, b_hp], [b_pl[gi]])
                continue
            for k in range(2):
                eng = "dve" if k == 0 else "pool"
                p.tt(eng, tmp3[:, k, 0:N], sw[:, k, 8:8 + N], rcv[:, gi, 0:N], ALU.mult, [b_sw, b_rc], [b_tmp])
                p.tt(eng, pl3[:, gi * 2 + k, 0:N], tmp3[:, k, 0:N], h3[:, gi * 2 + k, 8:8 + N], ALU.subtract,
                     [b_tmp, b_hp], [b_pl[gi]])
        gsv = v3(GS, 8, 2)[:, :, strm]
        for gi in range(4):
            for oc in range(2):
                c = gi * 2 + oc
                po = 5 + (c % 2)
                for k in range(2):
                    p.mm(g.psum[po][:, 0:N], pw4[:, gi, k, oc * 128:(oc + 1) * 128], pl3[:, gi * 2 + k, 0:N],
                         (k == 0), (k == 1), [b_pw, b_pl[gi]], [g.b_ps[po]])
                p.stt(x3[:, c, 8:8 + N], g.psum[po][:, 0:N], gsv[:, c:c + 1], x3[:, c, 8:8 + N],
                      ALU.mult, ALU.add, [g.b_ps[po], b_x, b_GS], [b_x])
        ob = (g.OUT if d_ is g.y else g.ST)[i % 2]
        p.dma("sp", d_.rearrange("(c q) t -> q c t", q=128)[:, :, t0:t0 + N], x3[:, :, 8:8 + N], [b_x], [ob])


def phase_ffn(g, l, src, dst, src_c, dst_c, with_ctx=True):
    nc, p = g.nc, g.p
    ar = Arena(g)
    wg = ar.bf16(8 * FH)
    wu = ar.bf16(8 * FH)
    wd = ar.bf16(NJ * D)
    wg3, wu3, wd3 = v3(wg, 8, FH), v3(wu, 8, FH), v3(wd, NJ, D)
    b_wg, b_wu, b_wd = Buf("wg"), Buf("wu"), Buf("wd")
    xs = [ar.f32(8 * 512) for _ in range(2)]
    b_xs = [Buf("x%d" % i) for i in range(2)]
    h = ar.bf16(8 * 512)
    a = ar.bf16(NJ * 512)
    sg = ar.f32(512)
    b_h = Buf("h")
    b_a = [Buf("a%d" % j) for j in range(NJ)]
    b_sg = Buf("sg0")

    load_w_cast(g, wg3, g.w_gate[l].rearrange("(c q) n -> q c n", q=128), b_wg, FH, 8)
    load_w_cast(g, wu3, g.w_up[l].rearrange("(c q) n -> q c n", q=128), b_wu, FH, 8)
    load_w_cast(g, wd3, g.w_down[l].rearrange("(j q) n -> q j n", q=128), b_wd, D, NJ)

    tiles = [(src, dst, t * 512, 512, 0) for t in range(g.NT)]
    if with_ctx:
        tiles.append((src_c, dst_c, 0, CTX, 1))

    def issue_load(i):
        s_, d_, t0, N, strm = tiles[i]
        x3 = v3(xs[i % 2], 8, 512)[:, :, 0:N]
        p.dma("sp", x3, s_.rearrange("(c q) t -> q c t", q=128)[:, :, t0:t0 + N], [], [b_xs[i % 2]])

    def do_prenorm(i):
        s_, d_, t0, N, strm = tiles[i]
        A, B, _ = vecs(g, l, 1, strm)
        prenorm(g, v3(xs[i % 2], 8, 512)[:, :, 0:N], N, A, B, v3(h, 8, 512)[:, :, 0:N], b_xs[i % 2], b_h)

    issue_load(0)
    if len(tiles) > 1:
        issue_load(1)
    do_prenorm(0)
    for i, (s_, d_, t0, N, strm) in enumerate(tiles):
        x3 = v3(xs[i % 2], 8, 512)[:, :, 0:N]
        b_x = b_xs[i % 2]
        h3 = v3(h, 8, 512)[:, :, 0:N]
        a3 = v3(a, NJ, 512)[:, :, 0:N]
        _, _, G = vecs(g, l, 1, strm)
        for j in range(NJ):
            pg, pu = 1 + 2 * (j % 2), 2 + 2 * (j % 2)
            for c in range(8):
                p.mm(g.psum[pg][:, 0:N], wg3[:, c, j * 128:(j + 1) * 128], h3[:, c, :],
                     (c == 0), (c == 7), [b_wg, b_h], [g.b_ps[pg]])
            for c in range(8):
                p.mm(g.psum[pu][:, 0:N], wu3[:, c, j * 128:(j + 1) * 128], h3[:, c, :],
                     (c == 0), (c == 7), [b_wu, b_h], [g.b_ps[pu]])
            sgt = sg[:, 0:N]
            p.act(sgt, g.psum[pg][:, 0:N], AF.Silu, [g.b_ps[pg]], [b_sg])
            p.tt("dve", a3[:, j, :], g.psum[pu][:, 0:N], sgt, ALU.mult, [g.b_ps[pu], b_sg], [b_a[j]])
        for c in range(8):
            po = 5 + (c % 2)
            for j in range(NJ):
                p.mm(g.psum[po][:, 0:N], wd3[:, j, c * 128:(c + 1) * 128], a3[:, j, :],
                     (j == 0), (j == NJ - 1), [b_wd, b_a[j]], [g.b_ps[po]])
            p.stt(x3[:, c, :], g.psum[po][:, 0:N], G[:, c:c + 1], x3[:, c, :], ALU.mult, ALU.add,
                  [g.b_ps[po], b_x, g.b_ADA], [b_x])
            if c == 1 and i + 1 < len(tiles):
                do_prenorm(i + 1)
        ob = (g.OUT if d_ is g.y else g.ST)[i % 2]
        p.dma("sp", d_.rearrange("(c q) t -> q c t", q=128)[:, :, t0:t0 + N], x3, [b_x], [ob])
        if i + 2 < len(tiles):
            issue_load(i + 2)


def host_small(c_b, c_ctx, ada_b, norm_mix, norm_ffn):
    depth = ada_b.shape[0]
    cvec = np.stack([c_b.reshape(8, 128).T, c_ctx.reshape(8, 128).T], axis=2).reshape(128, 16)
    ada_bT = np.ascontiguousarray(ada_b.reshape(depth, 48, 128).transpose(0, 2, 1))
    normT = np.stack([norm_mix.reshape(depth, 8, 128).transpose(0, 2, 1),
                      norm_ffn.reshape(depth, 8, 128).transpose(0, 2, 1)], axis=2).reshape(depth, 128, 16)
    return dict(cvec=np.ascontiguousarray(cvec, dtype=np.float32), ada_bT=ada_bT.astype(np.float32),
                normT=np.ascontiguousarray(normT, dtype=np.float32))


def host_pool_rc(T):
    out = np.zeros((4, 4, 512), np.float32)
    for gi, w in enumerate((2, 4, 8, 16)):
        def rcp(Ttot, t):
            lo = np.maximum(t - w // 2, 0)
            hi = np.minimum(t + w // 2, Ttot)
            return (1.0 / (hi - lo)).astype(np.float32)
        out[0, gi, :] = 1.0 / w
        out[1, gi, :] = rcp(T, np.arange(512))
        out[2, gi, :] = rcp(T, np.arange(T - 512, T))
        out[3, gi, :256] = rcp(CTX, np.arange(256))
        out[3, gi, 256:] = 1.0 / w
    return np.ascontiguousarray(np.broadcast_to(out.reshape(4, 1, 2048), (4, 128, 2048)))


def qk_norm_rot(g, src, nh, gain, cs, is_q, out_bf, b_src, b_gain, b_cs, b_out, W):
    qk_part1(g, src, nh, W["st"][:, 0:nh], b_src, W)
    p = g.p
    st, b_st = W["st"][:, 0:nh], W["b_st"]
    if is_q:
        p.act(st, st, AF.Sqrt, [b_st, W["b_e"]], [b_st], scale=1.0, bias=W["eps64"][:, 0:1])
    else:
        p.act(st, st, AF.Sqrt, [b_st, g.b_eps], [b_st], scale=1.0 / 64, bias=g.eps_t[:, 0:1])
    p.call("dve", "reciprocal", [b_st], [b_st], out=st, in_=st)
    qk_part2(g, src, nh, gain, cs, st, b_st, out_bf, b_src, b_gain, b_cs, b_out, W)


def qk_part1(g, src, nh, st, b_src, W):
    p = g.p
    n = nh * 64
    sq = W["sq"][:, 0:n]
    p.act(sq, src, AF.Square, [b_src], [W["b_sq"]])
    p.call("dve", "tensor_reduce", [W["b_sq"]], [W["b_st"]], out=st, in_=v3(sq, nh, 64), op=ALU.add, axis=AX.X)


def qk_part2(g, src, nh, gain, cs, st, b_st, out_bf, b_src, b_gain, b_cs, b_out, W):
    p = g.p
    n = nh * 64
    qn, t2 = W["qn"][:, 0:n], W["t2"][:, 0:n]
    b_qn, b_t2 = W["b_qn"], W["b_t2"]
    p.tt("dve", v3(qn, nh, 64), v3(src, nh, 64), st.unsqueeze(2).broadcast_to([128, nh, 64]), ALU.mult,
         [b_src, b_st], [b_qn])
    if cs is None:
        p.tt("pool", v3(out_bf, nh, 64), v3(qn, nh, 64), gain.unsqueeze(1).broadcast_to([128, nh, 64]), ALU.mult,
             [b_qn, b_gain], [b_out])
        return
    p.tt("pool", v3(qn, nh, 64), v3(qn, nh, 64), gain.unsqueeze(1).broadcast_to([128, nh, 64]), ALU.mult,
         [b_qn, b_gain], [b_qn])
    q4 = qn.rearrange("p (h two d) -> p h two d", h=nh, two=2, d=32)
    t4 = t2.rearrange("p (h two d) -> p h two d", h=nh, two=2, d=32)
    cosb = cs[:, 0:32].unsqueeze(1).broadcast_to([128, nh, 32])
    sinb = cs[:, 32:64].unsqueeze(1).broadcast_to([128, nh, 32])
    nsinb = cs[:, 64:96].unsqueeze(1).broadcast_to([128, nh, 32])
    p.tt("pool", t4[:, :, 0, :], q4[:, :, 1, :], nsinb, ALU.mult, [b_qn, b_cs], [b_t2])
    p.tt("pool", t4[:, :, 1, :], q4[:, :, 0, :], sinb, ALU.mult, [b_qn, b_cs], [b_t2])
    ce = "pool" if W.get("pool_only") else "dve"
    for two in range(2):
        p.tt(ce, q4[:, :, two, :], q4[:, :, two, :], cosb, ALU.mult, [b_qn, b_cs, b_t2], [b_qn])
    p.tt("pool", out_bf, qn, t2, ALU.add, [b_qn, b_t2], [b_out])


def phase_attn_v1(g, l, slot, src, dst, src_c, dst_c, need_ctx_out):
    nc, p = g.nc, g.p
    T, NT = g.T, g.NT
    NB = T // 128
    ar = Arena(g)
    KT = ar.bf16(2 * (T + CTX))
    KT3 = v3(KT, 2, T + CTX)
    VA = ar.bf16((NB + 2) * 4 * 65)
    VA4 = VA.rearrange("p (b g d) -> p b g d", b=NB + 2, g=4, d=65)
    wqkv = ar.bf16(8 * 1536)
    wo = ar.bf16(8 * 1024)
    wqkv3, wo3 = v3(wqkv, 8, 1536), v3(wo, 8, 1024)
    b_wqkv, b_wo = Buf("wqkv"), Buf("wo")
    load_w_cast(g, wqkv3, g.attn_wqkv[slot].rearrange("(c q) n -> q c n", q=128), b_wqkv, 1536, 8)
    load_w_cast(g, wo3, g.attn_wo[slot].rearrange("(c q) n -> q c n", q=128), b_wo, 1024, 8)
    ident = ar.bf16(128)
    masks = ar.bf16(2 * 512)
    b_ident, b_masks = Buf("ident"), Buf("masks")
    p.dma("pool", ident, g.ident, [], [b_ident])
    p.dma("pool", masks, g.attn_masks, [], [b_masks])
    gq, gk = ar.f32(64), ar.f32(64)
    b_gq, b_gk = Buf("gq"), Buf("gk")
    p.dma("sp", gq, g.attn_qn[slot], [], [b_gq])
    p.dma("sp", gk, g.attn_kn[slot], [], [b_gk])
    esink = ar.f32(16)
    b_esink = Buf("esink")
    p.dma("sp", esink, g.attn_sinkB[slot], [], [b_esink])
    p.act(esink, esink, AF.Exp, [b_esink], [b_esink])
    W = dict(sq=ar.f32(512), qn=ar.f32(512), t2=ar.f32(512), st=ar.f32(8), eps64=ar.f32(1),
             b_sq=Buf("sq"), b_qn=Buf("qn"), b_t2=Buf("t2"), b_st=Buf("st"), b_e=Buf("e64"))
    p.memset("dve", W["eps64"], 64 * EPS, [W["b_e"]])
    xs = [ar.f32(8 * 512) for _ in range(2)]
    b_xs = [Buf("ax0"), Buf("ax1")]
    cst = [ar.f32(4 * 96) for _ in range(2)]
    b_cst = [Buf("cs0"), Buf("cs1")]
    h = ar.bf16(8 * 512)
    b_h = Buf("ah")
    krot = ar.bf16(256)
    qrot = ar.bf16(1024)
    b_krot, b_qrot = Buf("krot"), [Buf("qrot0"), Buf("qrot1")]
    QT = ar.bf16(8 * 128)
    QT3 = v3(QT, 8, 128)
    b_QT = Buf("QT")
    PT = [ar.bf16(5 * 512) for _ in range(2)]
    b_PT = [[Buf("PT%d_%d" % (s, c)) for c in range(5)] for s in range(2)]
    O = ar.bf16(1024)
    b_O = Buf("O")
    OT = ar.bf16(8 * 512)
    OT3 = v3(OT, 8, 512)
    b_OT = Buf("OT")
    den = ar.f32(4)
    b_den = Buf("den")
    b_KT = [Buf("KT%d" % i) for i in range(NB + 2)]
    b_V = [Buf("V%d" % i) for i in range(NB + 2)]
    p.memset("pool", VA4[:, :, :, 64:65], 1.0, b_V)
    psT = g.psum[3][:].bitcast(BF16)
    psT3 = v3(psT, 8, 128)
    b_psT = g.b_ps[3]

    lat_tiles = [(src, dst, t * 512, 512, 0) for t in range(NT)]
    ctx_tile = (src_c, dst_c, 0, CTX, 1)

    def load(i, tile):
        s_, d_, t0, N, strm = tile
        x3 = v3(xs[i % 2], 8, 512)[:, :, 0:N]
        p.dma("sp", x3, s_.rearrange("(c q) t -> q c t", q=128)[:, :, t0:t0 + N], [], [b_xs[i % 2]])
        if strm == 0:
            p.dma("sp", v3(cst[i % 2], 4, 96), g.attn_cs[t0:t0 + 512].rearrange("(b q) d -> q b d", q=128),
                  [], [b_cst[i % 2]])

    tilesA = [ctx_tile] + lat_tiles
    load(0, tilesA[0])
    for i, tile in enumerate(tilesA):
        if i + 1 < len(tilesA):
            load(i + 1, tilesA[i + 1])
        s_, d_, t0, N, strm = tile
        x3 = v3(xs[i % 2], 8, 512)[:, :, 0:N]
        h3 = v3(h, 8, 512)[:, :, 0:N]
        A, B, _ = vecs(g, l, 0, strm)
        prenorm(g, x3, N, A, B, h3, b_xs[i % 2], b_h)
        for bl in range(N // 128):
            kb = (NB + bl) if strm == 1 else (t0 // 128 + bl)
            kcol = (T + bl * 128) if strm == 1 else (t0 + bl * 128)
            pk = 1 + (bl % 2)
            for c in range(8):
                p.mm(g.psum[pk][:, 0:512], h3[:, c, bl * 128:(bl + 1) * 128], wqkv3[:, c, 1024:1536],
                     (c == 0), (c == 7), [b_h, b_wqkv], [g.b_ps[pk]])
            cs = v3(cst[i % 2], 4, 96)[:, bl, :] if strm == 0 else None
            qk_norm_rot(g, g.psum[pk][:, 0:256], 4, gk, cs, False, krot, g.b_ps[pk], b_gk, b_cst[i % 2], b_krot, W)
            p.copy("act", VA4[:, kb, :, 0:64], v3(g.psum[pk][:, 256:512], 4, 64), [g.b_ps[pk]], [b_V[kb]])
            for pr in range(2):
                p.call("pe", "transpose", [b_krot, b_ident], [b_psT], psT3[:, pr, :],
                       krot[:, pr * 128:(pr + 1) * 128], ident)
            p.copy("dve", KT3[:, :, kcol:kcol + 128], psT3[:, 0:2, :], [b_psT], [b_KT[kb]])

    tilesB = lat_tiles + ([ctx_tile] if need_ctx_out else [])
    base = len(tilesA)
    load(base, tilesB[0])
    for ii, tile in enumerate(tilesB):
        i = base + ii
        if ii + 1 < len(tilesB):
            load(i + 1, tilesB[ii + 1])
        s_, d_, t0, N, strm = tile
        x3 = v3(xs[i % 2], 8, 512)[:, :, 0:N]
        b_x = b_xs[i % 2]
        h3 = v3(h, 8, 512)[:, :, 0:N]
        A, B, G = vecs(g, l, 0, strm)
        prenorm(g, x3, N, A, B, h3, b_x, b_h)
        for bl in range(N // 128):
            blk = t0 // 128 + bl
            for hf in range(2):
                for c in range(8):
                    p.mm(g.psum[1 + hf][:, 0:512], h3[:, c, bl * 128:(bl + 1) * 128],
                         wqkv3[:, c, hf * 512:(hf + 1) * 512], (c == 0), (c == 7), [b_h, b_wqkv], [g.b_ps[1 + hf]])
            cs = v3(cst[i % 2], 4, 96)[:, bl, :] if strm == 0 else None
            for hf in range(2):
                qk_norm_rot(g, g.psum[1 + hf][:, 0:512], 8, gq, cs, True, qrot[:, hf * 512:(hf + 1) * 512],
                            g.b_ps[1 + hf], b_gq, b_cst[i % 2], b_qrot[hf], W)
            for s in range(8):
                p.call("pe", "transpose", [b_qrot[s // 4], b_ident], [b_psT], psT3[:, s, :],
                       qrot[:, s * 128:(s + 1) * 128], ident)
            p.copy("dve", QT, psT, [b_psT], [b_QT])
            chunks = [(T, NB, None), (T + 128, NB + 1, None)]
            if strm == 0:
                if blk > 0:
                    chunks.append(((blk - 1) * 128, blk - 1, 0))
                chunks.append((blk * 128, blk, None))
                if blk < NB - 1:
                    chunks.append(((blk + 1) * 128, blk + 1, 1))
            for gi in range(4):
                pr, half = gi // 2, gi % 2
                P0 = half * 64
                pts = gi % 2
                PT3 = v3(PT[pts], 5, 512)
                for ci, (kcol, vb, mk) in enumerate(chunks):
                    pss = 4 + (ci % 2)
                    p.mm(g.psum[pss][:, 0:512], KT3[P0:P0 + 64, pr, kcol:kcol + 128],
                         QT3[P0:P0 + 64, pr * 4:(pr + 1) * 4, :], True, True, [b_KT[vb], b_QT], [g.b_ps[pss]])
                    p.act(PT3[:, ci, :], g.psum[pss][:, 0:512], AF.Exp, [g.b_ps[pss]], [b_PT[pts][ci]])
                    if mk is not None:
                        p.tt("pool", PT3[:, ci, :], PT3[:, ci, :], masks[:, mk * 512:(mk + 1) * 512], ALU.mult,
                             [b_PT[pts][ci], b_masks], [b_PT[pts][ci]])
                po = g.psum[6][:, 0:260].rearrange("p (h d) -> p h d", h=4, d=65)
                for hl in range(4):
                    for ci, (kcol, vb, mk) in enumerate(chunks):
                        p.mm(po[:, hl, :], PT3[:, ci, hl * 128:(hl + 1) * 128], VA4[:, vb, gi, :],
                             (ci == 0), (ci == len(chunks) - 1), [b_PT[pts][ci], b_V[vb]], [g.b_ps[6]])
                p.tt("dve", den, po[:, :, 64], esink[:, gi * 4:(gi + 1) * 4], ALU.add, [g.b_ps[6], b_esink], [b_den])
                p.call("dve", "reciprocal", [b_den], [b_den], out=den, in_=den)
                p.tt("dve", v3(O, 16, 64)[:, gi * 4:(gi + 1) * 4, :], po[:, :, 0:64],
                     den.unsqueeze(2).broadcast_to([128, 4, 64]), ALU.mult, [g.b_ps[6], b_den], [b_O])
            for c in range(8):
                p.call("pe", "transpose", [b_O, b_ident], [b_psT], psT3[:, c, :], O[:, c * 128:(c + 1) * 128], ident)
            p.copy("dve", OT3[:, :, bl * 128:(bl + 1) * 128], psT3, [b_psT], [b_OT])
        for c in range(8):
            py = 1 + (c % 2)
            for k in range(8):
                p.mm(g.psum[py][:, 0:N], wo3[:, k, c * 128:(c + 1) * 128], OT3[:, k, 0:N],
                     (k == 0), (k == 7), [b_wo, b_OT], [g.b_ps[py]])
            p.stt(x3[:, c, :], g.psum[py][:, 0:N], G[:, c:c + 1], x3[:, c, :], ALU.mult, ALU.add,
                  [g.b_ps[py], b_x, g.b_ADA], [b_x])
        ob = (g.OUT if d_ is g.y else g.ST)[i % 2]
        p.dma("sp", d_.rearrange("(c q) t -> q c t", q=128)[:, :, t0:t0 + N], x3, [b_x], [ob])


def qk_chain(g, src, nh, is_q, rot, CG, SG, gain, out_bf, b_src, b_tab, b_out, Wk):
    p = g.p
    n = nh * 64
    sq, t1, t2, st = Wk["sq"][:, 0:n], Wk["t1"][:, 0:n], Wk["t2"][:, 0:n], Wk["st"][:, 0:nh]
    b_sq, b_t1, b_t2, b_st = Wk["b_sq"], Wk["b_t1"], Wk["b_t2"], Wk["b_st"]
    p.act(sq, src, AF.Square, [b_src], [b_sq])
    p.call("dve", "tensor_reduce", [b_sq], [b_st], out=st, in_=v3(sq, nh, 64), op=ALU.add, axis=AX.X)
    if USE_LN:
        if is_q:
            p.act(st, st, AF.Ln, [b_st, g.b_eps], [b_st], scale=1.0, bias=g.eps64[:, 0:1])
        else:
            p.act(st, st, AF.Ln, [b_st, g.b_eps], [b_st], scale=1.0 / 64, bias=g.eps_t[:, 0:1])
        p.act(st, st, AF.Exp, [b_st], [b_st], scale=-0.5)
    else:
        if is_q:
            p.act(st, st, AF.Sqrt, [b_st, g.b_eps], [b_st], scale=1.0, bias=g.eps64[:, 0:1])
        else:
            p.act(st, st, AF.Sqrt, [b_st, g.b_eps], [b_st], scale=1.0 / 64, bias=g.eps_t[:, 0:1])
        p.call("dve", "reciprocal", [b_st], [b_st], out=st, in_=st)
    s3 = v3(src, nh, 64)
    stb = st.unsqueeze(2).broadcast_to([128, nh, 64])
    if not rot:
        p.tt("dve", v3(t1, nh, 64), s3, gain.unsqueeze(1).broadcast_to([128, nh, 64]), ALU.mult,
             [b_src, b_tab], [b_t1])
        p.tt("pool", v3(out_bf, nh, 64), v3(t1, nh, 64), stb, ALU.mult, [b_t1, b_st], [b_out])
        return
    p.tt("dve", v3(t1, nh, 64), s3, CG.unsqueeze(1).broadcast_to([128, nh, 64]), ALU.mult, [b_src, b_tab], [b_t1])
    s4 = src.rearrange("p (h two d) -> p h two d", h=nh, two=2, d=32)
    t4 = t2.rearrange("p (h two d) -> p h two d", h=nh, two=2, d=32)
    p.tt("dve", t4[:, :, 0, :], s4[:, :, 1, :], SG[:, 0:32].unsqueeze(1).broadcast_to([128, nh, 32]), ALU.mult,
         [b_src, b_tab], [b_t2])
    p.tt("dve", t4[:, :, 1, :], s4[:, :, 0, :], SG[:, 32:64].unsqueeze(1).broadcast_to([128, nh, 32]), ALU.mult,
         [b_src, b_tab], [b_t2])
    p.tt("pool", t1, t1, t2, ALU.add, [b_t1, b_t2], [b_t1])
    p.tt("pool", v3(out_bf, nh, 64), v3(t1, nh, 64), stb, ALU.mult, [b_t1, b_st], [b_out])


def phase_attn(g, l, slot, src, dst, src_c, dst_c, need_ctx_out):
    nc, p = g.nc, g.p
    T, NT = g.T, g.NT
    NB = T // 128
    ar = Arena(g)
    NS = 6
    KT = ar.bf16(2 * NS * 128)
    KT3 = v3(KT, 2, NS * 128)
    VA = ar.bf16(NS * 4 * 65)
    VA4 = VA.rearrange("p (b g d) -> p b g d", b=NS, g=4, d=65)
    wqkv = ar.bf16(8 * 1536)
    wo = ar.bf16(8 * 1024)
    wqkv3, wo3 = v3(wqkv, 8, 1536), v3(wo, 8, 1024)
    b_wqkv, b_wo = Buf("wqkv"), Buf("wo")
    load_w_cast(g, wqkv3, g.attn_wqkv[slot].rearrange("(c q) n -> q c n", q=128), b_wqkv, 1536, 8)
    load_w_cast(g, wo3, g.attn_wo[slot].rearrange("(c q) n -> q c n", q=128), b_wo, 1024, 8)
    ident = ar.bf16(128)
    masks = ar.bf16(2 * 512)
    b_ident, b_masks = Buf("ident"), Buf("masks")
    p.dma("pool", ident, g.ident, [], [b_ident])
    p.dma("pool", masks, g.attn_masks, [], [b_masks])
    gq, gk, gqs, gks = ar.f32(64), ar.f32(64), ar.f32(64), ar.f32(64)
    b_gn = Buf("gains")
    p.dma("sp", gq, g.attn_qn[slot], [], [b_gn])
    p.dma("sp", gk, g.attn_kn[slot], [], [b_gn])
    p.dma("sp", gqs, g.attn_qns[slot], [], [b_gn])
    p.dma("sp", gks, g.attn_kns[slot], [], [b_gn])
    esink = ar.f32(16)
    b_esink = Buf("esink")
    p.dma("sp", esink, g.attn_sinkB[slot], [], [b_esink])
    p.act(esink, esink, AF.Exp, [b_esink], [b_esink])

    def wk(name):
        return dict(sq=ar.f32(512), t1=ar.f32(512), t2=ar.f32(512), st=ar.f32(8),
                    b_sq=Buf(name + "sq"), b_t1=Buf(name + "t1"), b_t2=Buf(name + "t2"), b_st=Buf(name + "st"))
    WQ = [wk("q0"), wk("q1")]
    WK = dict(sq=ar.f32(256), t1=ar.f32(256), t2=ar.f32(256), st=ar.f32(4),
              b_sq=Buf("ksq"), b_t1=Buf("kt1"), b_t2=Buf("kt2"), b_st=Buf("kst"))
    def wk1(name, n):
        return dict(sq=ar.f32(n), qn=ar.f32(n), t2=ar.f32(n), st=ar.f32(8), eps64=g.eps64,
                    b_sq=Buf(name + "sq"), b_qn=Buf(name + "qn"), b_t2=Buf(name + "t2"), b_st=Buf(name + "st"),
                    b_e=g.b_eps)
    WQ1 = [wk1("q0", 512), wk1("q1", 512)]
    WK1 = wk1("k", 256)
    for w_ in (WQ1[0], WQ1[1], WK1):
        w_["pool_only"] = True
    ptmp = [ar.f32(512) for _ in range(2)]
    b_ptmp = [Buf("ptmp0"), Buf("ptmp1")]
    rstd_sb = ar.f32(512)
    b_rstd_sb = Buf("rstd_sb")
    tabs = [ar.f32(4 * 64) for _ in range(2)]
    b_tabs = [Buf("tab0"), Buf("tab1")]
    xs = [ar.f32(8 * 512) for _ in range(2)]
    b_xs = [Buf("ax0"), Buf("ax1")]
    cst = [ar.f32(4 * 96) for _ in range(2)]
    b_cst = [Buf("cs0"), Buf("cs1")]
    h = ar.bf16(8 * 512)
    b_h = Buf("ah")
    h2 = ar.bf16(8 * 512)
    b_h2 = Buf("ah2")
    krot = [ar.bf16(256) for _ in range(2)]
    qrot = [ar.bf16(1024) for _ in range(2)]
    b_krot = [Buf("krot0"), Buf("krot1")]
    b_qrot = [[Buf("qrot%d_%d" % (a, b)) for b in range(2)] for a in range(2)]
    QTz = [[ar.bf16(8 * 128) for _ in range(2)] for _ in range(2)]
    b_QT = [Buf("QT0"), Buf("QT1")]
    for par_ in range(2):
        for hf_ in range(2):
            p.memset("pool", QTz[par_][hf_], 0.0, [b_QT[par_]])
    PT = [ar.bf16(5 * 512) for _ in range(2)]
    b_PT = [[Buf("PT%d_%d" % (s, c)) for c in range(5)] for s in range(2)]
    O = ar.bf16(1024)
    b_O = [Buf("O%d" % i) for i in range(4)]
    OT = ar.bf16(8 * 512)
    OT3 = v3(OT, 8, 512)
    b_OT = Buf("OT")
    den = [ar.f32(4) for _ in range(2)]
    b_den = [Buf("den0"), Buf("den1")]
    b_KT = [Buf("KT%d" % i) for i in range(NS)]
    b_V = [Buf("V%d" % i) for i in range(NS)]
    p.memset("pool", VA4[:, :, :, 64:65], 1.0, b_V)
    rk = ar.f32(NS * 4)
    rk3 = v3(rk, NS, 4)
    b_rk = [Buf("rk%d" % i) for i in range(NS)]
    psQT = g.psum[3][:].bitcast(BF16)
    psQT3 = v3(psQT, 8, 128)
    psOT = g.psum[7][:].bitcast(BF16)
    psOT3 = v3(psOT, 8, 128)
    psKT3 = v3(g.psum[6][:].bitcast(BF16)[:, 768:1024], 2, 128)
    b_psKT = g.b_ps[6]
    b_psPV = g.b_ps[6]

    tiles = [(src_c, dst_c, 0, CTX, 1)] + [(src, dst, t * 512, 512, 0) for t in range(NT)]
    units = []
    for ti, (s_, d_, t0, N, strm) in enumerate(tiles):
        nb = N // 128
        for bl in range(nb):
            units.append(dict(ti=ti, bl=bl, strm=strm, first=(bl == 0), last=(bl == nb - 1), N=N, t0=t0,
                              blk=(t0 // 128 + bl), need_q=(strm == 0 or need_ctx_out),
                              slot=(4 + bl) if strm == 1 else ((t0 // 128 + bl) % 4)))
    for k, u in enumerate(units):
        u["par"] = k % 2

    def load(ti):
        if ti >= len(tiles):
            return
        s_, d_, t0, N, strm = tiles[ti]
        x3 = v3(xs[ti % 2], 8, 512)[:, :, 0:N]
        p.dma("sp", x3, s_.rearrange("(c q) t -> q c t", q=128)[:, :, t0:t0 + N], [], [b_xs[ti % 2]])
        if strm == 0:
            p.dma("sp", v3(cst[ti % 2], 4, 96), g.attn_cs[t0:t0 + 512].rearrange("(b q) d -> q b d", q=128),
                  [], [b_cst[ti % 2]])

    hbuf = [h, h2]
    b_hb = [b_h, b_h2]

    def S0_stats(u):
        ti, strm, N = u["ti"], u["strm"], u["N"]
        x3 = v3(xs[ti % 2], 8, 512)[:, :, 0:N]
        h3 = v3(hbuf[ti % 2], 8, 512)[:, :, 0:N]
        prenorm_stats(g, x3, N, h3, b_xs[ti % 2], b_hb[ti % 2], ps_i=0, rstd=rstd_sb[:, 0:N], b_rstd=b_rstd_sb)

    def S0_apply(u, c):
        ti, strm, N = u["ti"], u["strm"], u["N"]
        x3 = v3(xs[ti % 2], 8, 512)[:, :, 0:N]
        h3 = v3(hbuf[ti % 2], 8, 512)[:, :, 0:N]
        A, B, _ = vecs(g, l, 0, strm)
        prenorm_apply(g, x3, N, A, B, h3, b_xs[ti % 2], b_hb[ti % 2], c, ptmp[c % 2][:, 0:N], b_ptmp[c % 2],
                      rstd=rstd_sb[:, 0:N], b_rstd=b_rstd_sb)

    st20 = ar.f32(20)
    b_st20 = Buf("st20")

    def proj_q(u, hf):
        ti, bl, strm, N, par = u["ti"], u["bl"], u["strm"], u["N"], u["par"]
        hb = v3(hbuf[ti % 2], 8, 512)[:, :, bl * 128:(bl + 1) * 128]
        for c in range(8):
            p.mm(g.psum[1 + hf][:, 0:512], hb[:, c, :], wqkv3[:, c, hf * 512:(hf + 1) * 512],
                 (c == 0), (c == 7), [b_hb[ti % 2], b_wqkv], [g.b_ps[1 + hf]])
        WQ1[hf]["b_st"] = b_st20
        qk_part1(g, g.psum[1 + hf][:, 0:512], 8, st20[:, hf * 8:(hf + 1) * 8], g.b_ps[1 + hf], WQ1[hf])

    def proj_kv(u):
        ti, bl, strm, N, par = u["ti"], u["bl"], u["strm"], u["N"], u["par"]
        hb = v3(hbuf[ti % 2], 8, 512)[:, :, bl * 128:(bl + 1) * 128]
        for c in range(8):
            p.mm(g.psum[0][:, 0:512], hb[:, c, :], wqkv3[:, c, 1024:1536], (c == 0), (c == 7),
                 [b_hb[ti % 2], b_wqkv], [g.b_ps[0]])
        cs = v3(cst[ti % 2], 4, 96)[:, bl, :] if strm == 0 else None
        WK1["b_st"] = b_st20
        qk_part1(g, g.psum[0][:, 0:256], 4, st20[:, 16:20], g.b_ps[0], WK1)
        p.copy("act", VA4[:, u["slot"], :, 0:64], v3(g.psum[0][:, 256:512], 4, 64), [g.b_ps[0]], [b_V[u["slot"]]])
        ksrc = v3(g.psum[0][:, 0:256], 4, 64)
        gkb = gk.unsqueeze(1).broadcast_to([128, 4, 64])
        if cs is None:
            p.tt("dve", v3(krot[par], 4, 64), ksrc, gkb, ALU.mult, [g.b_ps[0], b_gn], [b_krot[par]])
        else:
            kq, kt2 = WK1["qn"][:, 0:256], WK1["t2"][:, 0:256]
            p.tt("dve", v3(kq, 4, 64), ksrc, gkb, ALU.mult, [g.b_ps[0], b_gn], [WK1["b_qn"]])
            q4 = kq.rearrange("p (h two d) -> p h two d", h=4, two=2, d=32)
            t4 = kt2.rearrange("p (h two d) -> p h two d", h=4, two=2, d=32)
            cosb = cs[:, 0:32].unsqueeze(1).broadcast_to([128, 4, 32])
            sinb = cs[:, 32:64].unsqueeze(1).broadcast_to([128, 4, 32])
            nsinb = cs[:, 64:96].unsqueeze(1).broadcast_to([128, 4, 32])
            bcs = b_cst[ti % 2]
            p.tt("pool", t4[:, :, 0, :], q4[:, :, 1, :], nsinb, ALU.mult, [WK1["b_qn"], bcs], [WK1["b_t2"]])
            p.tt("pool", t4[:, :, 1, :], q4[:, :, 0, :], sinb, ALU.mult, [WK1["b_qn"], bcs], [WK1["b_t2"]])
            for two in range(2):
                p.tt("pool", q4[:, :, two, :], q4[:, :, two, :], cosb, ALU.mult, [WK1["b_qn"], bcs, WK1["b_t2"]],
                     [WK1["b_qn"]])
            p.tt("pool", krot[par], kq, kt2, ALU.add, [WK1["b_qn"], WK1["b_t2"]], [b_krot[par]])

    def chain_b(u):
        ti, bl, strm, N, par = u["ti"], u["bl"], u["strm"], u["N"], u["par"]
        cs = v3(cst[ti % 2], 4, 96)[:, bl, :] if strm == 0 else None
        lo = 0 if u["need_q"] else 16
        sl_ = st20[:, lo:20]
        p.act(sl_, sl_, AF.Sqrt, [b_st20, g.b_eps], [b_st20], scale=1.0, bias=g.eps64[:, 0:1])
        p.call("dve", "reciprocal", [b_st20], [b_st20], out=sl_, in_=sl_)
        p.ts("dve", rk3[:, u["slot"], :], st20[:, 16:20], 8.0, None, ALU.mult, ALU.bypass, [b_st20], [b_rk[u["slot"]]])
        if u["need_q"]:
            for hf in range(2):
                qk_part2(g, g.psum[1 + hf][:, 0:512], 8, gq, cs, st20[:, hf * 8:(hf + 1) * 8], b_st20,
                         qrot[par][:, hf * 512:(hf + 1) * 512], g.b_ps[1 + hf], b_gn, b_cst[ti % 2],
                         b_qrot[par][hf], WQ1[hf])

    def S1b_K(u):
        par, sl = u["par"], u["slot"]
        for pr in range(2):
            p.call("pe", "transpose", [b_krot[par], b_ident], [b_psKT], psKT3[:, pr, :],
                   krot[par][:, pr * 128:(pr + 1) * 128], ident)
        p.copy("dve", KT3[:, :, sl * 128:(sl + 1) * 128], psKT3, [b_psKT], [b_KT[sl]])

    def S1b_Q(u):
        par = u["par"]
        if u["need_q"]:
            for s in range(8):
                p.call("pe", "transpose", [b_qrot[par][s // 4], b_ident], [g.b_ps[7]], psOT3[:, s, :],
                       qrot[par][:, s * 128:(s + 1) * 128], ident)
            p.copy("dve", QTz[par][0][0:64, :], psOT[0:64, :], [g.b_ps[7]], [b_QT[par]])
            p.copy("dve", QTz[par][1][64:128, :], psOT[64:128, :], [g.b_ps[7]], [b_QT[par]])

    SCB = (4, 5, 3)
    pend_ot = []

    def o_transposes(bl):
        for c in range(8):
            p.call("pe", "transpose", [b_O[c // 2], b_ident], [g.b_ps[7]], psOT3[:, c, :],
                   O[:, c * 128:(c + 1) * 128], ident)
        p.copy("dve", OT3[:, :, bl * 128:(bl + 1) * 128], psOT3, [g.b_ps[7]], [b_OT])


    def S2(u, fillers):
        ti, bl, strm, N, par, blk = u["ti"], u["bl"], u["strm"], u["N"], u["par"], u["blk"]
        chunks = [(4, None), (5, None)]
        if strm == 0:
            if blk > 0:
                chunks.append(((blk - 1) % 4, 0))
            chunks.append((blk % 4, None))
            if blk < NB - 1:
                chunks.append(((blk + 1) % 4, 1))
        for gi in range(4):
            pr, half = gi // 2, gi % 2
            QT3 = v3(QTz[par][half], 8, 128)
            pts = gi % 2
            PT3 = v3(PT[pts], 5, 512)
            for ci, (sl, mk) in enumerate(chunks):
                pss = SCB[(gi * 5 + ci) % 3]
                p.mm(g.psum[pss][:, 0:512], KT3[:, pr, sl * 128:(sl + 1) * 128],
                     QT3[:, pr * 4:(pr + 1) * 4, :], True, True, [b_KT[sl], b_QT[par]], [g.b_ps[pss]])
                p.act(PT3[:, ci, :], g.psum[pss][:, 0:512], AF.Exp, [g.b_ps[pss], b_rk[sl]], [b_PT[pts][ci]],
                      scale=rk3[:, sl, gi:gi + 1])
                if mk is not None:
                    p.tt("dve", PT3[:, ci, :], PT3[:, ci, :], masks[:, mk * 512:(mk + 1) * 512], ALU.mult,
                         [b_PT[pts][ci], b_masks], [b_PT[pts][ci]])
            for f in fillers[gi]:
                f()
            po = g.psum[6][:, 0:260].rearrange("p (h d) -> p h d", h=4, d=65)
            for hl in range(4):
                for ci, (sl, mk) in enumerate(chunks):
                    p.mm(po[:, hl, :], PT3[:, ci, hl * 128:(hl + 1) * 128], VA4[:, sl, gi, :],
                         (ci == 0), (ci == len(chunks) - 1), [b_PT[pts][ci], b_V[sl]], [b_psPV])
            dn = den[gi % 2]
            p.tt("dve", dn, po[:, :, 64], esink[:, gi * 4:(gi + 1) * 4], ALU.add, [b_psPV, b_esink], [b_den[gi % 2]])
            p.call("dve", "reciprocal", [b_den[gi % 2]], [b_den[gi % 2]], out=dn, in_=dn)
            p.tt("dve", v3(O, 16, 64)[:, gi * 4:(gi + 1) * 4, :], po[:, :, 0:64],
                 dn.unsqueeze(2).broadcast_to([128, 4, 64]), ALU.mult, [b_psPV, b_den[gi % 2]], [b_O[gi]])
        if not u["last"]:
            pend_ot.append(bl)
        else:
            o_transposes(bl)
        if u["last"]:
            s_, d_, t0, N, strm = tiles[ti]
            x3 = v3(xs[ti % 2], 8, 512)[:, :, 0:N]
            b_x = b_xs[ti % 2]
            _, _, G = vecs(g, l, 0, strm)
            for c in range(8):
                py = 1 + (c % 2)
                for k in range(8):
                    p.mm(g.psum[py][:, 0:N], wo3[:, k, c * 128:(c + 1) * 128], OT3[:, k, 0:N],
                         (k == 0), (k == 7), [b_wo, b_OT], [g.b_ps[py]])
                p.stt(x3[:, c, :], g.psum[py][:, 0:N], G[:, c:c + 1], x3[:, c, :], ALU.mult, ALU.add,
                      [g.b_ps[py], b_x, g.b_ADA], [b_x])
            ob = (g.OUT if d_ is g.y else g.ST)[ti % 2]
            p.dma("sp", d_.rearrange("(c q) t -> q c t", q=128)[:, :, t0:t0 + N], x3, [b_x], [ob])

    load(0)
    load(1)
    n = len(units)
    S0_stats(units[0])
    for c in range(8):
        S0_apply(units[0], c)
    for idx in range(n + 2):
        A = units[idx] if idx < n else None
        Bu = units[idx - 1] if 0 <= idx - 1 < n else None
        C = units[idx - 2] if 0 <= idx - 2 < n else None
        nxt = units[idx + 1] if idx + 1 < n else None
        nx2 = units[idx + 2] if idx + 2 < n else None
        fl = [[], [], [], []]
        if Bu is not None:
            S1b_K(Bu)
        while pend_ot:
            o_transposes(pend_ot.pop(0))
        if A is not None:
            if A["need_q"]:
                fl[0].append(lambda A=A: proj_q(A, 0))
                fl[1].append(lambda A=A: proj_q(A, 1))
            fl[2].append(lambda A=A: proj_kv(A))
            fl[3].append(lambda A=A: chain_b(A))
        if Bu is not None:
            fl[1].append(lambda Bu=Bu: S1b_Q(Bu))
        if nxt is not None and nxt["first"]:
            for k in range(4):
                fl[k].append(lambda nxt=nxt, k=k: (S0_apply(nxt, 2 * k), S0_apply(nxt, 2 * k + 1)))
        if C is not None and C["need_q"]:
            S2(C, fl)
        else:
            for fs in fl:
                for f in fs:
                    f()
        if nx2 is not None and nx2["first"]:
            S0_stats(nx2)
        if C is not None and C["last"]:
            load(C["ti"] + 2)


def host_attn_consts(T):
    rows = T // 64
    row = np.repeat(np.arange(rows, dtype=np.float32), 64)
    col = np.tile(np.arange(64, dtype=np.float32), rows)
    inv = (10000.0 ** (-np.arange(16, dtype=np.float32) / 16)).astype(np.float32)
    ang = np.concatenate([row[:, None] * inv, col[:, None] * inv], axis=-1).astype(np.float32)
    cs = np.concatenate([np.cos(ang), np.sin(ang), -np.sin(ang)], axis=1).astype(np.float32)
    j = np.arange(128)[:, None]
    r = np.arange(128)[None, :]
    m_prev = (j >= r).astype(np.float32)
    m_next = (j <= r).astype(np.float32)
    masks = np.concatenate([np.tile(m_prev, (1, 4)), np.tile(m_next, (1, 4))], axis=1)
    return dict(attn_cs=cs, attn_masks=np.ascontiguousarray(masks), ident=np.eye(128, dtype=np.float32))


def host_attn_weights(w_qkv, w_o, q_norm, k_norm, sink):
    ns = w_qkv.shape[0]
    order = []
    for pr in range(2):
        for i in range(4):
            for half in range(2):
                hd = (2 * pr + half) * 4 + i
                order.extend(range(hd * 64, hd * 64 + 64))
    order = np.array(order)
    wq = w_qkv[:, :, :1024][:, :, order]
    w2 = np.ascontiguousarray(np.concatenate([wq, w_qkv[:, :, 1024:]], axis=2))
    return dict(attn_wqkv=w2, attn_wo=np.ascontiguousarray(w_o),
                attn_qn=np.ascontiguousarray(np.broadcast_to(q_norm[:, None, :], (ns, 128, 64))),
                attn_kn=np.ascontiguousarray(np.broadcast_to(k_norm[:, None, :], (ns, 128, 64))),
                attn_qns=np.ascontiguousarray(np.broadcast_to(np.roll(q_norm, 32, axis=1)[:, None, :], (ns, 128, 64))),
                attn_kns=np.ascontiguousarray(np.broadcast_to(np.roll(k_norm, 32, axis=1)[:, None, :], (ns, 128, 64))),
                attn_sinkB=np.ascontiguousarray(np.broadcast_to(sink[:, None, :], (ns, 128, 16))))


def complete_inputs(nc, im):
    out = dict(im)
    for k, shp in nc._decl_inputs.items():
        if k not in out:
            out[k] = np.zeros(shp, np.float32)
        assert tuple(out[k].shape) == tuple(shp), (k, out[k].shape, shp)
    return out


RET_LG = [[float(np.log1p(-2.0 ** (-5.0 - h))) for h in range(4)],
          [float(np.log1p(-2.0 ** (-5.5 - h))) for h in range(4)]]


def phase_ret_sweep(g, l, slot, dirn, src, src_c):
    nc, p = g.nc, g.p
    T, NT = g.T, g.NT
    ar = Arena(g)
    win = g.ret_win[slot].rearrange("(c q) n -> q c n", q=128)
    b_wqk, b_wv, b_wg = Buf("wqk"), Buf("wv"), Buf("wg")
    if dirn == 1:
        wqk = ar.bf16(8 * 2048)
        wv = ar.bf16(8 * 2048)
        wqk3, wv3 = v3(wqk, 8, 2048), v3(wv, 8, 2048)
        load_w_cast(g, wqk3, win[:, :, 0:2048], b_wqk, 2048, 8)
        load_w_cast(g, wv3, win[:, :, 2048:4096], b_wv, 2048, 8)
    else:
        rbuf = ar.f32(16 * 512)
        rbuf3 = v3(rbuf, 16, 512)
        b_rbuf = Buf("rbuf")
    wg = ar.bf16(8 * 2048)
    wg3 = v3(wg, 8, 2048)
    gc0 = 4096 + 2048 * dirn
    load_w_cast(g, wg3, win[:, :, gc0:gc0 + 2048], b_wg, 2048, 8)
    xs = ar.f32(8 * 512)
    b_x = Buf("rx")
    b_scr = [Buf("scr0"), Buf("scr1")]
    h = ar.bf16(8 * 512)
    b_h = Buf("rh")
    hs, b_hs = [h], [b_h]
    if dirn == 0:
        hs.append(ar.bf16(8 * 512))
        b_hs.append(Buf("rh2"))
        ptmp_r = ar.f32(512)
        b_ptmp_r = Buf("ptmp_r")
    Tst = ar.f32(4 * 2 * 512)
    Tst4 = Tst.rearrange("p (h c v) -> p h c v", h=4, c=2, v=512)
    Sbf = ar.bf16(4 * 2 * 512)
    Sbf4 = Sbf.rearrange("p (h c v) -> p h c v", h=4, c=2, v=512)
    b_T = [Buf("T%d" % i) for i in range(4)]
    b_S = [Buf("S%d" % i) for i in range(4)]
    qsets = []
    for par_ in range(2 if dirn == 0 else 1):
        qT_, kT_ = ar.bf16(8 * 512), ar.bf16(8 * 512)
        qsets.append((v3(qT_, 8, 512), v3(kT_, 8, 512),
                      [Buf("qT%d_%d" % (par_, i)) for i in range(4)], [Buf("kT%d_%d" % (par_, i)) for i in range(4)]))
    b_RST = [Buf("RST0"), Buf("RST1")]
    cs = ar.f32(2 * 512)
    b_cs = Buf("rcs")
    dec = ar.f32(8 * 128)
    b_dec = Buf("dec")
    p.dma("sp", dec, g.ret_dec[dirn], [], [b_dec])
    dec3 = v3(dec, 8, 128)
    mask = ar.bf16(128)
    ident = ar.bf16(128)
    b_mask = b_ident = Buf("rconst")
    p.dma("pool", mask, g.ret_masks[dirn], [], [b_mask])
    p.dma("pool", ident, g.ident, [], [b_ident])
    Ktok = ar.bf16(1024)
    Ktok3 = v3(Ktok, 8, 128)
    b_Ktok = Buf("Ktok")
    Vtok = ar.bf16(2048)
    b_V = [Buf("Vt%d" % i) for i in range(4)]
    b_VST = [Buf("VST%d" % i) for i in range(4)]
    sg4 = [ar.f32(512) for _ in range(4)]
    b_sg4 = [Buf("rsg%d" % i) for i in range(4)]
    tmp = ar.f32(512)
    b_tmp = Buf("rtmp")
    PT4 = [ar.bf16(128) for _ in range(4)]
    b_PT4 = [Buf("rPT%d" % i) for i in range(4)]
    b_Kt4 = [Buf("Kt%d" % i) for i in range(4)]
    z = ar.bf16(2048)
    b_z = [Buf("z%d" % i) for i in range(4)]
    zT = ar.bf16(16 * 128)
    zT3 = v3(zT, 16, 128)
    b_zT = Buf("zT")
    zbT = ar.bf16(16 * 128)
    zbT3 = v3(zbT, 16, 128)
    b_zbT = Buf("zbT")
    ss = ar.f32(4)
    b_ss = [Buf("ss%d" % i) for i in range(4)]
    for hh in range(4):
        p.memset("pool", Tst4[:, hh], 0.0, [b_T[hh]])
        p.memset("pool", Sbf4[:, hh], 0.0, [b_S[hh]])
    psT = [g.psum[3][:].bitcast(BF16), g.psum[4][:].bitcast(BF16)]
    psT3 = [v3(psT[0], 8, 128), v3(psT[1], 8, 128)]

    lat = [(src, g.ZT, t * 512, 512, 0) for t in range(NT)]
    if dirn == 1:
        lat = lat[::-1]
    tiles = [(src_c, g.ZTC, 0, CTX, 1)] + lat
    cvals = [float(np.exp(128.0 * RET_LG[dirn][hh])) for hh in range(4)]

    def load(i):
        s_, d_, t0, N, strm = tiles[i]
        x3 = v3(xs, 8, 512)[:, :, 0:N]
        p.dma("sp", x3, s_.rearrange("(c q) t -> q c t", q=128)[:, :, t0:t0 + N], [], [b_x])

    def rs_load(j):
        if j >= len(tiles):
            return
        _, _, t0j, Nj, strmj = tiles[j]
        srcR = g.RSC if strmj == 1 else g.RS
        p.dma("sp", rbuf3[:, :, 0:Nj], srcR[:, :, t0j:t0j + Nj].rearrange("r q t -> q r t"), [], [b_rbuf])

    def decay_ops(j, rows):
        _, _, t0j, Nj, strmj = tiles[j]
        nbj = Nj // 128
        q3_, k3_, bq_, bk_ = qsets[j % 2]
        for r in rows:
            qk_, hh_, dc_ = r // 8, (r // 2) % 4, r % 2
            dcb_ = dec3[:, qk_ * 4 + hh_, :].unsqueeze(1).broadcast_to([128, nbj, 128])
            dst_ = (q3_ if qk_ == 0 else k3_)[:, hh_ * 2 + dc_, 0:Nj]
            p.tt("pool", dst_.rearrange("p (b t) -> p b t", b=nbj, t=128),
                 rbuf3[:, r, 0:Nj].rearrange("p (b t) -> p b t", b=nbj, t=128), dcb_, ALU.mult,
                 [b_rbuf, b_dec], [(bq_ if qk_ == 0 else bk_)[hh_]])

    def mkctx(j):
        _, _, t0j, Nj, strmj = tiles[j]
        q3_, k3_, bq_, bk_ = qsets[j % len(qsets)]
        return dict(h3=v3(hs[j % len(hs)], 8, 512)[:, :, 0:Nj], b_h=b_hs[j % len(hs)], qT3=q3_, kT3=k3_,
                    b_qT=bq_, b_kT=bk_, t0=t0j, strm=strmj)

    load(0)
    if dirn == 0:
        rs_load(0)
    for i, (s_, d_, t0, N, strm) in enumerate(tiles):
        x3 = v3(xs, 8, 512)[:, :, 0:N]
        h3 = v3(hs[i % len(hs)], 8, 512)[:, :, 0:N]
        b_h = b_hs[i % len(hs)]
        A, B, _ = vecs(g, l, 0, strm)
        qT3, kT3, b_qT, b_kT = qsets[i % len(qsets)]
        if i == 0:
            prenorm(g, x3, N, A, B, h3, b_x, b_h)
            if dirn == 0:
                decay_ops(0, range(16))
                rs_load(1)
        if strm == 0 and dirn == 1:
            p.dma("sp", v3(cs, 2, 512), g.ret_cs[:, :, t0:t0 + 512].rearrange("s q t -> q s t"), [], [b_cs])
        nb = N // 128
        scr = [xs[:, k * 512:(k + 1) * 512] for k in range(8)]
        for qk in range(2 if dirn == 1 else 0):
            for hh in range(4):
                pb = (1, 2) if ((qk * 4 + hh) % 2 == 0) else (5, 6)
                for dc in range(2):
                    col = qk * 1024 + hh * 256 + dc * 128
                    for c in range(8):
                        p.mm(g.psum[pb[dc]][:, 0:N], wqk3[:, c, col:col + 128], h3[:, c, :],
                             (c == 0), (c == 7), [b_wqk, b_h], [g.b_ps[pb[dc]]])
                x1, x2 = g.psum[pb[0]][:, 0:N], g.psum[pb[1]][:, 0:N]
                bx1, bx2 = g.b_ps[pb[0]], g.b_ps[pb[1]]
                dst3 = (qT3 if qk == 0 else kT3)
                b_dst = (b_qT if qk == 0 else b_kT)[hh]
                dcb = dec3[:, qk * 4 + hh, :].unsqueeze(1).broadcast_to([128, nb, 128])
                sset = (qk * 4 + hh) % 2
                so = sset * 4
                bs = b_scr[sset]
                ta, tb, tc, td = (scr[so + k][:, 0:N] for k in range(4))
                if strm == 1:
                    p.copy("act", ta, x1, [bx1, b_x], [bs])
                    p.copy("act", tc, x2, [bx2, b_x], [bs])
                else:
                    cosv, sinv = cs[:, 0:N], cs[:, 512:512 + N]
                    p.tt("dve", ta, x1, cosv, ALU.mult, [bx1, b_cs, b_x], [bs])
                    p.tt("dve", tb, x2, sinv, ALU.mult, [bx2, b_cs, b_x], [bs])
                    p.tt("dve", tc, x2, cosv, ALU.mult, [bx2, b_cs, b_x], [bs])
                    p.tt("dve", td, x1, sinv, ALU.mult, [bx1, b_cs, b_x], [bs])
                    p.tt("pool", ta, ta, tb, ALU.subtract, [bs, b_x], [bs])
                    p.tt("dve", tc, tc, td, ALU.add, [bs, b_x], [bs])
                dstR = g.RSC if strm == 1 else g.RS
                r0 = (qk * 4 + hh) * 2
                p.dma("sp", dstR[r0, :, t0:t0 + N], ta, [bs, b_x], [b_RST[sset]])
                p.dma("sp", dstR[r0 + 1, :, t0:t0 + N], tc, [bs, b_x], [b_RST[sset]])
                for dc, rr in ((0, ta), (1, tc)):
                    p.tt("pool", dst3[:, hh * 2 + dc, 0:N].rearrange("p (b t) -> p b t", b=nb, t=128),
                         rr.rearrange("p (b t) -> p b t", b=nb, t=128), dcb, ALU.mult, [bs, b_x, b_dec], [b_dst])
        if i + 1 < len(tiles):
            load(i + 1)
        order = list(range(nb))
        if dirn == 1:
            order = order[::-1]
        xt = (dirn == 0)
        cx = mkctx(i)
        cxn = mkctx(i + 1) if i + 1 < len(tiles) else None

        def P1(c_, bl, hh):
            cols = slice(bl * 128, (bl + 1) * 128)
            vc = slice(hh * 512, (hh + 1) * 512)
            vsd = (g.VSC if c_["strm"] == 1 else g.VS)[c_["t0"] + bl * 128:c_["t0"] + (bl + 1) * 128, vc]
            if dirn == 1:
                for c in range(8):
                    p.mm(g.psum[1][:, 0:512], c_["h3"][:, c, cols], wv3[:, c, vc], (c == 0), (c == 7),
                         [c_["b_h"], b_wv], [g.b_ps[1]])
                p.copy("act", Vtok[:, vc], g.psum[1][:, 0:512], [g.b_ps[1]], [b_V[hh]])
                p.dma("sp", vsd, Vtok[:, vc], [b_V[hh]], [b_VST[hh]])
            else:
                p.dma("sp", Vtok[:, vc], vsd, [], [b_V[hh]])
            for dc in range(2):
                p.call("pe", "transpose", [c_["b_kT"][hh], b_ident], [g.b_ps[3]], psT3[0][:, hh * 2 + dc, :],
                       c_["kT3"][:, hh * 2 + dc, cols], ident)
            p.copy("dve", Ktok3[:, hh * 2:hh * 2 + 2, :], psT3[0][:, hh * 2:hh * 2 + 2, :], [g.b_ps[3]], [b_Kt4[hh]])
            sc_ps = g.psum[4][:, hh * 128:(hh + 1) * 128]
            for dc in range(2):
                p.mm(sc_ps, c_["kT3"][:, hh * 2 + dc, cols], c_["qT3"][:, hh * 2 + dc, cols], (dc == 0), (dc == 1),
                     [c_["b_kT"][hh], c_["b_qT"][hh]], [g.b_ps[4]])
            p.tt("dve", PT4[hh], sc_ps, mask, ALU.mult, [g.b_ps[4], b_mask], [b_PT4[hh]])

        def P1g(c_, bl, hh):
            cols = slice(bl * 128, (bl + 1) * 128)
            vc = slice(hh * 512, (hh + 1) * 512)
            for c in range(8):
                p.mm(g.psum[2][:, 0:512], c_["h3"][:, c, cols], wg3[:, c, vc], (c == 0), (c == 7),
                     [c_["b_h"], b_wg], [g.b_ps[2]])
            p.act(sg4[hh], g.psum[2][:, 0:512], AF.Silu, [g.b_ps[2]], [b_sg4[hh]])

        def P2(bl, hh):
            cols = slice(bl * 128, (bl + 1) * 128)
            vc = slice(hh * 512, (hh + 1) * 512)
            yb = (5, 0)[hh % 2]
            yps = g.psum[yb][:, 0:512]
            p.mm(yps, PT4[hh], Vtok[:, vc], True, False, [b_PT4[hh], b_V[hh]], [g.b_ps[yb]])
            for dc in range(2):
                p.mm(yps, qT3[:, hh * 2 + dc, cols], Sbf4[:, hh, dc, :], False, (dc == 1),
                     [b_qT[hh], b_S[hh]], [g.b_ps[yb]])
            for dc in range(2):
                dps = g.psum[6 + dc][:, 0:512]
                p.mm(dps, Ktok3[:, hh * 2 + dc, :], Vtok[:, vc], True, True, [b_Kt4[hh], b_V[hh]], [g.b_ps[6 + dc]])
                p.stt(Tst4[:, hh, dc, :], Tst4[:, hh, dc, :], cvals[hh], dps, ALU.mult, ALU.add,
                      [g.b_ps[6 + dc], b_T[hh]], [b_T[hh]])
                p.ts("pool", Sbf4[:, hh, dc, :], Tst4[:, hh, dc, :], cvals[hh], 1.0, ALU.mult, ALU.mult,
                     [b_T[hh]], [b_S[hh]])
            p.act(tmp, yps, AF.Square, [g.b_ps[yb]], [b_tmp])
            p.call("dve", "tensor_reduce", [b_tmp], [b_ss[hh]], out=ss[:, hh:hh + 1], in_=tmp, op=ALU.add, axis=AX.X)

        def P2b(bl, hh):
            vc = slice(hh * 512, (hh + 1) * 512)
            yb = (5, 0)[hh % 2]
            yps = g.psum[yb][:, 0:512]
            p.act(ss[:, hh:hh + 1], ss[:, hh:hh + 1], AF.Sqrt, [b_ss[hh], g.b_eps], [b_ss[hh]],
                  scale=1.0 / 512, bias=g.eps_t[:, 0:1])
            p.call("dve", "reciprocal", [b_ss[hh]], [b_ss[hh]], out=ss[:, hh:hh + 1], in_=ss[:, hh:hh + 1])
            p.stt(z[:, vc], yps, ss[:, hh:hh + 1], sg4[hh], ALU.mult, ALU.mult,
                  [g.b_ps[yb], b_ss[hh], b_sg4[hh]], [b_z[hh]])

        if not (xt and i > 0):
            for hh in range(4):
                P1(cx, order[0], hh)
                P1g(cx, order[0], hh)
        pre_k = (len(order) - 2) if xt else (len(order) - 1)
        for k, bl in enumerate(order):
            nxt = order[k + 1] if k + 1 < len(order) else None
            tok0 = t0 + bl * 128
            if dirn == 0:
                p.dma("sp", zbT3, d_.rearrange("(k q) t -> q k t", q=128)[:, :, tok0:tok0 + 128], [], [b_zbT])
            early = (k == pre_k and cxn is not None)
            if early:
                s2_, d2_, t02, N2, strm2 = tiles[i + 1]
                nx3 = v3(xs, 8, 512)[:, :, 0:N2]
                nh3 = cxn["h3"]
                A2, B2, _ = vecs(g, l, 0, strm2)
                prenorm_stats(g, nx3, N2, nh3, b_x, cxn["b_h"], ps_i=1)
            for hh in range(4):
                P2(bl, hh)
                if nxt is not None:
                    P1(cx, nxt, hh)
                elif xt and cxn is not None:
                    P1(cxn, 0, hh)
                P2b(bl, hh)
                if nxt is not None:
                    P1g(cx, nxt, hh)
                elif xt and cxn is not None:
                    P1g(cxn, 0, hh)
                if early:
                    for c2 in (2 * hh, 2 * hh + 1):
                        if xt:
                            prenorm_apply(g, nx3, N2, A2, B2, nh3, b_x, cxn["b_h"], c2, ptmp_r[:, 0:N2], b_ptmp_r, ps_i=1)
                        else:
                            prenorm_apply(g, nx3, N2, A2, B2, nh3, b_x, cxn["b_h"], c2, g.psum[2][:, 0:N2], g.b_ps[2],
                                          ps_i=1)
                    if dirn == 0:
                        decay_ops(i + 1, range(4 * hh, 4 * hh + 4))
                        if hh == 3:
                            rs_load(i + 2)
            for k2 in range(16):
                p.call("pe", "transpose", [b_z[k2 // 4], b_ident], [g.b_ps[3 + k2 // 8]], psT3[k2 // 8][:, k2 % 8, :],
                       z[:, k2 * 128:(k2 + 1) * 128], ident)
            for hf in range(2):
                if dirn == 1:
                    p.copy("dve", zT3[:, hf * 8:(hf + 1) * 8, :], psT3[hf], [g.b_ps[3 + hf]], [b_zT])
                else:
                    p.tt("dve", zT3[:, hf * 8:(hf + 1) * 8, :], psT3[hf], zbT3[:, hf * 8:(hf + 1) * 8, :], ALU.add,
                         [g.b_ps[3 + hf], b_zbT], [b_zT])
            p.dma("sp", d_.rearrange("(k q) t -> q k t", q=128)[:, :, tok0:tok0 + 128], zT3, [b_zT],
                  [g.ST[2]])


def phase_ret_out(g, l, slot, src, dst, src_c, dst_c, need_ctx_out=True):
    nc, p = g.nc, g.p
    ar = Arena(g)
    wo = ar.bf16(16 * 1024)
    wo3 = v3(wo, 16, 1024)
    b_wo = Buf("rwo")
    load_w_cast(g, wo3, g.ret_wo[slot].rearrange("(k q) n -> q k n", q=128), b_wo, 1024, 16)
    xs = [ar.f32(8 * 512) for _ in range(2)]
    zs = [ar.bf16(16 * 512) for _ in range(2)]
    b_xs = [Buf("ox0"), Buf("ox1")]
    b_zs = [Buf("oz0"), Buf("oz1")]
    tiles = [(src, dst, g.ZT, t * 512, 512, 0) for t in range(g.NT)]
    if need_ctx_out:
        tiles.append((src_c, dst_c, g.ZTC, 0, CTX, 1))

    def load(i):
        s_, d_, z_, t0, N, strm = tiles[i]
        p.dma("sp", v3(xs[i % 2], 8, 512)[:, :, 0:N], s_.rearrange("(c q) t -> q c t", q=128)[:, :, t0:t0 + N],
              [], [b_xs[i % 2]])
        p.dma("sp", v3(zs[i % 2], 16, 512)[:, :, 0:N], z_.rearrange("(k q) t -> q k t", q=128)[:, :, t0:t0 + N],
              [], [b_zs[i % 2]])

    load(0)
    for i, (s_, d_, z_, t0, N, strm) in enumerate(tiles):
        if i + 1 < len(tiles):
            load(i + 1)
        x3 = v3(xs[i % 2], 8, 512)[:, :, 0:N]
        z3 = v3(zs[i % 2], 16, 512)[:, :, 0:N]
        _, _, G = vecs(g, l, 0, strm)
        for c in range(8):
            py = 1 + (c % 2)
            for k in range(16):
                p.mm(g.psum[py][:, 0:N], wo3[:, k, c * 128:(c + 1) * 128], z3[:, k, :], (k == 0), (k == 15),
                     [b_wo, b_zs[i % 2]], [g.b_ps[py]])
            p.stt(x3[:, c, :], g.psum[py][:, 0:N], G[:, c:c + 1], x3[:, c, :], ALU.mult, ALU.add,
                  [g.b_ps[py], b_xs[i % 2], g.b_ADA], [b_xs[i % 2]])
        ob = (g.OUT if d_ is g.y else g.ST)[i % 2]
        p.dma("sp", d_.rearrange("(c q) t -> q c t", q=128)[:, :, t0:t0 + N], x3, [b_xs[i % 2]], [ob])


def host_ret_consts(T):
    inv = (10000.0 ** (-np.linspace(0.0, 1.0, 128, dtype=np.float32))).astype(np.float32)
    ang = (np.arange(T, dtype=np.float32)[:, None] * inv).astype(np.float32)
    cs = np.stack([np.cos(ang).T, np.sin(ang).T]).astype(np.float32)
    dec = np.zeros((2, 8, 128), np.float64)
    pos = np.arange(128, dtype=np.float64)
    for hh in range(4):
        lf, lb = RET_LG[0][hh], RET_LG[1][hh]
        dec[0, hh] = np.exp((pos + 1.0) * lf)
        dec[0, 4 + hh] = np.exp(-(pos + 1.0) * lf) / 16.0
        dec[1, hh] = np.exp((128.0 - pos) * lb)
        dec[1, 4 + hh] = np.exp(-(128.0 - pos) * lb) / 16.0
    decB = np.ascontiguousarray(np.broadcast_to(dec.reshape(2, 1, 1024), (2, 128, 1024))).astype(np.float32)
    j = np.arange(128)[:, None]
    i = np.arange(128)[None, :]
    masks = np.stack([(i >= j), (i <= j)]).astype(np.float32)
    return dict(ret_cs=np.ascontiguousarray(cs), ret_dec=decB, ret_masks=masks)


SEQ = 8192
BATCH = 8


def full_phases():
    def L0a(g):
        phase_attn(g, 0, 0, g.x_in, g.X1, g.ctx_in, g.XC, True)

    def L0f(g):
        phase_ffn(g, 0, g.X1, g.X1, g.XC, g.XC, True)

    def L1p(g):
        phase_pool(g, 1, 0, g.X1, g.X2, g.XC, g.XC2)

    def L1f(g):
        phase_ffn(g, 1, g.X2, g.X2, g.XC2, g.XC2, True)

    def L2b(g):
        phase_ret_sweep(g, 2, 0, 1, g.X2, g.XC2)

    def L2fw(g):
        phase_ret_sweep(g, 2, 0, 0, g.X2, g.XC2)

    def L2o(g):
        phase_ret_out(g, 2, 0, g.X2, g.X2, g.XC2, g.XC2, True)

    def L2f(g):
        phase_ffn(g, 2, g.X2, g.X2, g.XC2, g.XC2, True)

    def L3a(g):
        phase_attn(g, 3, 1, g.X2, g.X2, g.XC2, None, False)

    def L3f(g):
        phase_ffn(g, 3, g.X2, g.y, None, None, False)

    return [phase_ada, L0a, L0f, L1p, L1f, L2b, L2fw, L2o, L2f, L3a, L3f]


def host_inputs(T, x_b, c_b, ctx_b, c_ctx, ada_w, ada_b, norm_mix, norm_ffn, attn_w_qkv, attn_w_o, attn_q_norm,
                attn_k_norm, attn_sink, pool_w, pool_scale, ret_w_in, ret_w_o, ffn_w_gate, ffn_w_up, ffn_w_down,
                shared=None):
    if shared is None:
        shared = {}
        shared.update(host_attn_consts(T))
        shared.update(host_attn_weights(attn_w_qkv, attn_w_o, attn_q_norm, attn_k_norm, attn_sink))
        shared.update(host_ret_consts(T))
        shared["pool_rc"] = host_pool_rc(T)
        shared["pool_w"] = np.ascontiguousarray(pool_w)
        shared["pool_scT"] = np.ascontiguousarray(pool_scale.reshape(-1, 8, 128).transpose(0, 2, 1))
        shared["ret_w_in"] = np.ascontiguousarray(ret_w_in)
        shared["ret_w_o"] = np.ascontiguousarray(ret_w_o)
        shared["ada_w"] = np.ascontiguousarray(ada_w)
        shared["ffn_w_gate"] = np.ascontiguousarray(ffn_w_gate)
        shared["ffn_w_up"] = np.ascontiguousarray(ffn_w_up)
        shared["ffn_w_down"] = np.ascontiguousarray(ffn_w_down)
    im = dict(shared)
    im["xT"] = np.ascontiguousarray(x_b.T)
    im["ctxT"] = np.ascontiguousarray(ctx_b.T)
    im.update(host_small(c_b, c_ctx, ada_b, norm_mix, norm_ffn))
    return im, shared


_NC_CACHE = {}


def kernel(x, c, ctx, c_ctx, ada_w, ada_b, norm_mix, norm_ffn, attn_w_qkv, attn_w_o, attn_q_norm,
           attn_k_norm, attn_sink, pool_w, pool_scale, ret_w_in, ret_w_o, ffn_w_gate, ffn_w_up, ffn_w_down):
    f = lambda a: np.asarray(a, dtype=np.float32)
    x, c, ctx, c_ctx = f(x), f(c), f(ctx), f(c_ctx)
    args = [f(a) for a in (ada_w, ada_b, norm_mix, norm_ffn, attn_w_qkv, attn_w_o, attn_q_norm, attn_k_norm,
                           attn_sink, pool_w, pool_scale, ret_w_in, ret_w_o, ffn_w_gate, ffn_w_up, ffn_w_down)]
    B, T, _ = x.shape
    if T not in _NC_CACHE:
        _NC_CACHE[T] = build(T, full_phases())
    nc = _NC_CACHE[T]
    in_maps = []
    shared = None
    for b in range(B):
        im, shared = host_inputs(T, x[b], c[b], ctx[b], c_ctx, *args, shared=shared)
        in_maps.append(im)
    res = run_bass_kernel_spmd(nc, in_maps, core_ids=list(range(B)))
    out = np.empty((B, T, D), np.float32)
    for b in range(B):
        out[b] = np.asarray(res.results[b]["yT"]).T
    return out
```

```python
import contextlib
import numpy as np
import concourse.bass as bass
import concourse.mybir as mybir
from concourse.bass_utils import run_bass_kernel_spmd

F32 = mybir.dt.float32
BF16 = mybir.dt.bfloat16
AF = mybir.ActivationFunctionType
ALU = mybir.AluOpType
AX = mybir.AxisListType

D = 1024
NC8 = 8
FH = 2816
NJ = 22
CTX = 256
EPS = 1e-6
USE_LN = False
USE_POW = False
DBG = {}


class Buf:
    __slots__ = ("name", "writers", "readers", "dsem", "dcount")

    def __init__(self, name):
        self.name = name
        self.writers = {}
        self.readers = {}
        self.dsem = None
        self.dcount = 0


class Op:
    __slots__ = ("eng", "fn", "deps", "needed", "value", "is_dma", "dbuf", "multi")

    def __init__(self, eng, fn, is_dma=False, dbuf=None, multi=False):
        self.eng = eng
        self.fn = fn
        self.deps = []
        self.needed = False
        self.value = None
        self.is_dma = is_dma
        self.dbuf = dbuf
        self.multi = multi


ENGS = ("pe", "act", "dve", "pool", "sp")


class Prog:
    def __init__(self, nc):
        self.nc = nc
        self.ops = {e: [] for e in ENGS}
        self.all_ops = []
        self.dma_bufs = []
        self.barrier_deps = None
        self.barrier_pending = set()

    def _key(self, op):
        return ("d", id(op.dbuf)) if op.is_dma else op.eng

    def _record(self, op, reads, writes):
        deps = op.deps
        if self.barrier_deps is not None and op.eng in self.barrier_pending:
            deps.extend(self.barrier_deps)
            self.barrier_pending.discard(op.eng)
        for b in reads:
            for w in b.writers.values():
                deps.append(w)
        for b in writes:
            same_dma = op.is_dma and b.writers and all(
                w.is_dma and w.dbuf is op.dbuf for w in b.writers.values()) and not b.readers
            if not same_dma:
                for w in b.writers.values():
                    deps.append(w)
                for r in b.readers.values():
                    deps.append(r)
        k = self._key(op)
        for b in reads:
            b.readers[k] = op
        for b in writes:
            same_dma = op.is_dma and b.writers and all(
                w.is_dma and w.dbuf is op.dbuf for w in b.writers.values()) and not b.readers
            if not same_dma:
                b.writers = {}
                b.readers = {}
            b.writers[k] = op
        self.ops[op.eng].append(op)
        self.all_ops.append(op)
        return op

    def op(self, eng, fn, reads=(), writes=(), multi=False):
        return self._record(Op(eng, fn, multi=multi), reads, writes)

    def call(self, eng, method, reads, writes, *args, **kw):
        multi = kw.pop("_multi", False)

        def fn(e, method=method, args=args, kw=kw):
            return getattr(e, method)(*args, **kw)

        return self._record(Op(eng, fn, multi=multi), reads, writes)

    def mm(self, out, lhsT, rhs, start, stop, reads, writes):
        return self.call("pe", "matmul", reads, writes, out, lhsT, rhs, start=start, stop=stop)

    def act(self, out, in_, func, reads, writes, **kw):
        return self.call("act", "activation", reads, writes, out=out, in_=in_, func=func, **kw)

    def tt(self, eng, out, in0, in1, op, reads, writes):
        return self.call(eng, "tensor_tensor", reads, writes, out=out, in0=in0, in1=in1, op=op)

    def stt(self, out, in0, scalar, in1, op0, op1, reads, writes):
        return self.call("dve", "scalar_tensor_tensor", reads, writes, out=out, in0=in0, scalar=scalar,
                         in1=in1, op0=op0, op1=op1)

    def ts(self, eng, out, in0, s1, s2, op0, op1, reads, writes):
        return self.call(eng, "tensor_scalar", reads, writes, out=out, in0=in0, scalar1=s1, scalar2=s2,
                         op0=op0, op1=op1)

    def copy(self, eng, out, in_, reads, writes):
        if eng == "act":
            return self.call("act", "activation", reads, writes, out=out, in_=in_, func=AF.Copy)
        return self.call(eng, "tensor_copy", reads, writes, out=out, in_=in_)

    def memset(self, eng, ap, val, writes):
        return self.call(eng, "memset", [], writes, ap, val)

    def dma(self, eng, out_ap, in_ap, reads, writes, dbuf=None):
        dbuf = dbuf or writes[0]
        if dbuf.dsem is None:
            dbuf.dsem = True
            self.dma_bufs.append(dbuf)

        def fn(e, out_ap=out_ap, in_ap=in_ap):
            return e.dma_start(out=out_ap, in_=in_ap)

        return self._record(Op(eng, fn, is_dma=True, dbuf=dbuf), reads, writes)

    def barrier(self):
        deps = []
        for e in ENGS:
            if self.ops[e]:
                last = [o for o in self.ops[e] if not o.is_dma]
                if last:
                    deps.append(last[-1])
        for b in self.dma_bufs:
            pass
        for o in self.all_ops[::-1]:
            if o.is_dma and not any((d.is_dma and d.dbuf is o.dbuf) for d in deps):
                deps.append(o)
        self.barrier_deps = deps
        self.barrier_pending = set(ENGS)

    def emit(self, final_bufs):
        nc = self.nc
        for o in self.all_ops:
            for d in o.deps:
                d.needed = True
        cnt = {e: 0 for e in ENGS}
        for o in self.all_ops:
            if o.is_dma:
                o.dbuf.dcount += 16
                o.value = o.dbuf.dcount
            elif o.needed:
                cnt[o.eng] += 1
                o.value = cnt[o.eng]
        for b in self.dma_bufs:
            b.dcount = 0
        with contextlib.ExitStack() as st:
            esem = {e: st.enter_context(nc.semaphore("S_" + e)) for e in ENGS}
            for i, b in enumerate(self.dma_bufs):
                b.dsem = st.enter_context(nc.semaphore("D%d" % i))
            assert len(self.dma_bufs) < 90, len(self.dma_bufs)
            block = st.enter_context(nc.Block())

            def resolve(op, waited):
                best = {}
                for d in op.deps:
                    if d.is_dma:
                        key, sem = ("d", id(d.dbuf)), d.dbuf.dsem
                    else:
                        if d.eng == op.eng and (op.eng == "pe"):
                            continue
                        key, sem = d.eng, esem[d.eng]
                    v = d.value
                    if waited.get(key, 0) >= v:
                        continue
                    if key not in best or best[key][1] < v:
                        best[key] = (sem, v)
                for key, (sem, v) in best.items():
                    waited[key] = v
                return list(best.values())

            def run(eng_name, e):
                waited = {}
                for op in self.ops[eng_name]:
                    ws = resolve(op, waited)
                    if op.multi:
                        for (s, v) in ws:
                            e.wait_ge(s, v)
                        ins = op.fn(e)
                    else:
                        for (s, v) in ws[:-1]:
                            e.wait_ge(s, v)
                        ins = op.fn(e)
                        if ws:
                            ins._wait_ge(ws[-1][0], ws[-1][1])
                    if op.is_dma:
                        ins.then_inc(op.dbuf.dsem, 16)
                    elif op.needed:
                        ins.then_inc(esem[eng_name], 1)
                if eng_name == "sp":
                    tot = {}
                    for o in self.all_ops:
                        if o.is_dma:
                            tot[id(o.dbuf)] = (o.dbuf.dsem, o.value)
                    for (s, v) in tot.values():
                        e.wait_ge(s, v)

            @block.tensor
            def _(e):
                run("pe", e)

            @block.scalar
            def _(e):
                run("act", e)

            @block.vector
            def _(e):
                run("dve", e)

            @block.gpsimd
            def _(e):
                run("pool", e)

            @block.sync
            def _(e):
                run("sp", e)


class Ctx:
    pass


def _alloc(nc, st, name, shape, dt):
    return st.enter_context(nc.sbuf_tensor(name, list(shape), dt))


def build(T, phases, depth=4, dbg=None):
    nc = bass.Bass("TRN2", target_bir_lowering=False)
    p = Prog(nc)
    g = Ctx()
    g.nc, g.p, g.T = nc, p, T
    NT = T // 512
    g.NT = NT

    decl = {}

    def dram(name, shape, dt=F32, kind="ExternalInput"):
        if kind == "ExternalInput":
            decl[name] = tuple(shape)
        return nc.dram_tensor(name, list(shape), dt, kind=kind).ap()

    nc._decl_inputs = decl

    g.x_in = dram("xT", [D, T])
    g.ctx_in = dram("ctxT", [D, CTX])
    g.cvec = dram("cvec", [128, 16])
    g.ada_w = dram("ada_w", [depth, D, 6 * D])
    g.ada_b = dram("ada_bT", [depth, 128, 48])
    g.normT = dram("normT", [depth, 128, 16])
    g.w_gate = dram("ffn_w_gate", [depth, D, FH])
    g.w_up = dram("ffn_w_up", [depth, D, FH])
    g.w_down = dram("ffn_w_down", [depth, FH, D])
    g.attn_wqkv = dram("attn_wqkv", [2, D, 1536])
    g.attn_wo = dram("attn_wo", [2, D, D])
    g.attn_qn = dram("attn_qn", [2, 128, 64])
    g.attn_kn = dram("attn_kn", [2, 128, 64])
    g.attn_qns = dram("attn_qns", [2, 128, 64])
    g.attn_kns = dram("attn_kns", [2, 128, 64])
    g.attn_sinkB = dram("attn_sinkB", [2, 128, 16])
    g.attn_cs = dram("attn_cs", [T, 96])
    g.attn_masks = dram("attn_masks", [128, 1024])
    g.ident = dram("ident", [128, 128])
    g.ret_win = dram("ret_w_in", [1, D, 8192])
    g.ret_wo = dram("ret_w_o", [1, 2048, D])
    g.ret_cs = dram("ret_cs", [2, 128, T])
    g.ret_dec = dram("ret_dec", [2, 128, 1024])
    g.ret_masks = dram("ret_masks", [2, 128, 128])
    g.ZT = dram("ZT", [2048, T], BF16, kind="Internal")
    g.ZTC = dram("ZTC", [2048, CTX], BF16, kind="Internal")
    g.VS = dram("VS", [T, 2048], BF16, kind="Internal")
    g.VSC = dram("VSC", [CTX, 2048], BF16, kind="Internal")
    g.RS = dram("RS", [16, 128, T], F32, kind="Internal")
    g.RSC = dram("RSC", [16, 128, CTX], F32, kind="Internal")
    g.pool_w = dram("pool_w", [1, 4, 256, 256])
    g.pool_scT = dram("pool_scT", [1, 128, 8])
    g.pool_rc = dram("pool_rc", [4, 128, 4 * 512])
    g.y = dram("yT", [D, T], kind="ExternalOutput")
    g.X2 = dram("X2", [D, T], kind="Internal")
    g.XC2 = dram("XC2", [D, CTX], kind="Internal")
    g.X1 = dram("X1", [D, T], kind="Internal")
    g.XC = dram("XC", [D, CTX], kind="Internal")
    g.dbg = {}
    if dbg:
        for k, shp in dbg.items():
            g.dbg[k] = dram(k, shp, kind="ExternalOutput")

    with contextlib.ExitStack() as st:
        g.st = st
        g.ones_bf = _alloc(nc, st, "ones_bf", [128, 128], BF16)
        g.ADA = _alloc(nc, st, "ADA", [128, depth, 48, 2], F32)
        g.AB = _alloc(nc, st, "AB", [128, depth, 2, 2, 8, 2], F32)
        g.eps_t = _alloc(nc, st, "eps_t", [128, 1], F32)
        g.eps64 = _alloc(nc, st, "eps64", [128, 1], F32)
        g.b_eps = Buf("eps")
        g.b_ADA = Buf("ADA")
        g.b_ones = Buf("ones")
        g.psum = [st.enter_context(nc.psum_tensor("ps%d" % i, [128, 512], F32)) for i in range(8)]
        g.b_ps = [Buf("ps%d" % i) for i in range(8)]
        g.ARENA_W = 51000
        g.ST = [Buf("ST0"), Buf("ST1"), Buf("ST2")]
        g.OUT = [Buf("OUT0"), Buf("OUT1"), Buf("OUT2")]
        g.arena = _alloc(nc, st, "arena", [128, g.ARENA_W], F32)

        p.memset("dve", g.ones_bf[:], (1.0 / D) if USE_POW else 1.0, [g.b_ones])
        p.memset("dve", g.eps_t[:], EPS, [g.b_eps])
        p.memset("dve", g.eps64[:], 64 * EPS, [g.b_eps])
        for ph in phases:
            ph(g)
            p.barrier()
        finals = [b for b in p.dma_bufs if b.name.startswith("OUT")]
        p.emit(finals)
    return nc


class Arena:
    def __init__(self, g):
        self.g = g
        self.off = 0

    def f32(self, words, shape=None):
        a = self.g.arena[:, self.off:self.off + words]
        self.off += words
        assert self.off <= self.g.ARENA_W, self.off
        return a

    def bf16(self, elems):
        words = (elems + 1) // 2
        a = self.g.arena[:, self.off:self.off + words].bitcast(BF16)
        self.off += words
        assert self.off <= self.g.ARENA_W, self.off
        return a


def v3(ap, a, b):
    return ap.rearrange("p (a b) -> p a b", a=a, b=b)


def phase_ada(g, depth=4):
    nc, p = g.nc, g.p
    ar = Arena(g)
    cv = ar.f32(16)
    cond = ar.f32(16)
    sig = ar.f32(16)
    badab = ar.f32(depth * 48)
    nrm = ar.f32(depth * 2 * 8)
    wst = [ar.f32(8 * 512) for _ in range(3)]
    b_cv, b_cond, b_bias, b_nrm = Buf("cv"), Buf("cond"), Buf("adab"), Buf("nrm")
    b_sig = Buf("sig")
    b_w = [Buf("adaw%d" % i) for i in range(3)]
    p.dma("sp", cv, g.cvec, [], [b_cv])
    for l in range(depth):
        p.dma("sp", badab[:, l * 48:(l + 1) * 48], g.ada_b[l], [], [b_bias])
        p.dma("sp", nrm[:, (l * 2) * 8:(l * 2 + 2) * 8], g.normT[l], [], [b_nrm])
    p.act(sig, cv, AF.Sigmoid, [b_cv], [b_sig])
    p.tt("dve", cond, cv, sig, ALU.mult, [b_cv, b_sig], [b_cond])
    cond3 = v3(cond, 8, 2)
    ps = g.psum[0]
    bps = g.b_ps[0]
    piece = 0
    for l in range(depth):
        for nb in range(12):
            slot = piece % 3
            piece += 1
            w3 = v3(wst[slot], 8, 512)
            p.dma("sp", w3, g.ada_w[l].rearrange("(c q) n -> q c n", q=128)[:, :, nb * 512:(nb + 1) * 512],
                  [], [b_w[slot]])
            for j in range(4):
                n = nb * 4 + j
                for c in range(8):
                    p.mm(ps[:, 2 * n:2 * n + 2], w3[:, c, j * 128:(j + 1) * 128], cond3[:, c, :],
                         (c == 0), (c == 7), [b_w[slot], b_cond], [bps])
        bb = badab[:, l * 48:(l + 1) * 48]
        for t in range(2):
            p.tt("dve", g.ADA[:, l, :, t], v3(ps[:, 0:96], 48, 2)[:, :, t], bb, ALU.add,
                 [bps, b_bias], [g.b_ADA])
        for s in range(2):
            nr = nrm[:, (l * 2 + s) * 8:(l * 2 + s + 1) * 8]
            for t in range(2):
                sc = g.ADA[:, l, s * 24 + 8:s * 24 + 16, t]
                sh = g.ADA[:, l, s * 24 + 0:s * 24 + 8, t]
                p.stt(g.AB[:, l, s, 0, :, t], sc, 1.0, nr, ALU.add, ALU.mult, [g.b_ADA, b_nrm], [g.b_ADA])
                p.copy("dve", g.AB[:, l, s, 1, :, t], sh, [g.b_ADA], [g.b_ADA])


def vecs(g, l, s, t):
    A = g.AB[:, l, s, 0, :, t]
    B = g.AB[:, l, s, 1, :, t]
    G = g.ADA[:, l, s * 24 + 16:s * 24 + 24, t]
    return A, B, G


def load_w_cast(g, dst3, src3, buf, ncols, rows):
    p = g.p
    step = 1024
    for c0 in range(0, ncols, step):
        c1 = min(ncols, c0 + step)
        for r0 in range(0, rows, 4):
            r1 = min(rows, r0 + 4)
            p.dma("pool", dst3[:, r0:r1, c0:c1], src3[:, r0:r1, c0:c1], [], [buf])


def prenorm(g, x3, N, A, B, h3, b_x, b_h, ps_i=0, ps_t=7, sq3=None, b_sq=None, lnexp=False):
    p = g.p
    ps = g.psum[ps_i][:, 0:N]
    bps = g.b_ps[ps_i]
    pt = g.psum[ps_t][:, 0:N]
    bpt = g.b_ps[ps_t]
    if sq3 is None:
        sq3, b_sq = h3, b_h
    p.act(sq3, x3, AF.Square, [b_x], [b_sq])
    for c in range(8):
        p.mm(ps, g.ones_bf[:], sq3[:, c, :], (c == 0), (c == 7), [b_sq, g.b_ones], [bps])
    if USE_POW:
        p.ts("dve", ps, ps, EPS, -0.5, ALU.add, ALU.pow, [bps], [bps])
    else:
        p.act(ps, ps, AF.Sqrt, [bps, g.b_eps], [bps], scale=1.0 / D, bias=g.eps_t[:, 0:1])
        p.call("dve", "reciprocal", [bps], [bps], out=ps, in_=ps)
    for c in range(8):
        p.stt(pt, x3[:, c, :], A[:, c:c + 1], ps, ALU.mult, ALU.mult, [b_x, bps, g.b_ADA], [bpt])
        p.act(h3[:, c, :], pt, AF.Identity, [bpt, g.b_ADA], [b_h], scale=1.0, bias=B[:, c:c + 1])


def prenorm_stats(g, x3, N, h3, b_x, b_h, ps_i=0, rstd=None, b_rstd=None):
    p = g.p
    ps = g.psum[ps_i][:, 0:N]
    bps = g.b_ps[ps_i]
    p.act(h3, x3, AF.Square, [b_x], [b_h])
    for c in range(8):
        p.mm(ps, g.ones_bf[:], h3[:, c, :], (c == 0), (c == 7), [b_h, g.b_ones], [bps])
    p.act(ps, ps, AF.Sqrt, [bps, g.b_eps], [bps], scale=1.0 / D, bias=g.eps_t[:, 0:1])
    if rstd is None:
        p.call("dve", "reciprocal", [bps], [bps], out=ps, in_=ps)
    else:
        p.call("dve", "reciprocal", [bps], [b_rstd], out=rstd, in_=ps)


def prenorm_apply(g, x3, N, A, B, h3, b_x, b_h, c, tmp, b_tmp, ps_i=0, rstd=None, b_rstd=None):
    p = g.p
    if rstd is None:
        rstd, b_rstd = g.psum[ps_i][:, 0:N], g.b_ps[ps_i]
    p.stt(tmp, x3[:, c, :], A[:, c:c + 1], rstd, ALU.mult, ALU.mult, [b_x, b_rstd, g.b_ADA], [b_tmp])
    p.act(h3[:, c, :], tmp, AF.Identity, [b_tmp, g.b_ADA], [b_h], scale=1.0, bias=B[:, c:c + 1])


def phase_pool(g, l, slot, src, dst, src_c, dst_c):
    nc, p = g.nc, g.p
    ar = Arena(g)
    W = 528
    pw = ar.bf16(4 * 2 * 256)
    pw4 = pw.rearrange("p (g k d) -> p g k d", g=4, k=2, d=256)
    b_pw = Buf("pw")
    load_w_cast(g, pw.rearrange("p (r d) -> p r d", r=8, d=256),
                g.pool_w[slot].rearrange("g (k q) d -> q (g k) d", q=128), b_pw, 256, 8)
    rc = [ar.f32(4 * 512) for _ in range(4)]
    b_rc = Buf("rc")
    for v in range(4):
        p.dma("sp", rc[v], g.pool_rc[v], [], [b_rc])
    psc = ar.f32(8)
    GS = ar.f32(16)
    b_psc, b_GS = Buf("psc"), Buf("GS")
    p.dma("sp", psc, g.pool_scT[slot], [], [b_psc])
    for t in range(2):
        _, _, G = vecs(g, l, 0, t)
        p.tt("dve", v3(GS, 8, 2)[:, :, t], G, psc, ALU.mult, [g.b_ADA, b_psc], [b_GS])
    xs = [ar.f32(8 * W) for _ in range(2)]
    b_xs = [Buf("px0"), Buf("px1")]
    hp = ar.f32(8 * W)
    sq = ar.bf16(8 * W)
    S2, S4, S8, S16 = ar.f32(8 * W), ar.f32(6 * W), ar.f32(4 * W), ar.f32(2 * W)
    tmp = ar.f32(2 * 512)
    pooled = ar.bf16(8 * 512)
    prs = ar.f32(W)
    b_prs = Buf("prs")
    ptm = [ar.f32(W) for _ in range(2)]
    b_ptm = [Buf("ptm0"), Buf("ptm1")]
    b_hp, b_sq, b_S2, b_S4, b_S8, b_S16, b_tmp = (Buf(n) for n in ("hp", "sq", "S2", "S4", "S8", "S16", "ptmp"))
    b_pl = [Buf("pl%d" % i) for i in range(4)]

    tiles = [(src, dst, t * 512, 512, 0, g.T) for t in range(g.NT)] + [(src_c, dst_c, 0, CTX, 1, CTX)]

    def issue_load(i):
        s_, d_, t0, N, strm, Ttot = tiles[i]
        x3 = v3(xs[i % 2], 8, W)
        lo, hi = max(t0 - 8, 0), min(t0 + N + 8, Ttot)
        c0 = lo - (t0 - 8)
        if lo != t0 - 8:
            p.memset("pool", x3[:, :, 0:8], 0.0, [b_xs[i % 2]])
        if hi != t0 + N + 8:
            p.memset("pool", x3[:, :, N + 8:N + 16], 0.0, [b_xs[i % 2]])
        p.dma("sp", x3[:, :, c0:c0 + (hi - lo)], s_.rearrange("(c q) t -> q c t", q=128)[:, :, lo:hi],
              [], [b_xs[i % 2]])

    issue_load(0)
    for i, (s_, d_, t0, N, strm, Ttot) in enumerate(tiles):
        if i + 1 < len(tiles):
            issue_load(i + 1)
        WN = N + 16
        x3 = v3(xs[i % 2], 8, W)
        b_x = b_xs[i % 2]
        h3 = v3(hp, 8, W)
        sq3 = v3(sq, 8, W)
        A, B, _ = vecs(g, l, 0, strm)
        p.act(sq3[:, :, 0:WN], x3[:, :, 0:WN], AF.Square, [b_x], [b_sq])
        for gi_, (a0, a1) in enumerate(((0, min(512, WN)), (512, WN))):
            if a1 > a0:
                psr = g.psum[(0, 7)[gi_]][:, 0:a1 - a0]
                bpsr = g.b_ps[(0, 7)[gi_]]
                for c in range(8):
                    p.mm(psr, g.ones_bf[:], sq3[:, c, a0:a1], (c == 0), (c == 7), [b_sq, g.b_ones], [bpsr])
                p.act(psr, psr, AF.Sqrt, [bpsr, g.b_eps], [bpsr], scale=1.0 / D, bias=g.eps_t[:, 0:1])
                p.call("dve", "reciprocal", [bpsr], [b_prs], out=prs[:, a0:a1], in_=psr)
        for c in range(8):
            p.stt(ptm[c % 2][:, 0:WN], x3[:, c, 0:WN], A[:, c:c + 1], prs[:, 0:WN], ALU.mult, ALU.mult,
                  [b_x, b_prs, g.b_ADA], [b_ptm[c % 2]])
            p.act(h3[:, c, 0:WN], ptm[c % 2][:, 0:WN], AF.Identity, [b_ptm[c % 2], g.b_ADA], [b_hp],
                  scale=1.0, bias=B[:, c:c + 1])
        if t0 == 0:
            p.memset("pool", h3[:, :, 0:8], 0.0, [b_hp])
        if t0 + N == Ttot:
            p.memset("pool", h3[:, :, N + 8:N + 16], 0.0, [b_hp])
        s2, s4, s8, s16 = v3(S2, 8, W), v3(S4, 6, W), v3(S8, 4, W), v3(S16, 2, W)
        p.tt("dve", s2[:, :, 1:WN], h3[:, :, 0:WN - 1], h3[:, :, 1:WN], ALU.add, [b_hp], [b_S2])
        p.tt("dve", s4[:, :, 2:WN - 1], s2[:, 2:8, 1:WN - 2], s2[:, 2:8, 3:WN], ALU.add, [b_S2], [b_S4])
        p.tt("pool", s8[:, :, 4:WN - 3], s4[:, 2:6, 2:WN - 5], s4[:, 2:6, 6:WN - 1], ALU.add, [b_S4], [b_S8])
        p.tt("pool", s16[:, :, 8:WN - 8], s8[:, 2:4, 4:WN - 12], s8[:, 2:4, 12:WN - 4], ALU.add, [b_S8], [b_S16])
        if strm == 1:
            var = 3
        elif t0 == 0:
            var = 1
        elif t0 + N == Ttot:
            var = 2
        else:
            var = 0
        rcv = v3(rc[var], 4, 512)
        srcs = [(s2, 0, b_S2), (s4, 0, b_S4), (s8, 0, b_S8), (s16, 0, b_S16)]
        pl3 = v3(pooled, 8, 512)
        tmp3 = v3(tmp, 2, 512)
        for gi in range(4):
            sw, _, b_sw = srcs[gi]
            if var == 0:
                for k in range(2):
                    p.stt(pl3[:, gi * 2 + k, 0:N], sw[:, k, 8:8 + N], 1.0 / (2, 4, 8, 16)[gi], h3[:, gi * 2 + k, 8:8 + N],
                          ALU.mult, ALU.subtract, [b_sw, b_hp], [b_pl[gi]])
                continue
            for k in range(2):
                eng = "dve" if k == 0 else "pool"
                p.tt(eng, tmp3[:, k, 0:N], sw[:, k, 8:8 + N], rcv[:, gi, 0:N], ALU.mult, [b_sw, b_rc], [b_tmp])
                p.tt(eng, pl3[:, gi * 2 + k, 0:N], tmp3[:, k, 0:N], h3[:, gi * 2 + k, 8:8 + N], ALU.subtract,
                     [b_tmp, b_hp], [b_pl[gi]])
        gsv = v3(GS, 8, 2)[:, :, strm]
        for gi in range(4):
            for oc in range(2):
                c = gi * 2 + oc
                po = 5 + (c % 2)
                for k in range(2):
                    p.mm(g.psum[po][:, 0:N], pw4[:, gi, k, oc * 128:(oc + 1) * 128], pl3[:, gi * 2 + k, 0:N],
                         (k == 0), (k == 1), [b_pw, b_pl[gi]], [g.b_ps[po]])
                p.stt(x3[:, c, 8:8 + N], g.psum[po][:, 0:N], gsv[:, c:c + 1], x3[:, c, 8:8 + N],
                      ALU.mult, ALU.add, [g.b_ps[po], b_x, b_GS], [b_x])
        ob = (g.OUT if d_ is g.y else g.ST)[i % 2]
        p.dma("sp", d_.rearrange("(c q) t -> q c t", q=128)[:, :, t0:t0 + N], x3[:, :, 8:8 + N], [b_x], [ob])


def phase_ffn(g, l, src, dst, src_c, dst_c, with_ctx=True):
    nc, p = g.nc, g.p
    ar = Arena(g)
    wg = ar.bf16(8 * FH)
    wu = ar.bf16(8 * FH)
    wd = ar.bf16(NJ * D)
    wg3, wu3, wd3 = v3(wg, 8, FH), v3(wu, 8, FH), v3(wd, NJ, D)
    b_wg, b_wu, b_wd = Buf("wg"), Buf("wu"), Buf("wd")
    xs = [ar.f32(8 * 512) for _ in range(2)]
    b_xs = [Buf("x%d" % i) for i in range(2)]
    h = ar.bf16(8 * 512)
    a = ar.bf16(NJ * 512)
    sgs = [ar.f32(512), ar.f32(512)]
    b_h = Buf("h")
    b_a = [Buf("a%d" % j) for j in range(NJ)]
    b_sgs = [Buf("sg0"), Buf("sg1")]

    load_w_cast(g, wg3, g.w_gate[l].rearrange("(c q) n -> q c n", q=128), b_wg, FH, 8)
    load_w_cast(g, wu3, g.w_up[l].rearrange("(c q) n -> q c n", q=128), b_wu, FH, 8)
    load_w_cast(g, wd3, g.w_down[l].rearrange("(j q) n -> q j n", q=128), b_wd, D, NJ)

    tiles = [(src, dst, t * 512, 512, 0) for t in range(g.NT)]
    if with_ctx:
        tiles.append((src_c, dst_c, 0, CTX, 1))

    def issue_load(i):
        s_, d_, t0, N, strm = tiles[i]
        x3 = v3(xs[i % 2], 8, 512)[:, :, 0:N]
        p.dma("sp", x3, s_.rearrange("(c q) t -> q c t", q=128)[:, :, t0:t0 + N], [], [b_xs[i % 2]])

    def do_prenorm(i):
        s_, d_, t0, N, strm = tiles[i]
        A, B, _ = vecs(g, l, 1, strm)
        prenorm(g, v3(xs[i % 2], 8, 512)[:, :, 0:N], N, A, B, v3(h, 8, 512)[:, :, 0:N], b_xs[i % 2], b_h)

    issue_load(0)
    if len(tiles) > 1:
        issue_load(1)
    do_prenorm(0)
    for i, (s_, d_, t0, N, strm) in enumerate(tiles):
        x3 = v3(xs[i % 2], 8, 512)[:, :, 0:N]
        b_x = b_xs[i % 2]
        h3 = v3(h, 8, 512)[:, :, 0:N]
        a3 = v3(a, NJ, 512)[:, :, 0:N]
        _, _, G = vecs(g, l, 1, strm)
        for j in range(NJ):
            pg, pu = 1 + 2 * (j % 2), 2 + 2 * (j % 2)
            for c in range(8):
                p.mm(g.psum[pg][:, 0:N], wg3[:, c, j * 128:(j + 1) * 128], h3[:, c, :],
                     (c == 0), (c == 7), [b_wg, b_h], [g.b_ps[pg]])
            for c in range(8):
                p.mm(g.psum[pu][:, 0:N], wu3[:, c, j * 128:(j + 1) * 128], h3[:, c, :],
                     (c == 0), (c == 7), [b_wu, b_h], [g.b_ps[pu]])
            sgt = sgs[j % 2][:, 0:N]
            b_sg = b_sgs[j % 2]
            p.act(sgt, g.psum[pg][:, 0:N], AF.Silu, [g.b_ps[pg]], [b_sg])
            p.tt("dve", a3[:, j, :], g.psum[pu][:, 0:N], sgt, ALU.mult, [g.b_ps[pu], b_sg], [b_a[j]])
        for c in range(8):
            po = 5 + (c % 2)
            for j in range(NJ):
                p.mm(g.psum[po][:, 0:N], wd3[:, j, c * 128:(c + 1) * 128], a3[:, j, :],
                     (j == 0), (j == NJ - 1), [b_wd, b_a[j]], [g.b_ps[po]])
            p.stt(x3[:, c, :], g.psum[po][:, 0:N], G[:, c:c + 1], x3[:, c, :], ALU.mult, ALU.add,
                  [g.b_ps[po], b_x, g.b_ADA], [b_x])
            if c == 1 and i + 1 < len(tiles):
                do_prenorm(i + 1)
        ob = (g.OUT if d_ is g.y else g.ST)[i % 2]
        p.dma("sp", d_.rearrange("(c q) t -> q c t", q=128)[:, :, t0:t0 + N], x3, [b_x], [ob])
        if i + 2 < len(tiles):
            issue_load(i + 2)


def host_small(c_b, c_ctx, ada_b, norm_mix, norm_ffn):
    depth = ada_b.shape[0]
    cvec = np.stack([c_b.reshape(8, 128).T, c_ctx.reshape(8, 128).T], axis=2).reshape(128, 16)
    ada_bT = np.ascontiguousarray(ada_b.reshape(depth, 48, 128).transpose(0, 2, 1))
    normT = np.stack([norm_mix.reshape(depth, 8, 128).transpose(0, 2, 1),
                      norm_ffn.reshape(depth, 8, 128).transpose(0, 2, 1)], axis=2).reshape(depth, 128, 16)
    return dict(cvec=np.ascontiguousarray(cvec, dtype=np.float32), ada_bT=ada_bT.astype(np.float32),
                normT=np.ascontiguousarray(normT, dtype=np.float32))


def host_pool_rc(T):
    out = np.zeros((4, 4, 512), np.float32)
    for gi, w in enumerate((2, 4, 8, 16)):
        def rcp(Ttot, t):
            lo = np.maximum(t - w // 2, 0)
            hi = np.minimum(t + w // 2, Ttot)
            return (1.0 / (hi - lo)).astype(np.float32)
        out[0, gi, :] = 1.0 / w
        out[1, gi, :] = rcp(T, np.arange(512))
        out[2, gi, :] = rcp(T, np.arange(T - 512, T))
        out[3, gi, :256] = rcp(CTX, np.arange(256))
        out[3, gi, 256:] = 1.0 / w
    return np.ascontiguousarray(np.broadcast_to(out.reshape(4, 1, 2048), (4, 128, 2048)))


def qk_norm_rot(g, src, nh, gain, cs, is_q, out_bf, b_src, b_gain, b_cs, b_out, W):
    qk_part1(g, src, nh, W["st"][:, 0:nh], b_src, W)
    p = g.p
    st, b_st = W["st"][:, 0:nh], W["b_st"]
    if is_q:
        p.act(st, st, AF.Sqrt, [b_st, W["b_e"]], [b_st], scale=1.0, bias=W["eps64"][:, 0:1])
    else:
        p.act(st, st, AF.Sqrt, [b_st, g.b_eps], [b_st], scale=1.0 / 64, bias=g.eps_t[:, 0:1])
    p.call("dve", "reciprocal", [b_st], [b_st], out=st, in_=st)
    qk_part2(g, src, nh, gain, cs, st, b_st, out_bf, b_src, b_gain, b_cs, b_out, W)


def qk_part1(g, src, nh, st, b_src, W):
    p = g.p
    n = nh * 64
    sq = W["sq"][:, 0:n]
    p.act(sq, src, AF.Square, [b_src], [W["b_sq"]])
    p.call("dve", "tensor_reduce", [W["b_sq"]], [W["b_st"]], out=st, in_=v3(sq, nh, 64), op=ALU.add, axis=AX.X)


def qk_part2(g, src, nh, gain, cs, st, b_st, out_bf, b_src, b_gain, b_cs, b_out, W):
    p = g.p
    n = nh * 64
    qn, t2 = W["qn"][:, 0:n], W["t2"][:, 0:n]
    b_qn, b_t2 = W["b_qn"], W["b_t2"]
    p.tt("dve", v3(qn, nh, 64), v3(src, nh, 64), st.unsqueeze(2).broadcast_to([128, nh, 64]), ALU.mult,
         [b_src, b_st], [b_qn])
    if cs is None:
        p.tt("pool", v3(out_bf, nh, 64), v3(qn, nh, 64), gain.unsqueeze(1).broadcast_to([128, nh, 64]), ALU.mult,
             [b_qn, b_gain], [b_out])
        return
    p.tt("pool", v3(qn, nh, 64), v3(qn, nh, 64), gain.unsqueeze(1).broadcast_to([128, nh, 64]), ALU.mult,
         [b_qn, b_gain], [b_qn])
    q4 = qn.rearrange("p (h two d) -> p h two d", h=nh, two=2, d=32)
    t4 = t2.rearrange("p (h two d) -> p h two d", h=nh, two=2, d=32)
    cosb = cs[:, 0:32].unsqueeze(1).broadcast_to([128, nh, 32])
    sinb = cs[:, 32:64].unsqueeze(1).broadcast_to([128, nh, 32])
    nsinb = cs[:, 64:96].unsqueeze(1).broadcast_to([128, nh, 32])
    p.tt("pool", t4[:, :, 0, :], q4[:, :, 1, :], nsinb, ALU.mult, [b_qn, b_cs], [b_t2])
    p.tt("pool", t4[:, :, 1, :], q4[:, :, 0, :], sinb, ALU.mult, [b_qn, b_cs], [b_t2])
    ce = "pool" if W.get("pool_only") else "dve"
    for two in range(2):
        p.tt(ce, q4[:, :, two, :], q4[:, :, two, :], cosb, ALU.mult, [b_qn, b_cs, b_t2], [b_qn])
    p.tt("pool", out_bf, qn, t2, ALU.add, [b_qn, b_t2], [b_out])


def phase_attn_v1(g, l, slot, src, dst, src_c, dst_c, need_ctx_out):
    nc, p = g.nc, g.p
    T, NT = g.T, g.NT
    NB = T // 128
    ar = Arena(g)
    KT = ar.bf16(2 * (T + CTX))
    KT3 = v3(KT, 2, T + CTX)
    VA = ar.bf16((NB + 2) * 4 * 65)
    VA4 = VA.rearrange("p (b g d) -> p b g d", b=NB + 2, g=4, d=65)
    wqkv = ar.bf16(8 * 1536)
    wo = ar.bf16(8 * 1024)
    wqkv3, wo3 = v3(wqkv, 8, 1536), v3(wo, 8, 1024)
    b_wqkv, b_wo = Buf("wqkv"), Buf("wo")
    load_w_cast(g, wqkv3, g.attn_wqkv[slot].rearrange("(c q) n -> q c n", q=128), b_wqkv, 1536, 8)
    load_w_cast(g, wo3, g.attn_wo[slot].rearrange("(c q) n -> q c n", q=128), b_wo, 1024, 8)
    ident = ar.bf16(128)
    masks = ar.bf16(2 * 512)
    b_ident, b_masks = Buf("ident"), Buf("masks")
    p.dma("pool", ident, g.ident, [], [b_ident])
    p.dma("pool", masks, g.attn_masks, [], [b_masks])
    gq, gk = ar.f32(64), ar.f32(64)
    b_gq, b_gk = Buf("gq"), Buf("gk")
    p.dma("sp", gq, g.attn_qn[slot], [], [b_gq])
    p.dma("sp", gk, g.attn_kn[slot], [], [b_gk])
    esink = ar.f32(16)
    b_esink = Buf("esink")
    p.dma("sp", esink, g.attn_sinkB[slot], [], [b_esink])
    p.act(esink, esink, AF.Exp, [b_esink], [b_esink])
    W = dict(sq=ar.f32(512), qn=ar.f32(512), t2=ar.f32(512), st=ar.f32(8), eps64=ar.f32(1),
             b_sq=Buf("sq"), b_qn=Buf("qn"), b_t2=Buf("t2"), b_st=Buf("st"), b_e=Buf("e64"))
    p.memset("dve", W["eps64"], 64 * EPS, [W["b_e"]])
    xs = [ar.f32(8 * 512) for _ in range(2)]
    b_xs = [Buf("ax0"), Buf("ax1")]
    cst = [ar.f32(4 * 96) for _ in range(2)]
    b_cst = [Buf("cs0"), Buf("cs1")]
    h = ar.bf16(8 * 512)
    b_h = Buf("ah")
    krot = ar.bf16(256)
    qrot = ar.bf16(1024)
    b_krot, b_qrot = Buf("krot"), [Buf("qrot0"), Buf("qrot1")]
    QT = ar.bf16(8 * 128)
    QT3 = v3(QT, 8, 128)
    b_QT = Buf("QT")
    PT = [ar.bf16(5 * 512) for _ in range(2)]
    b_PT = [[Buf("PT%d_%d" % (s, c)) for c in range(5)] for s in range(2)]
    O = ar.bf16(1024)
    b_O = Buf("O")
    OT = ar.bf16(8 * 512)
    OT3 = v3(OT, 8, 512)
    b_OT = Buf("OT")
    den = ar.f32(4)
    b_den = Buf("den")
    b_KT = [Buf("KT%d" % i) for i in range(NB + 2)]
    b_V = [Buf("V%d" % i) for i in range(NB + 2)]
    p.memset("pool", VA4[:, :, :, 64:65], 1.0, b_V)
    psT = g.psum[3][:].bitcast(BF16)
    psT3 = v3(psT, 8, 128)
    b_psT = g.b_ps[3]

    lat_tiles = [(src, dst, t * 512, 512, 0) for t in range(NT)]
    ctx_tile = (src_c, dst_c, 0, CTX, 1)

    def load(i, tile):
        s_, d_, t0, N, strm = tile
        x3 = v3(xs[i % 2], 8, 512)[:, :, 0:N]
        p.dma("sp", x3, s_.rearrange("(c q) t -> q c t", q=128)[:, :, t0:t0 + N], [], [b_xs[i % 2]])
        if strm == 0:
            p.dma("sp", v3(cst[i % 2], 4, 96), g.attn_cs[t0:t0 + 512].rearrange("(b q) d -> q b d", q=128),
                  [], [b_cst[i % 2]])

    tilesA = [ctx_tile] + lat_tiles
    load(0, tilesA[0])
    for i, tile in enumerate(tilesA):
        if i + 1 < len(tilesA):
            load(i + 1, tilesA[i + 1])
        s_, d_, t0, N, strm = tile
        x3 = v3(xs[i % 2], 8, 512)[:, :, 0:N]
        h3 = v3(h, 8, 512)[:, :, 0:N]
        A, B, _ = vecs(g, l, 0, strm)
        prenorm(g, x3, N, A, B, h3, b_xs[i % 2], b_h)
        for bl in range(N // 128):
            kb = (NB + bl) if strm == 1 else (t0 // 128 + bl)
            kcol = (T + bl * 128) if strm == 1 else (t0 + bl * 128)
            pk = 1 + (bl % 2)
            for c in range(8):
                p.mm(g.psum[pk][:, 0:512], h3[:, c, bl * 128:(bl + 1) * 128], wqkv3[:, c, 1024:1536],
                     (c == 0), (c == 7), [b_h, b_wqkv], [g.b_ps[pk]])
            cs = v3(cst[i % 2], 4, 96)[:, bl, :] if strm == 0 else None
            qk_norm_rot(g, g.psum[pk][:, 0:256], 4, gk, cs, False, krot, g.b_ps[pk], b_gk, b_cst[i % 2], b_krot, W)
            p.copy("act", VA4[:, kb, :, 0:64], v3(g.psum[pk][:, 256:512], 4, 64), [g.b_ps[pk]], [b_V[kb]])
            for pr in range(2):
                p.call("pe", "transpose", [b_krot, b_ident], [b_psT], psT3[:, pr, :],
                       krot[:, pr * 128:(pr + 1) * 128], ident)
            p.copy("dve", KT3[:, :, kcol:kcol + 128], psT3[:, 0:2, :], [b_psT], [b_KT[kb]])

    tilesB = lat_tiles + ([ctx_tile] if need_ctx_out else [])
    base = len(tilesA)
    load(base, tilesB[0])
    for ii, tile in enumerate(tilesB):
        i = base + ii
        if ii + 1 < len(tilesB):
            load(i + 1, tilesB[ii + 1])
        s_, d_, t0, N, strm = tile
        x3 = v3(xs[i % 2], 8, 512)[:, :, 0:N]
        b_x = b_xs[i % 2]
        h3 = v3(h, 8, 512)[:, :, 0:N]
        A, B, G = vecs(g, l, 0, strm)
        prenorm(g, x3, N, A, B, h3, b_x, b_h)
        for bl in range(N // 128):
            blk = t0 // 128 + bl
            for hf in range(2):
                for c in range(8):
                    p.mm(g.psum[1 + hf][:, 0:512], h3[:, c, bl * 128:(bl + 1) * 128],
                         wqkv3[:, c, hf * 512:(hf + 1) * 512], (c == 0), (c == 7), [b_h, b_wqkv], [g.b_ps[1 + hf]])
            cs = v3(cst[i % 2], 4, 96)[:, bl, :] if strm == 0 else None
            for hf in range(2):
                qk_norm_rot(g, g.psum[1 + hf][:, 0:512], 8, gq, cs, True, qrot[:, hf * 512:(hf + 1) * 512],
                            g.b_ps[1 + hf], b_gq, b_cst[i % 2], b_qrot[hf], W)
            for s in range(8):
                p.call("pe", "transpose", [b_qrot[s // 4], b_ident], [b_psT], psT3[:, s, :],
                       qrot[:, s * 128:(s + 1) * 128], ident)
            p.copy("dve", QT, psT, [b_psT], [b_QT])
            chunks = [(T, NB, None), (T + 128, NB + 1, None)]
            if strm == 0:
                if blk > 0:
                    chunks.append(((blk - 1) * 128, blk - 1, 0))
                chunks.append((blk * 128, blk, None))
                if blk < NB - 1:
                    chunks.append(((blk + 1) * 128, blk + 1, 1))
            for gi in range(4):
                pr, half = gi // 2, gi % 2
                P0 = half * 64
                pts = gi % 2
                PT3 = v3(PT[pts], 5, 512)
                for ci, (kcol, vb, mk) in enumerate(chunks):
                    pss = 4 + (ci % 2)
                    p.mm(g.psum[pss][:, 0:512], KT3[P0:P0 + 64, pr, kcol:kcol + 128],
                         QT3[P0:P0 + 64, pr * 4:(pr + 1) * 4, :], True, True, [b_KT[vb], b_QT], [g.b_ps[pss]])
                    p.act(PT3[:, ci, :], g.psum[pss][:, 0:512], AF.Exp, [g.b_ps[pss]], [b_PT[pts][ci]])
                    if mk is not None:
                        p.tt("pool", PT3[:, ci, :], PT3[:, ci, :], masks[:, mk * 512:(mk + 1) * 512], ALU.mult,
                             [b_PT[pts][ci], b_masks], [b_PT[pts][ci]])
                po = g.psum[6][:, 0:260].rearrange("p (h d) -> p h d", h=4, d=65)
                for hl in range(4):
                    for ci, (kcol, vb, mk) in enumerate(chunks):
                        p.mm(po[:, hl, :], PT3[:, ci, hl * 128:(hl + 1) * 128], VA4[:, vb, gi, :],
                             (ci == 0), (ci == len(chunks) - 1), [b_PT[pts][ci], b_V[vb]], [g.b_ps[6]])
                p.tt("dve", den, po[:, :, 64], esink[:, gi * 4:(gi + 1) * 4], ALU.add, [g.b_ps[6], b_esink], [b_den])
                p.call("dve", "reciprocal", [b_den], [b_den], out=den, in_=den)
                p.tt("dve", v3(O, 16, 64)[:, gi * 4:(gi + 1) * 4, :], po[:, :, 0:64],
                     den.unsqueeze(2).broadcast_to([128, 4, 64]), ALU.mult, [g.b_ps[6], b_den], [b_O])
            for c in range(8):
                p.call("pe", "transpose", [b_O, b_ident], [b_psT], psT3[:, c, :], O[:, c * 128:(c + 1) * 128], ident)
            p.copy("dve", OT3[:, :, bl * 128:(bl + 1) * 128], psT3, [b_psT], [b_OT])
        for c in range(8):
            py = 1 + (c % 2)
            for k in range(8):
                p.mm(g.psum[py][:, 0:N], wo3[:, k, c * 128:(c + 1) * 128], OT3[:, k, 0:N],
                     (k == 0), (k == 7), [b_wo, b_OT], [g.b_ps[py]])
            p.stt(x3[:, c, :], g.psum[py][:, 0:N], G[:, c:c + 1], x3[:, c, :], ALU.mult, ALU.add,
                  [g.b_ps[py], b_x, g.b_ADA], [b_x])
        ob = (g.OUT if d_ is g.y else g.ST)[i % 2]
        p.dma("sp", d_.rearrange("(c q) t -> q c t", q=128)[:, :, t0:t0 + N], x3, [b_x], [ob])


def qk_chain(g, src, nh, is_q, rot, CG, SG, gain, out_bf, b_src, b_tab, b_out, Wk):
    p = g.p
    n = nh * 64
    sq, t1, t2, st = Wk["sq"][:, 0:n], Wk["t1"][:, 0:n], Wk["t2"][:, 0:n], Wk["st"][:, 0:nh]
    b_sq, b_t1, b_t2, b_st = Wk["b_sq"], Wk["b_t1"], Wk["b_t2"], Wk["b_st"]
    p.act(sq, src, AF.Square, [b_src], [b_sq])
    p.call("dve", "tensor_reduce", [b_sq], [b_st], out=st, in_=v3(sq, nh, 64), op=ALU.add, axis=AX.X)
    if USE_LN:
        if is_q:
            p.act(st, st, AF.Ln, [b_st, g.b_eps], [b_st], scale=1.0, bias=g.eps64[:, 0:1])
        else:
            p.act(st, st, AF.Ln, [b_st, g.b_eps], [b_st], scale=1.0 / 64, bias=g.eps_t[:, 0:1])
        p.act(st, st, AF.Exp, [b_st], [b_st], scale=-0.5)
    else:
        if is_q:
            p.act(st, st, AF.Sqrt, [b_st, g.b_eps], [b_st], scale=1.0, bias=g.eps64[:, 0:1])
        else:
            p.act(st, st, AF.Sqrt, [b_st, g.b_eps], [b_st], scale=1.0 / 64, bias=g.eps_t[:, 0:1])
        p.call("dve", "reciprocal", [b_st], [b_st], out=st, in_=st)
    s3 = v3(src, nh, 64)
    stb = st.unsqueeze(2).broadcast_to([128, nh, 64])
    if not rot:
        p.tt("dve", v3(t1, nh, 64), s3, gain.unsqueeze(1).broadcast_to([128, nh, 64]), ALU.mult,
             [b_src, b_tab], [b_t1])
        p.tt("pool", v3(out_bf, nh, 64), v3(t1, nh, 64), stb, ALU.mult, [b_t1, b_st], [b_out])
        return
    p.tt("dve", v3(t1, nh, 64), s3, CG.unsqueeze(1).broadcast_to([128, nh, 64]), ALU.mult, [b_src, b_tab], [b_t1])
    s4 = src.rearrange("p (h two d) -> p h two d", h=nh, two=2, d=32)
    t4 = t2.rearrange("p (h two d) -> p h two d", h=nh, two=2, d=32)
    p.tt("dve", t4[:, :, 0, :], s4[:, :, 1, :], SG[:, 0:32].unsqueeze(1).broadcast_to([128, nh, 32]), ALU.mult,
         [b_src, b_tab], [b_t2])
    p.tt("dve", t4[:, :, 1, :], s4[:, :, 0, :], SG[:, 32:64].unsqueeze(1).broadcast_to([128, nh, 32]), ALU.mult,
         [b_src, b_tab], [b_t2])
    p.tt("pool", t1, t1, t2, ALU.add, [b_t1, b_t2], [b_t1])
    p.tt("pool", v3(out_bf, nh, 64), v3(t1, nh, 64), stb, ALU.mult, [b_t1, b_st], [b_out])


def phase_attn(g, l, slot, src, dst, src_c, dst_c, need_ctx_out):
    nc, p = g.nc, g.p
    T, NT = g.T, g.NT
    NB = T // 128
    ar = Arena(g)
    NS = 6
    KT = ar.bf16(2 * NS * 128)
    KT3 = v3(KT, 2, NS * 128)
    VA = ar.bf16(NS * 4 * 65)
    VA4 = VA.rearrange("p (b g d) -> p b g d", b=NS, g=4, d=65)
    wqkv = ar.bf16(8 * 1536)
    wo = ar.bf16(8 * 1024)
    wqkv3, wo3 = v3(wqkv, 8, 1536), v3(wo, 8, 1024)
    b_wqkv, b_wo = Buf("wqkv"), Buf("wo")
    load_w_cast(g, wqkv3, g.attn_wqkv[slot].rearrange("(c q) n -> q c n", q=128), b_wqkv, 1536, 8)
    load_w_cast(g, wo3, g.attn_wo[slot].rearrange("(c q) n -> q c n", q=128), b_wo, 1024, 8)
    ident = ar.bf16(128)
    masks = ar.bf16(2 * 512)
    b_ident, b_masks = Buf("ident"), Buf("masks")
    p.dma("pool", ident, g.ident, [], [b_ident])
    p.dma("pool", masks, g.attn_masks, [], [b_masks])
    gq, gk, gqs, gks = ar.f32(64), ar.f32(64), ar.f32(64), ar.f32(64)
    b_gn = Buf("gains")
    p.dma("sp", gq, g.attn_qn[slot], [], [b_gn])
    p.dma("sp", gk, g.attn_kn[slot], [], [b_gn])
    p.dma("sp", gqs, g.attn_qns[slot], [], [b_gn])
    p.dma("sp", gks, g.attn_kns[slot], [], [b_gn])
    esink = ar.f32(16)
    b_esink = Buf("esink")
    p.dma("sp", esink, g.attn_sinkB[slot], [], [b_esink])
    p.act(esink, esink, AF.Exp, [b_esink], [b_esink])

    def wk(name):
        return dict(sq=ar.f32(512), t1=ar.f32(512), t2=ar.f32(512), st=ar.f32(8),
                    b_sq=Buf(name + "sq"), b_t1=Buf(name + "t1"), b_t2=Buf(name + "t2"), b_st=Buf(name + "st"))
    WQ = [wk("q0"), wk("q1")]
    WK = dict(sq=ar.f32(256), t1=ar.f32(256), t2=ar.f32(256), st=ar.f32(4),
              b_sq=Buf("ksq"), b_t1=Buf("kt1"), b_t2=Buf("kt2"), b_st=Buf("kst"))
    def wk1(name, n):
        return dict(sq=ar.f32(n), qn=ar.f32(n), t2=ar.f32(n), st=ar.f32(8), eps64=g.eps64,
                    b_sq=Buf(name + "sq"), b_qn=Buf(name + "qn"), b_t2=Buf(name + "t2"), b_st=Buf(name + "st"),
                    b_e=g.b_eps)
    WQ1 = [wk1("q0", 512), wk1("q1", 512)]
    WK1 = wk1("k", 256)
    for w_ in (WQ1[0], WQ1[1], WK1):
        w_["pool_only"] = True
    ptmp = [ar.f32(512) for _ in range(2)]
    b_ptmp = [Buf("ptmp0"), Buf("ptmp1")]
    rstd_sb = ar.f32(512)
    b_rstd_sb = Buf("rstd_sb")
    tabs = [ar.f32(4 * 64) for _ in range(2)]
    b_tabs = [Buf("tab0"), Buf("tab1")]
    xs = [ar.f32(8 * 512) for _ in range(2)]
    b_xs = [Buf("ax0"), Buf("ax1")]
    cst = [ar.f32(4 * 96) for _ in range(2)]
    b_cst = [Buf("cs0"), Buf("cs1")]
    h = ar.bf16(8 * 512)
    b_h = Buf("ah")
    h2 = ar.bf16(8 * 512)
    b_h2 = Buf("ah2")
    krot = [ar.bf16(256) for _ in range(2)]
    qrot = [ar.bf16(1024) for _ in range(2)]
    b_krot = [Buf("krot0"), Buf("krot1")]
    b_qrot = [[Buf("qrot%d_%d" % (a, b)) for b in range(2)] for a in range(2)]
    QTz = [[ar.bf16(8 * 128) for _ in range(2)] for _ in range(2)]
    b_QT = [Buf("QT0"), Buf("QT1")]
    for par_ in range(2):
        for hf_ in range(2):
            p.memset("pool", QTz[par_][hf_], 0.0, [b_QT[par_]])
    PT = [ar.bf16(5 * 512) for _ in range(2)]
    b_PT = [[Buf("PT%d_%d" % (s, c)) for c in range(5)] for s in range(2)]
    O = ar.bf16(1024)
    b_O = [Buf("O%d" % i) for i in range(4)]
    OT = ar.bf16(8 * 512)
    OT3 = v3(OT, 8, 512)
    b_OT = Buf("OT")
    den = [ar.f32(4) for _ in range(2)]
    b_den = [Buf("den0"), Buf("den1")]
    b_KT = [Buf("KT%d" % i) for i in range(NS)]
    b_V = [Buf("V%d" % i) for i in range(NS)]
    p.memset("pool", VA4[:, :, :, 64:65], 1.0, b_V)
    rk = ar.f32(NS * 4)
    rk3 = v3(rk, NS, 4)
    b_rk = [Buf("rk%d" % i) for i in range(NS)]
    psQT = g.psum[3][:].bitcast(BF16)
    psQT3 = v3(psQT, 8, 128)
    psOT = g.psum[7][:].bitcast(BF16)
    psOT3 = v3(psOT, 8, 128)
    psKT3 = v3(g.psum[6][:].bitcast(BF16)[:, 768:1024], 2, 128)
    b_psKT = g.b_ps[6]
    b_psPV = g.b_ps[6]

    tiles = [(src_c, dst_c, 0, CTX, 1)] + [(src, dst, t * 512, 512, 0) for t in range(NT)]
    units = []
    for ti, (s_, d_, t0, N, strm) in enumerate(tiles):
        nb = N // 128
        for bl in range(nb):
            units.append(dict(ti=ti, bl=bl, strm=strm, first=(bl == 0), last=(bl == nb - 1), N=N, t0=t0,
                              blk=(t0 // 128 + bl), need_q=(strm == 0 or need_ctx_out),
                              slot=(4 + bl) if strm == 1 else ((t0 // 128 + bl) % 4)))
    for k, u in enumerate(units):
        u["par"] = k % 2

    def load(ti):
        if ti >= len(tiles):
            return
        s_, d_, t0, N, strm = tiles[ti]
        x3 = v3(xs[ti % 2], 8, 512)[:, :, 0:N]
        p.dma("sp", x3, s_.rearrange("(c q) t -> q c t", q=128)[:, :, t0:t0 + N], [], [b_xs[ti % 2]])
        if strm == 0:
            p.dma("sp", v3(cst[ti % 2], 4, 96), g.attn_cs[t0:t0 + 512].rearrange("(b q) d -> q b d", q=128),
                  [], [b_cst[ti % 2]])

    hbuf = [h, h2]
    b_hb = [b_h, b_h2]

    def S0_stats(u):
        ti, strm, N = u["ti"], u["strm"], u["N"]
        x3 = v3(xs[ti % 2], 8, 512)[:, :, 0:N]
        h3 = v3(hbuf[ti % 2], 8, 512)[:, :, 0:N]
        prenorm_stats(g, x3, N, h3, b_xs[ti % 2], b_hb[ti % 2], ps_i=0, rstd=rstd_sb[:, 0:N], b_rstd=b_rstd_sb)

    def S0_apply(u, c):
        ti, strm, N = u["ti"], u["strm"], u["N"]
        x3 = v3(xs[ti % 2], 8, 512)[:, :, 0:N]
        h3 = v3(hbuf[ti % 2], 8, 512)[:, :, 0:N]
        A, B, _ = vecs(g, l, 0, strm)
        prenorm_apply(g, x3, N, A, B, h3, b_xs[ti % 2], b_hb[ti % 2], c, ptmp[c % 2][:, 0:N], b_ptmp[c % 2],
                      rstd=rstd_sb[:, 0:N], b_rstd=b_rstd_sb)

    st20 = ar.f32(20)
    b_st20 = Buf("st20")

    def proj_q(u, hf):
        ti, bl, strm, N, par = u["ti"], u["bl"], u["strm"], u["N"], u["par"]
        hb = v3(hbuf[ti % 2], 8, 512)[:, :, bl * 128:(bl + 1) * 128]
        for c in range(8):
            p.mm(g.psum[1 + hf][:, 0:512], hb[:, c, :], wqkv3[:, c, hf * 512:(hf + 1) * 512],
                 (c == 0), (c == 7), [b_hb[ti % 2], b_wqkv], [g.b_ps[1 + hf]])
        WQ1[hf]["b_st"] = b_st20
        qk_part1(g, g.psum[1 + hf][:, 0:512], 8, st20[:, hf * 8:(hf + 1) * 8], g.b_ps[1 + hf], WQ1[hf])

    def proj_kv(u):
        ti, bl, strm, N, par = u["ti"], u["bl"], u["strm"], u["N"], u["par"]
        hb = v3(hbuf[ti % 2], 8, 512)[:, :, bl * 128:(bl + 1) * 128]
        for c in range(8):
            p.mm(g.psum[0][:, 0:512], hb[:, c, :], wqkv3[:, c, 1024:1536], (c == 0), (c == 7),
                 [b_hb[ti % 2], b_wqkv], [g.b_ps[0]])
        cs = v3(cst[ti % 2], 4, 96)[:, bl, :] if strm == 0 else None
        WK1["b_st"] = b_st20
        qk_part1(g, g.psum[0][:, 0:256], 4, st20[:, 16:20], g.b_ps[0], WK1)
        p.copy("act", VA4[:, u["slot"], :, 0:64], v3(g.psum[0][:, 256:512], 4, 64), [g.b_ps[0]], [b_V[u["slot"]]])
        ksrc = v3(g.psum[0][:, 0:256], 4, 64)
        gkb = gk.unsqueeze(1).broadcast_to([128, 4, 64])
        if cs is None:
            p.tt("dve", v3(krot[par], 4, 64), ksrc, gkb, ALU.mult, [g.b_ps[0], b_gn], [b_krot[par]])
        else:
            kq, kt2 = WK1["qn"][:, 0:256], WK1["t2"][:, 0:256]
            p.tt("dve", v3(kq, 4, 64), ksrc, gkb, ALU.mult, [g.b_ps[0], b_gn], [WK1["b_qn"]])
            q4 = kq.rearrange("p (h two d) -> p h two d", h=4, two=2, d=32)
            t4 = kt2.rearrange("p (h two d) -> p h two d", h=4, two=2, d=32)
            cosb = cs[:, 0:32].unsqueeze(1).broadcast_to([128, 4, 32])
            sinb = cs[:, 32:64].unsqueeze(1).broadcast_to([128, 4, 32])
            nsinb = cs[:, 64:96].unsqueeze(1).broadcast_to([128, 4, 32])
            bcs = b_cst[ti % 2]
            p.tt("pool", t4[:, :, 0, :], q4[:, :, 1, :], nsinb, ALU.mult, [WK1["b_qn"], bcs], [WK1["b_t2"]])
            p.tt("pool", t4[:, :, 1, :], q4[:, :, 0, :], sinb, ALU.mult, [WK1["b_qn"], bcs], [WK1["b_t2"]])
            for two in range(2):
                p.tt("pool", q4[:, :, two, :], q4[:, :, two, :], cosb, ALU.mult, [WK1["b_qn"], bcs, WK1["b_t2"]],
                     [WK1["b_qn"]])
            p.tt("pool", krot[par], kq, kt2, ALU.add, [WK1["b_qn"], WK1["b_t2"]], [b_krot[par]])

    def chain_b(u):
        ti, bl, strm, N, par = u["ti"], u["bl"], u["strm"], u["N"], u["par"]
        cs = v3(cst[ti % 2], 4, 96)[:, bl, :] if strm == 0 else None
        lo = 0 if u["need_q"] else 16
        sl_ = st20[:, lo:20]
        p.act(sl_, sl_, AF.Sqrt, [b_st20, g.b_eps], [b_st20], scale=1.0, bias=g.eps64[:, 0:1])
        p.call("dve", "reciprocal", [b_st20], [b_st20], out=sl_, in_=sl_)
        p.ts("dve", rk3[:, u["slot"], :], st20[:, 16:20], 8.0, None, ALU.mult, ALU.bypass, [b_st20], [b_rk[u["slot"]]])
        if u["need_q"]:
            for hf in range(2):
                qk_part2(g, g.psum[1 + hf][:, 0:512], 8, gq, cs, st20[:, hf * 8:(hf + 1) * 8], b_st20,
                         qrot[par][:, hf * 512:(hf + 1) * 512], g.b_ps[1 + hf], b_gn, b_cst[ti % 2],
                         b_qrot[par][hf], WQ1[hf])

    def S1b_K(u):
        par, sl = u["par"], u["slot"]
        for pr in range(2):
            p.call("pe", "transpose", [b_krot[par], b_ident], [b_psKT], psKT3[:, pr, :],
                   krot[par][:, pr * 128:(pr + 1) * 128], ident)
        p.copy("dve", KT3[:, :, sl * 128:(sl + 1) * 128], psKT3, [b_psKT], [b_KT[sl]])

    def S1b_Q(u):
        par = u["par"]
        if u["need_q"]:
            for s in range(8):
                p.call("pe", "transpose", [b_qrot[par][s // 4], b_ident], [g.b_ps[7]], psOT3[:, s, :],
                       qrot[par][:, s * 128:(s + 1) * 128], ident)
            p.copy("dve", QTz[par][0][0:64, :], psOT[0:64, :], [g.b_ps[7]], [b_QT[par]])
            p.copy("dve", QTz[par][1][64:128, :], psOT[64:128, :], [g.b_ps[7]], [b_QT[par]])

    SCB = (4, 5, 3)
    pend_ot = []

    def o_transposes(bl):
        for c in range(8):
            p.call("pe", "transpose", [b_O[c // 2], b_ident], [g.b_ps[7]], psOT3[:, c, :],
                   O[:, c * 128:(c + 1) * 128], ident)
        p.copy("dve", OT3[:, :, bl * 128:(bl + 1) * 128], psOT3, [g.b_ps[7]], [b_OT])


    def S2(u, fillers):
        ti, bl, strm, N, par, blk = u["ti"], u["bl"], u["strm"], u["N"], u["par"], u["blk"]
        chunks = [(4, None), (5, None)]
        if strm == 0:
            if blk > 0:
                chunks.append(((blk - 1) % 4, 0))
            chunks.append((blk % 4, None))
            if blk < NB - 1:
                chunks.append(((blk + 1) % 4, 1))
        for gi in range(4):
            pr, half = gi // 2, gi % 2
            QT3 = v3(QTz[par][half], 8, 128)
            pts = gi % 2
            PT3 = v3(PT[pts], 5, 512)
            for ci, (sl, mk) in enumerate(chunks):
                pss = SCB[(gi * 5 + ci) % 3]
                p.mm(g.psum[pss][:, 0:512], KT3[:, pr, sl * 128:(sl + 1) * 128],
                     QT3[:, pr * 4:(pr + 1) * 4, :], True, True, [b_KT[sl], b_QT[par]], [g.b_ps[pss]])
                p.act(PT3[:, ci, :], g.psum[pss][:, 0:512], AF.Exp, [g.b_ps[pss], b_rk[sl]], [b_PT[pts][ci]],
                      scale=rk3[:, sl, gi:gi + 1])
                if mk is not None:
                    p.tt("dve", PT3[:, ci, :], PT3[:, ci, :], masks[:, mk * 512:(mk + 1) * 512], ALU.mult,
                         [b_PT[pts][ci], b_masks], [b_PT[pts][ci]])
            for f in fillers[gi]:
                f()
            po = g.psum[6][:, 0:260].rearrange("p (h d) -> p h d", h=4, d=65)
            for hl in range(4):
                for ci, (sl, mk) in enumerate(chunks):
                    p.mm(po[:, hl, :], PT3[:, ci, hl * 128:(hl + 1) * 128], VA4[:, sl, gi, :],
                         (ci == 0), (ci == len(chunks) - 1), [b_PT[pts][ci], b_V[sl]], [b_psPV])
            dn = den[gi % 2]
            p.tt("dve", dn, po[:, :, 64], esink[:, gi * 4:(gi + 1) * 4], ALU.add, [b_psPV, b_esink], [b_den[gi % 2]])
            p.call("dve", "reciprocal", [b_den[gi % 2]], [b_den[gi % 2]], out=dn, in_=dn)
            p.tt("dve", v3(O, 16, 64)[:, gi * 4:(gi + 1) * 4, :], po[:, :, 0:64],
                 dn.unsqueeze(2).broadcast_to([128, 4, 64]), ALU.mult, [b_psPV, b_den[gi % 2]], [b_O[gi]])
        if not u["last"]:
            pend_ot.append(bl)
        else:
            o_transposes(bl)
        if u["last"]:
            s_, d_, t0, N, strm = tiles[ti]
            x3 = v3(xs[ti % 2], 8, 512)[:, :, 0:N]
            b_x = b_xs[ti % 2]
            _, _, G = vecs(g, l, 0, strm)
            for c in range(8):
                py = 1 + (c % 2)
                for k in range(8):
                    p.mm(g.psum[py][:, 0:N], wo3[:, k, c * 128:(c + 1) * 128], OT3[:, k, 0:N],
                         (k == 0), (k == 7), [b_wo, b_OT], [g.b_ps[py]])
                p.stt(x3[:, c, :], g.psum[py][:, 0:N], G[:, c:c + 1], x3[:, c, :], ALU.mult, ALU.add,
                      [g.b_ps[py], b_x, g.b_ADA], [b_x])
            ob = (g.OUT if d_ is g.y else g.ST)[ti % 2]
            p.dma("sp", d_.rearrange("(c q) t -> q c t", q=128)[:, :, t0:t0 + N], x3, [b_x], [ob])

    load(0)
    load(1)
    n = len(units)
    S0_stats(units[0])
    for c in range(8):
        S0_apply(units[0], c)
    for idx in range(n + 2):
        A = units[idx] if idx < n else None
        Bu = units[idx - 1] if 0 <= idx - 1 < n else None
        C = units[idx - 2] if 0 <= idx - 2 < n else None
        nxt = units[idx + 1] if idx + 1 < n else None
        nx2 = units[idx + 2] if idx + 2 < n else None
        fl = [[], [], [], []]
        if Bu is not None:
            S1b_K(Bu)
        while pend_ot:
            o_transposes(pend_ot.pop(0))
        if A is not None:
            if A["need_q"]:
                fl[0].append(lambda A=A: proj_q(A, 0))
                fl[1].append(lambda A=A: proj_q(A, 1))
            fl[2].append(lambda A=A: proj_kv(A))
            fl[3].append(lambda A=A: chain_b(A))
        if Bu is not None:
            fl[1].append(lambda Bu=Bu: S1b_Q(Bu))
        if nxt is not None and nxt["first"]:
            for k in range(4):
                fl[k].append(lambda nxt=nxt, k=k: (S0_apply(nxt, 2 * k), S0_apply(nxt, 2 * k + 1)))
        if C is not None and C["need_q"]:
            S2(C, fl)
        else:
            for fs in fl:
                for f in fs:
                    f()
        if nx2 is not None and nx2["first"]:
            S0_stats(nx2)
        if C is not None and C["last"]:
            load(C["ti"] + 2)


def host_attn_consts(T):
    rows = T // 64
    row = np.repeat(np.arange(rows, dtype=np.float32), 64)
    col = np.tile(np.arange(64, dtype=np.float32), rows)
    inv = (10000.0 ** (-np.arange(16, dtype=np.float32) / 16)).astype(np.float32)
    ang = np.concatenate([row[:, None] * inv, col[:, None] * inv], axis=-1).astype(np.float32)
    cs = np.concatenate([np.cos(ang), np.sin(ang), -np.sin(ang)], axis=1).astype(np.float32)
    j = np.arange(128)[:, None]
    r = np.arange(128)[None, :]
    m_prev = (j >= r).astype(np.float32)
    m_next = (j <= r).astype(np.float32)
    masks = np.concatenate([np.tile(m_prev, (1, 4)), np.tile(m_next, (1, 4))], axis=1)
    return dict(attn_cs=cs, attn_masks=np.ascontiguousarray(masks), ident=np.eye(128, dtype=np.float32))


def host_attn_weights(w_qkv, w_o, q_norm, k_norm, sink):
    ns = w_qkv.shape[0]
    order = []
    for pr in range(2):
        for i in range(4):
            for half in range(2):
                hd = (2 * pr + half) * 4 + i
                order.extend(range(hd * 64, hd * 64 + 64))
    order = np.array(order)
    wq = w_qkv[:, :, :1024][:, :, order]
    w2 = np.ascontiguousarray(np.concatenate([wq, w_qkv[:, :, 1024:]], axis=2))
    return dict(attn_wqkv=w2, attn_wo=np.ascontiguousarray(w_o),
                attn_qn=np.ascontiguousarray(np.broadcast_to(q_norm[:, None, :], (ns, 128, 64))),
                attn_kn=np.ascontiguousarray(np.broadcast_to(k_norm[:, None, :], (ns, 128, 64))),
                attn_qns=np.ascontiguousarray(np.broadcast_to(np.roll(q_norm, 32, axis=1)[:, None, :], (ns, 128, 64))),
                attn_kns=np.ascontiguousarray(np.broadcast_to(np.roll(k_norm, 32, axis=1)[:, None, :], (ns, 128, 64))),
                attn_sinkB=np.ascontiguousarray(np.broadcast_to(sink[:, None, :], (ns, 128, 16))))


def complete_inputs(nc, im):
    out = dict(im)
    for k, shp in nc._decl_inputs.items():
        if k not in out:
            out[k] = np.zeros(shp, np.float32)
        assert tuple(out[k].shape) == tuple(shp), (k, out[k].shape, shp)
    return out


RET_LG = [[float(np.log1p(-2.0 ** (-5.0 - h))) for h in range(4)],
          [float(np.log1p(-2.0 ** (-5.5 - h))) for h in range(4)]]


def phase_ret_sweep(g, l, slot, dirn, src, src_c):
    nc, p = g.nc, g.p
    T, NT = g.T, g.NT
    ar = Arena(g)
    win = g.ret_win[slot].rearrange("(c q) n -> q c n", q=128)
    b_wqk, b_wv, b_wg = Buf("wqk"), Buf("wv"), Buf("wg")
    if dirn == 1:
        wqk = ar.bf16(8 * 2048)
        wv = ar.bf16(8 * 2048)
        wqk3, wv3 = v3(wqk, 8, 2048), v3(wv, 8, 2048)
        load_w_cast(g, wqk3, win[:, :, 0:2048], b_wqk, 2048, 8)
        load_w_cast(g, wv3, win[:, :, 2048:4096], b_wv, 2048, 8)
    else:
        rbuf = ar.f32(16 * 512)
        rbuf3 = v3(rbuf, 16, 512)
        b_rbuf = Buf("rbuf")
    wg = ar.bf16(8 * 2048)
    wg3 = v3(wg, 8, 2048)
    gc0 = 4096 + 2048 * dirn
    load_w_cast(g, wg3, win[:, :, gc0:gc0 + 2048], b_wg, 2048, 8)
    xs = ar.f32(8 * 512)
    b_x = Buf("rx")
    b_scr = [Buf("scr0"), Buf("scr1")]
    h = ar.bf16(8 * 512)
    b_h = Buf("rh")
    hs, b_hs = [h], [b_h]
    if dirn == 0:
        hs.append(ar.bf16(8 * 512))
        b_hs.append(Buf("rh2"))
        ptmp_r = ar.f32(512)
        b_ptmp_r = Buf("ptmp_r")
    Tst = ar.f32(4 * 2 * 512)
    Tst4 = Tst.rearrange("p (h c v) -> p h c v", h=4, c=2, v=512)
    Sbf = ar.bf16(4 * 2 * 512)
    Sbf4 = Sbf.rearrange("p (h c v) -> p h c v", h=4, c=2, v=512)
    b_T = [Buf("T%d" % i) for i in range(4)]
    b_S = [Buf("S%d" % i) for i in range(4)]
    qsets = []
    for par_ in range(2 if dirn == 0 else 1):
        qT_, kT_ = ar.bf16(8 * 512), ar.bf16(8 * 512)
        qsets.append((v3(qT_, 8, 512), v3(kT_, 8, 512),
                      [Buf("qT%d_%d" % (par_, i)) for i in range(4)], [Buf("kT%d_%d" % (par_, i)) for i in range(4)]))
    b_RST = [Buf("RST0"), Buf("RST1")]
    cs = ar.f32(2 * 512)
    b_cs = Buf("rcs")
    dec = ar.f32(8 * 128)
    b_dec = Buf("dec")
    p.dma("sp", dec, g.ret_dec[dirn], [], [b_dec])
    dec3 = v3(dec, 8, 128)
    mask = ar.bf16(128)
    ident = ar.bf16(128)
    b_mask = b_ident = Buf("rconst")
    p.dma("pool", mask, g.ret_masks[dirn], [], [b_mask])
    p.dma("pool", ident, g.ident, [], [b_ident])
    Ktok = ar.bf16(1024)
    Ktok3 = v3(Ktok, 8, 128)
    b_Ktok = Buf("Ktok")
    Vtok = ar.bf16(2048)
    b_V = [Buf("Vt%d" % i) for i in range(4)]
    b_VST = [Buf("VST%d" % i) for i in range(4)]
    sg4 = [ar.f32(512) for _ in range(4)]
    b_sg4 = [Buf("rsg%d" % i) for i in range(4)]
    tmp = ar.f32(512)
    b_tmp = Buf("rtmp")
    PT4 = [ar.bf16(128) for _ in range(4)]
    b_PT4 = [Buf("rPT%d" % i) for i in range(4)]
    b_Kt4 = [Buf("Kt%d" % i) for i in range(4)]
    z = ar.bf16(2048)
    b_z = [Buf("z%d" % i) for i in range(4)]
    zT = ar.bf16(16 * 128)
    zT3 = v3(zT, 16, 128)
    b_zT = Buf("zT")
    zbT = ar.bf16(16 * 128)
    zbT3 = v3(zbT, 16, 128)
    b_zbT = Buf("zbT")
    ss = ar.f32(4)
    b_ss = [Buf("ss%d" % i) for i in range(4)]
    for hh in range(4):
        p.memset("pool", Tst4[:, hh], 0.0, [b_T[hh]])
        p.memset("pool", Sbf4[:, hh], 0.0, [b_S[hh]])
    psT = [g.psum[3][:].bitcast(BF16), g.psum[4][:].bitcast(BF16)]
    psT3 = [v3(psT[0], 8, 128), v3(psT[1], 8, 128)]

    lat = [(src, g.ZT, t * 512, 512, 0) for t in range(NT)]
    if dirn == 1:
        lat = lat[::-1]
    tiles = [(src_c, g.ZTC, 0, CTX, 1)] + lat
    cvals = [float(np.exp(128.0 * RET_LG[dirn][hh])) for hh in range(4)]

    def load(i):
        s_, d_, t0, N, strm = tiles[i]
        x3 = v3(xs, 8, 512)[:, :, 0:N]
        p.dma("sp", x3, s_.rearrange("(c q) t -> q c t", q=128)[:, :, t0:t0 + N], [], [b_x])

    def rs_load(j):
        if j >= len(tiles):
            return
        _, _, t0j, Nj, strmj = tiles[j]
        srcR = g.RSC if strmj == 1 else g.RS
        p.dma("sp", rbuf3[:, :, 0:Nj], srcR[:, :, t0j:t0j + Nj].rearrange("r q t -> q r t"), [], [b_rbuf])

    def decay_ops(j, rows):
        _, _, t0j, Nj, strmj = tiles[j]
        nbj = Nj // 128
        q3_, k3_, bq_, bk_ = qsets[j % 2]
        for r in rows:
            qk_, hh_, dc_ = r // 8, (r // 2) % 4, r % 2
            dcb_ = dec3[:, qk_ * 4 + hh_, :].unsqueeze(1).broadcast_to([128, nbj, 128])
            dst_ = (q3_ if qk_ == 0 else k3_)[:, hh_ * 2 + dc_, 0:Nj]
            p.tt("pool", dst_.rearrange("p (b t) -> p b t", b=nbj, t=128),
                 rbuf3[:, r, 0:Nj].rearrange("p (b t) -> p b t", b=nbj, t=128), dcb_, ALU.mult,
                 [b_rbuf, b_dec], [(bq_ if qk_ == 0 else bk_)[hh_]])

    def mkctx(j):
        _, _, t0j, Nj, strmj = tiles[j]
        q3_, k3_, bq_, bk_ = qsets[j % len(qsets)]
        return dict(h3=v3(hs[j % len(hs)], 8, 512)[:, :, 0:Nj], b_h=b_hs[j % len(hs)], qT3=q3_, kT3=k3_,
                    b_qT=bq_, b_kT=bk_, t0=t0j, strm=strmj)

    load(0)
    if dirn == 0:
        rs_load(0)
    for i, (s_, d_, t0, N, strm) in enumerate(tiles):
        x3 = v3(xs, 8, 512)[:, :, 0:N]
        h3 = v3(hs[i % len(hs)], 8, 512)[:, :, 0:N]
        b_h = b_hs[i % len(hs)]
        A, B, _ = vecs(g, l, 0, strm)
        qT3, kT3, b_qT, b_kT = qsets[i % len(qsets)]
        if i == 0:
            prenorm(g, x3, N, A, B, h3, b_x, b_h)
            if dirn == 0:
                decay_ops(0, range(16))
                rs_load(1)
        if strm == 0 and dirn == 1:
            p.dma("sp", v3(cs, 2, 512), g.ret_cs[:, :, t0:t0 + 512].rearrange("s q t -> q s t"), [], [b_cs])
        nb = N // 128
        scr = [xs[:, k * 512:(k + 1) * 512] for k in range(8)]
        for qk in range(2 if dirn == 1 else 0):
            for hh in range(4):
                pb = (1, 2) if ((qk * 4 + hh) % 2 == 0) else (5, 6)
                for dc in range(2):
                    col = qk * 1024 + hh * 256 + dc * 128
                    for c in range(8):
                        p.mm(g.psum[pb[dc]][:, 0:N], wqk3[:, c, col:col + 128], h3[:, c, :],
                             (c == 0), (c == 7), [b_wqk, b_h], [g.b_ps[pb[dc]]])
                x1, x2 = g.psum[pb[0]][:, 0:N], g.psum[pb[1]][:, 0:N]
                bx1, bx2 = g.b_ps[pb[0]], g.b_ps[pb[1]]
                dst3 = (qT3 if qk == 0 else kT3)
                b_dst = (b_qT if qk == 0 else b_kT)[hh]
                dcb = dec3[:, qk * 4 + hh, :].unsqueeze(1).broadcast_to([128, nb, 128])
                sset = (qk * 4 + hh) % 2
                so = sset * 4
                bs = b_scr[sset]
                ta, tb, tc, td = (scr[so + k][:, 0:N] for k in range(4))
                if strm == 1:
                    p.copy("act", ta, x1, [bx1, b_x], [bs])
                    p.copy("act", tc, x2, [bx2, b_x], [bs])
                else:
                    cosv, sinv = cs[:, 0:N], cs[:, 512:512 + N]
                    p.tt("dve", ta, x1, cosv, ALU.mult, [bx1, b_cs, b_x], [bs])
                    p.tt("dve", tb, x2, sinv, ALU.mult, [bx2, b_cs, b_x], [bs])
                    p.tt("dve", tc, x2, cosv, ALU.mult, [bx2, b_cs, b_x], [bs])
                    p.tt("dve", td, x1, sinv, ALU.mult, [bx1, b_cs, b_x], [bs])
                    p.tt("pool", ta, ta, tb, ALU.subtract, [bs, b_x], [bs])
                    p.tt("dve", tc, tc, td, ALU.add, [bs, b_x], [bs])
                dstR = g.RSC if strm == 1 else g.RS
                r0 = (qk * 4 + hh) * 2
                p.dma("sp", dstR[r0, :, t0:t0 + N], ta, [bs, b_x], [b_RST[sset]])
                p.dma("sp", dstR[r0 + 1, :, t0:t0 + N], tc, [bs, b_x], [b_RST[sset]])
                for dc, rr in ((0, ta), (1, tc)):
                    p.tt("pool", dst3[:, hh * 2 + dc, 0:N].rearrange("p (b t) -> p b t", b=nb, t=128),
                         rr.rearrange("p (b t) -> p b t", b=nb, t=128), dcb, ALU.mult, [bs, b_x, b_dec], [b_dst])
        if i + 1 < len(tiles):
            load(i + 1)
        order = list(range(nb))
        if dirn == 1:
            order = order[::-1]
        xt = (dirn == 0)
        cx = mkctx(i)
        cxn = mkctx(i + 1) if i + 1 < len(tiles) else None

        def P1(c_, bl, hh):
            cols = slice(bl * 128, (bl + 1) * 128)
            vc = slice(hh * 512, (hh + 1) * 512)
            vsd = (g.VSC if c_["strm"] == 1 else g.VS)[c_["t0"] + bl * 128:c_["t0"] + (bl + 1) * 128, vc]
            if dirn == 1:
                for c in range(8):
                    p.mm(g.psum[1][:, 0:512], c_["h3"][:, c, cols], wv3[:, c, vc], (c == 0), (c == 7),
                         [c_["b_h"], b_wv], [g.b_ps[1]])
                p.copy("act", Vtok[:, vc], g.psum[1][:, 0:512], [g.b_ps[1]], [b_V[hh]])
                p.dma("sp", vsd, Vtok[:, vc], [b_V[hh]], [b_VST[hh]])
            else:
                p.dma("sp", Vtok[:, vc], vsd, [], [b_V[hh]])
            for dc in range(2):
                p.call("pe", "transpose", [c_["b_kT"][hh], b_ident], [g.b_ps[3]], psT3[0][:, hh * 2 + dc, :],
                       c_["kT3"][:, hh * 2 + dc, cols], ident)
            p.copy("dve", Ktok3[:, hh * 2:hh * 2 + 2, :], psT3[0][:, hh * 2:hh * 2 + 2, :], [g.b_ps[3]], [b_Kt4[hh]])
            sc_ps = g.psum[4][:, hh * 128:(hh + 1) * 128]
            for dc in range(2):
                p.mm(sc_ps, c_["kT3"][:, hh * 2 + dc, cols], c_["qT3"][:, hh * 2 + dc, cols], (dc == 0), (dc == 1),
                     [c_["b_kT"][hh], c_["b_qT"][hh]], [g.b_ps[4]])
            p.tt("dve", PT4[hh], sc_ps, mask, ALU.mult, [g.b_ps[4], b_mask], [b_PT4[hh]])

        def P1g(c_, bl, hh):
            cols = slice(bl * 128, (bl + 1) * 128)
            vc = slice(hh * 512, (hh + 1) * 512)
            for c in range(8):
                p.mm(g.psum[2][:, 0:512], c_["h3"][:, c, cols], wg3[:, c, vc], (c == 0), (c == 7),
                     [c_["b_h"], b_wg], [g.b_ps[2]])
            p.act(sg4[hh], g.psum[2][:, 0:512], AF.Silu, [g.b_ps[2]], [b_sg4[hh]])

        def P2(bl, hh):
            cols = slice(bl * 128, (bl + 1) * 128)
            vc = slice(hh * 512, (hh + 1) * 512)
            yb = (5, 0)[hh % 2]
            yps = g.psum[yb][:, 0:512]
            p.mm(yps, PT4[hh], Vtok[:, vc], True, False, [b_PT4[hh], b_V[hh]], [g.b_ps[yb]])
            for dc in range(2):
                p.mm(yps, qT3[:, hh * 2 + dc, cols], Sbf4[:, hh, dc, :], False, (dc == 1),
                     [b_qT[hh], b_S[hh]], [g.b_ps[yb]])
            for dc in range(2):
                dps = g.psum[6 + dc][:, 0:512]
                p.mm(dps, Ktok3[:, hh * 2 + dc, :], Vtok[:, vc], True, True, [b_Kt4[hh], b_V[hh]], [g.b_ps[6 + dc]])
                p.stt(Tst4[:, hh, dc, :], Tst4[:, hh, dc, :], cvals[hh], dps, ALU.mult, ALU.add,
                      [g.b_ps[6 + dc], b_T[hh]], [b_T[hh]])
                p.ts("pool", Sbf4[:, hh, dc, :], Tst4[:, hh, dc, :], cvals[hh], 1.0, ALU.mult, ALU.mult,
                     [b_T[hh]], [b_S[hh]])
            p.act(tmp, yps, AF.Square, [g.b_ps[yb]], [b_tmp])
            p.call("dve", "tensor_reduce", [b_tmp], [b_ss[hh]], out=ss[:, hh:hh + 1], in_=tmp, op=ALU.add, axis=AX.X)

        def P2b(bl, hh):
            vc = slice(hh * 512, (hh + 1) * 512)
            yb = (5, 0)[hh % 2]
            yps = g.psum[yb][:, 0:512]
            p.act(ss[:, hh:hh + 1], ss[:, hh:hh + 1], AF.Sqrt, [b_ss[hh], g.b_eps], [b_ss[hh]],
                  scale=1.0 / 512, bias=g.eps_t[:, 0:1])
            p.call("dve", "reciprocal", [b_ss[hh]], [b_ss[hh]], out=ss[:, hh:hh + 1], in_=ss[:, hh:hh + 1])
            p.stt(z[:, vc], yps, ss[:, hh:hh + 1], sg4[hh], ALU.mult, ALU.mult,
                  [g.b_ps[yb], b_ss[hh], b_sg4[hh]], [b_z[hh]])

        if not (xt and i > 0):
            for hh in range(4):
                P1(cx, order[0], hh)
                P1g(cx, order[0], hh)
        pre_k = (len(order) - 2) if xt else (len(order) - 1)
        for k, bl in enumerate(order):
            nxt = order[k + 1] if k + 1 < len(order) else None
            tok0 = t0 + bl * 128
            if dirn == 0:
                p.dma("sp", zbT3, d_.rearrange("(k q) t -> q k t", q=128)[:, :, tok0:tok0 + 128], [], [b_zbT])
            early = (k == pre_k and cxn is not None)
            if early:
                s2_, d2_, t02, N2, strm2 = tiles[i + 1]
                nx3 = v3(xs, 8, 512)[:, :, 0:N2]
                nh3 = cxn["h3"]
                A2, B2, _ = vecs(g, l, 0, strm2)
                prenorm_stats(g, nx3, N2, nh3, b_x, cxn["b_h"], ps_i=1)
            for hh in range(4):
                P2(bl, hh)
                if nxt is not None:
                    P1(cx, nxt, hh)
                elif xt and cxn is not None:
                    P1(cxn, 0, hh)
                P2b(bl, hh)
                if nxt is not None:
                    P1g(cx, nxt, hh)
                elif xt and cxn is not None:
                    P1g(cxn, 0, hh)
                if early:
                    for c2 in (2 * hh, 2 * hh + 1):
                        if xt:
                            prenorm_apply(g, nx3, N2, A2, B2, nh3, b_x, cxn["b_h"], c2, ptmp_r[:, 0:N2], b_ptmp_r, ps_i=1)
                        else:
                            prenorm_apply(g, nx3, N2, A2, B2, nh3, b_x, cxn["b_h"], c2, g.psum[2][:, 0:N2], g.b_ps[2],
                                          ps_i=1)
                    if dirn == 0:
                        decay_ops(i + 1, range(4 * hh, 4 * hh + 4))
                        if hh == 3:
                            rs_load(i + 2)
            for k2 in range(16):
                p.call("pe", "transpose", [b_z[k2 // 4], b_ident], [g.b_ps[3 + k2 // 8]], psT3[k2 // 8][:, k2 % 8, :],
                       z[:, k2 * 128:(k2 + 1) * 128], ident)
            for hf in range(2):
                if dirn == 1:
                    p.copy("dve", zT3[:, hf * 8:(hf + 1) * 8, :], psT3[hf], [g.b_ps[3 + hf]], [b_zT])
                else:
                    p.tt("dve", zT3[:, hf * 8:(hf + 1) * 8, :], psT3[hf], zbT3[:, hf * 8:(hf + 1) * 8, :], ALU.add,
                         [g.b_ps[3 + hf], b_zbT], [b_zT])
            p.dma("sp", d_.rearrange("(k q) t -> q k t", q=128)[:, :, tok0:tok0 + 128], zT3, [b_zT],
                  [g.ST[2]])


def phase_ret_out(g, l, slot, src, dst, src_c, dst_c, need_ctx_out=True):
    nc, p = g.nc, g.p
    ar = Arena(g)
    wo = ar.bf16(16 * 1024)
    wo3 = v3(wo, 16, 1024)
    b_wo = Buf("rwo")
    load_w_cast(g, wo3, g.ret_wo[slot].rearrange("(k q) n -> q k n", q=128), b_wo, 1024, 16)
    xs = [ar.f32(8 * 512) for _ in range(2)]
    zs = [ar.bf16(16 * 512) for _ in range(2)]
    b_xs = [Buf("ox0"), Buf("ox1")]
    b_zs = [Buf("oz0"), Buf("oz1")]
    tiles = [(src, dst, g.ZT, t * 512, 512, 0) for t in range(g.NT)]
    if need_ctx_out:
        tiles.append((src_c, dst_c, g.ZTC, 0, CTX, 1))

    def load(i):
        s_, d_, z_, t0, N, strm = tiles[i]
        p.dma("sp", v3(xs[i % 2], 8, 512)[:, :, 0:N], s_.rearrange("(c q) t -> q c t", q=128)[:, :, t0:t0 + N],
              [], [b_xs[i % 2]])
        p.dma("sp", v3(zs[i % 2], 16, 512)[:, :, 0:N], z_.rearrange("(k q) t -> q k t", q=128)[:, :, t0:t0 + N],
              [], [b_zs[i % 2]])

    load(0)
    for i, (s_, d_, z_, t0, N, strm) in enumerate(tiles):
        if i + 1 < len(tiles):
            load(i + 1)
        x3 = v3(xs[i % 2], 8, 512)[:, :, 0:N]
        z3 = v3(zs[i % 2], 16, 512)[:, :, 0:N]
        _, _, G = vecs(g, l, 0, strm)
        for c in range(8):
            py = 1 + (c % 2)
            for k in range(16):
                p.mm(g.psum[py][:, 0:N], wo3[:, k, c * 128:(c + 1) * 128], z3[:, k, :], (k == 0), (k == 15),
                     [b_wo, b_zs[i % 2]], [g.b_ps[py]])
            p.stt(x3[:, c, :], g.psum[py][:, 0:N], G[:, c:c + 1], x3[:, c, :], ALU.mult, ALU.add,
                  [g.b_ps[py], b_xs[i % 2], g.b_ADA], [b_xs[i % 2]])
        ob = (g.OUT if d_ is g.y else g.ST)[i % 2]
        p.dma("sp", d_.rearrange("(c q) t -> q c t", q=128)[:, :, t0:t0 + N], x3, [b_xs[i % 2]], [ob])


def host_ret_consts(T):
    inv = (10000.0 ** (-np.linspace(0.0, 1.0, 128, dtype=np.float32))).astype(np.float32)
    ang = (np.arange(T, dtype=np.float32)[:, None] * inv).astype(np.float32)
    cs = np.stack([np.cos(ang).T, np.sin(ang).T]).astype(np.float32)
    dec = np.zeros((2, 8, 128), np.float64)
    pos = np.arange(128, dtype=np.float64)
    for hh in range(4):
        lf, lb = RET_LG[0][hh], RET_LG[1][hh]
        dec[0, hh] = np.exp((pos + 1.0) * lf)
        dec[0, 4 + hh] = np.exp(-(pos + 1.0) * lf) / 16.0
        dec[1, hh] = np.exp((128.0 - pos) * lb)
        dec[1, 4 + hh] = np.exp(-(128.0 - pos) * lb) / 16.0
    decB = np.ascontiguousarray(np.broadcast_to(dec.reshape(2, 1, 1024), (2, 128, 1024))).astype(np.float32)
    j = np.arange(128)[:, None]
    i = np.arange(128)[None, :]
    masks = np.stack([(i >= j), (i <= j)]).astype(np.float32)
    return dict(ret_cs=np.ascontiguousarray(cs), ret_dec=decB, ret_masks=masks)


SEQ = 8192
BATCH = 8


def full_phases():
    def L0a(g):
        phase_attn(g, 0, 0, g.x_in, g.X1, g.ctx_in, g.XC, True)

    def L0f(g):
        phase_ffn(g, 0, g.X1, g.X1, g.XC, g.XC, True)

    def L1p(g):
        phase_pool(g, 1, 0, g.X1, g.X2, g.XC, g.XC2)

    def L1f(g):
        phase_ffn(g, 1, g.X2, g.X2, g.XC2, g.XC2, True)

    def L2b(g):
        phase_ret_sweep(g, 2, 0, 1, g.X2, g.XC2)

    def L2fw(g):
        phase_ret_sweep(g, 2, 0, 0, g.X2, g.XC2)

    def L2o(g):
        phase_ret_out(g, 2, 0, g.X2, g.X2, g.XC2, g.XC2, True)

    def L2f(g):
        phase_ffn(g, 2, g.X2, g.X2, g.XC2, g.XC2, True)

    def L3a(g):
        phase_attn(g, 3, 1, g.X2, g.X2, g.XC2, None, False)

    def L3f(g):
        phase_ffn(g, 3, g.X2, g.y, None, None, False)

    return [phase_ada, L0a, L0f, L1p, L1f, L2b, L2fw, L2o, L2f, L3a, L3f]


def host_inputs(T, x_b, c_b, ctx_b, c_ctx, ada_w, ada_b, norm_mix, norm_ffn, attn_w_qkv, attn_w_o, attn_q_norm,
                attn_k_norm, attn_sink, pool_w, pool_scale, ret_w_in, ret_w_o, ffn_w_gate, ffn_w_up, ffn_w_down,
                shared=None):
    if shared is None:
        shared = {}
        shared.update(host_attn_consts(T))
        shared.update(host_attn_weights(attn_w_qkv, attn_w_o, attn_q_norm, attn_k_norm, attn_sink))
        shared.update(host_ret_consts(T))
        shared["pool_rc"] = host_pool_rc(T)
        shared["pool_w"] = np.ascontiguousarray(pool_w)
        shared["pool_scT"] = np.ascontiguousarray(pool_scale.reshape(-1, 8, 128).transpose(0, 2, 1))
        shared["ret_w_in"] = np.ascontiguousarray(ret_w_in)
        shared["ret_w_o"] = np.ascontiguousarray(ret_w_o)
        shared["ada_w"] = np.ascontiguousarray(ada_w)
        shared["ffn_w_gate"] = np.ascontiguousarray(ffn_w_gate)
        shared["ffn_w_up"] = np.ascontiguousarray(ffn_w_up)
        shared["ffn_w_down"] = np.ascontiguousarray(ffn_w_down)
    im = dict(shared)
    im["xT"] = np.ascontiguousarray(x_b.T)
    im["ctxT"] = np.ascontiguousarray(ctx_b.T)
    im.update(host_small(c_b, c_ctx, ada_b, norm_mix, norm_ffn))
    return im, shared


_NC_CACHE = {}


def kernel(x, c, ctx, c_ctx, ada_w, ada_b, norm_mix, norm_ffn, attn_w_qkv, attn_w_o, attn_q_norm,
           attn_k_norm, attn_sink, pool_w, pool_scale, ret_w_in, ret_w_o, ffn_w_gate, ffn_w_up, ffn_w_down):
    f = lambda a: np.asarray(a, dtype=np.float32)
    x, c, ctx, c_ctx = f(x), f(c), f(ctx), f(c_ctx)
    args = [f(a) for a in (ada_w, ada_b, norm_mix, norm_ffn, attn_w_qkv, attn_w_o, attn_q_norm, attn_k_norm,
                           attn_sink, pool_w, pool_scale, ret_w_in, ret_w_o, ffn_w_gate, ffn_w_up, ffn_w_down)]
    B, T, _ = x.shape
    if T not in _NC_CACHE:
        _NC_CACHE[T] = build(T, full_phases())
    nc = _NC_CACHE[T]
    in_maps = []
    shared = None
    for b in range(B):
        im, shared = host_inputs(T, x[b], c[b], ctx[b], c_ctx, *args, shared=shared)
        in_maps.append(im)
    res = run_bass_kernel_spmd(nc, in_maps, core_ids=list(range(B)))
    out = np.empty((B, T, D), np.float32)
    for b in range(B):
        out[b] = np.asarray(res.results[b]["yT"]).T
    return out
```
